# Optimizing a Trainium2 kernel written in Bass

```python
import jax
import jax.numpy as jnp
from jax import lax
import numpy as np

D_MODEL = 2048
BATCH = 4
SEQ = 2048
DEPTH = 4
DEC_BATCH = 32
DEC_SEQ = 4
PAST_LEN = 16384
PAGE_SIZE = 128

HEAD_DIM = 64
A_HEADS = (3 * D_MODEL // 8) // HEAD_DIM
A_BRANCHES = ((128, 1), (512, 4), (2048, 16))
A_WINDOW_MAX = 2048
B_HEAD_DIM = 128
B_HEADS = (D_MODEL // 4) // B_HEAD_DIM
CONV_W = 4
DELTA_CHUNK = 64
C_HEADS = (3 * D_MODEL // 8) // HEAD_DIM
C_KV_HEADS = C_HEADS // 3
C_WINDOW = 128
A_WIDTH = A_HEADS * HEAD_DIM
B_WIDTH = B_HEADS * B_HEAD_DIM
C_WIDTH = C_HEADS * HEAD_DIM
MIX_WIDTH = A_WIDTH + B_WIDTH + C_WIDTH
D_FF = 4 * D_MODEL
BLOCK = 128
EPS = 1e-6
ATTN_SCALE = HEAD_DIM ** -0.5
IN_SIZES = (3 * A_WIDTH, 3 * B_WIDTH, B_WIDTH, B_HEADS, B_HEADS, C_WIDTH, 2 * C_KV_HEADS * HEAD_DIM)
IN_COLS = sum(IN_SIZES)
IN_SPLITS = tuple(int(c) for c in np.cumsum(IN_SIZES)[:-1])

kernel_name = 'hybrid_dilated_delta_swa_step'


def alibi_slopes(n):
    return np.asarray([2.0 ** (-8.0 * (i + 1) / n) for i in range(n)], dtype=np.float32)


def rms_norm(x, g):
    xf = x.astype(jnp.float32)
    y = xf * lax.rsqrt(jnp.mean(xf * xf, axis=-1, keepdims=True) + EPS)
    return (y * g.astype(jnp.float32)).astype(x.dtype)


def l2_normalize(x):
    return x * lax.rsqrt(jnp.sum(x * x, axis=-1, keepdims=True) + EPS)


def banded_attention(q, k, v, window, step, slopes):
    n, L, hq, dh = q.shape
    hk = k.shape[2]
    grp = hq // hk
    nb = -(-L // BLOCK)
    pad = nb * BLOCK - L
    qb = jnp.pad(q, ((0, 0), (0, pad), (0, 0), (0, 0))).reshape(n, nb, BLOCK, hk, grp, dh)
    kp = jnp.pad(k, ((0, 0), (BLOCK, pad), (0, 0), (0, 0))).reshape(n, nb + 1, BLOCK, hk, dh)
    vp = jnp.pad(v, ((0, 0), (BLOCK, pad), (0, 0), (0, 0))).reshape(n, nb + 1, BLOCK, hk, dh)
    kb = jnp.concatenate([kp[:, :-1], kp[:, 1:]], axis=2)
    vb = jnp.concatenate([vp[:, :-1], vp[:, 1:]], axis=2)
    s = jnp.einsum('nbqhgd,nbkhd->nbhgqk', qb, kb, preferred_element_type=jnp.float32) * ATTN_SCALE
    qpos = np.arange(BLOCK)[:, None] + BLOCK
    kpos = np.arange(2 * BLOCK)[None, :]
    dist = qpos - kpos
    kglob = (np.arange(nb) * BLOCK - BLOCK)[:, None, None] + kpos[None]
    valid = (dist[None] >= 0) & (dist[None] <= window) & (kglob >= 0)
    bias = -(slopes.reshape(hk, grp)[:, :, None, None] * (step * dist).astype(np.float32))
    s = jnp.where(valid[None, :, None, None], s + bias, -jnp.inf)
    m = jnp.max(s, axis=-1, keepdims=True)
    p = jnp.exp(s - m)
    den = jnp.sum(p, axis=-1, keepdims=True)
    o = jnp.einsum('nbhgqk,nbkhd->nbhgqd', p, vb.astype(jnp.float32)) / den
    lse = (m + jnp.log(den))[..., 0]
    o = o.transpose(0, 1, 4, 2, 3, 5).reshape(n, nb * BLOCK, hq, dh)[:, :L]
    lse = lse.transpose(0, 1, 4, 2, 3).reshape(n, nb * BLOCK, hq)[:, :L]
    return o, lse


def attend_rows(q, kg, vg, valid, dist, slopes):
    n, T, hq, dh = q.shape
    hk = kg.shape[3]
    grp = hq // hk
    qg = q.reshape(n, T, hk, grp, dh)
    s = jnp.einsum('nthgd,ntjhd->nthgj', qg, kg, preferred_element_type=jnp.float32) * ATTN_SCALE
    bias = -(slopes.reshape(hk, grp)[None, :, :, None] * dist[:, None, None, :])
    s = jnp.where(valid[:, None, None, :], s + bias, -jnp.inf)
    m = jnp.max(s, axis=-1, keepdims=True)
    p = jnp.exp(s - m)
    den = jnp.sum(p, axis=-1, keepdims=True)
    o = jnp.einsum('nthgj,ntjhd->nthgd', p, vg.astype(jnp.float32)) / den
    lse = (m + jnp.log(den))[..., 0]
    return o.reshape(n, T, hq, dh), lse.reshape(n, T, hq)


def gather_rows(past, new, idx):
    P = past.shape[1]
    T = new.shape[1]
    from_past = idx < P
    rp = past[:, np.clip(idx, 0, P - 1)]
    rn = new[:, np.clip(idx - P, 0, T - 1)]
    sel = from_past.reshape(from_past.shape + (1,) * (rp.ndim - 3))
    return jnp.where(sel[None], rp, rn.astype(rp.dtype))


def merge_branches(outs, lses):
    wts = jax.nn.softmax(jnp.stack(lses), axis=0)
    return jnp.sum(wts[..., None] * jnp.stack(outs), axis=0)


def dilated_prompt(q, k, v):
    n, S, H, dh = q.shape
    slopes = alibi_slopes(A_HEADS)
    outs, lses = [], []
    for window, dil in A_BRANCHES:
        sub = S // dil
        fold = lambda t: t.reshape(n, sub, dil, H, dh).transpose(0, 2, 1, 3, 4).reshape(n * dil, sub, H, dh)
        o, lse = banded_attention(fold(q), fold(k), fold(v), window // dil, dil, slopes)
        outs.append(o.reshape(n, dil, sub, H, dh).transpose(0, 2, 1, 3, 4).reshape(n, S, H, dh))
        lses.append(lse.reshape(n, dil, sub, H).transpose(0, 2, 1, 3).reshape(n, S, H))
    return merge_branches(outs, lses)


def dilated_sample(q, past_kv, new_kv):
    P = past_kv.shape[1]
    T = q.shape[1]
    slopes = alibi_slopes(A_HEADS)
    qi = np.arange(T)[:, None]
    outs, lses = [], []
    for window, dil in A_BRANCHES:
        nk = window // dil + 1
        dist = np.broadcast_to(dil * np.arange(nk)[None, :], (T, nk))
        idx = P + qi - dist
        rows = gather_rows(past_kv, new_kv, idx)
        o, lse = attend_rows(q, rows[:, :, :, 0], rows[:, :, :, 1], idx >= 0, dist.astype(np.float32), slopes)
        outs.append(o)
        lses.append(lse)
    return merge_branches(outs, lses)


def swa_sample(q, past_kv, new_kv):
    P = past_kv.shape[1]
    T = q.shape[1]
    idx = np.broadcast_to(np.arange(P + T)[None, :], (T, P + T))
    dist = (P + np.arange(T)[:, None]) - idx
    valid = (dist >= 0) & (dist <= C_WINDOW)
    rows = gather_rows(past_kv, new_kv, idx)
    return attend_rows(q, rows[:, :, :, 0], rows[:, :, :, 1], valid, dist.astype(np.float32), alibi_slopes(C_HEADS))


def gated_delta_chunked(q, k, v, g, beta, s0):
    n, L, H, dk = q.shape
    dv = v.shape[-1]
    C = min(DELTA_CHUNK, L)
    nc = -(-L // C)
    pad = nc * C - L

    def prep(t):
        t = jnp.pad(t.astype(jnp.float32), [(0, 0), (0, pad)] + [(0, 0)] * (t.ndim - 2))
        t = t.reshape((n, nc, C) + t.shape[2:])
        return jnp.moveaxis(t, 3, 1)

    q, k, v, g, beta = prep(q), prep(k), prep(v), prep(g), prep(beta)
    gc = jnp.cumsum(g, axis=-1)
    tri_incl = np.tril(np.ones((C, C), dtype=bool))
    tri_strict = np.tril(np.ones((C, C), dtype=bool), -1)
    decay = jnp.exp(jnp.where(tri_incl, gc[..., :, None] - gc[..., None, :], -jnp.inf))
    kb = k * beta[..., None]
    vb = v * beta[..., None]
    a = jnp.where(tri_strict, jnp.einsum('nhcid,nhcjd->nhcij', kb, k) * decay, 0.0)
    rhs = jnp.concatenate([vb, kb * jnp.exp(gc)[..., None]], axis=-1)
    sol = lax.linalg.triangular_solve(a + np.eye(C, dtype=np.float32), rhs, left_side=True, lower=True)
    u, w = sol[..., :dv], sol[..., dv:]
    attn = jnp.einsum('nhcid,nhcjd->nhcij', q, k) * decay
    qd = q * jnp.exp(gc)[..., None]
    kd = k * jnp.exp(gc[..., -1:] - gc)[..., None]
    glast = jnp.exp(gc[..., -1])
    xs = tuple(jnp.moveaxis(t, 2, 0) for t in (qd, kd, u, w, attn, glast))

    def step(S, xc):
        qd_c, kd_c, u_c, w_c, attn_c, gl_c = xc
        v_new = u_c - jnp.einsum('nhcd,nhde->nhce', w_c, S)
        o = jnp.einsum('nhcd,nhde->nhce', qd_c, S) + jnp.einsum('nhij,nhje->nhie', attn_c, v_new)
        S = S * gl_c[..., None, None] + jnp.einsum('nhcd,nhce->nhde', kd_c, v_new)
        return S, o

    S, o = lax.scan(step, s0.astype(jnp.float32), xs)
    o = o.transpose(1, 0, 3, 2, 4).reshape(n, nc * C, H, dv)[:, :L]
    return o, S


def delta_branch(b_qkv, b_z, b_beta, b_alpha, conv_buf, s0, conv_w, a_log, dt_bias, dn_g):
    n, L, _ = b_qkv.shape
    full = jnp.concatenate([conv_buf.astype(b_qkv.dtype), b_qkv], axis=1)
    conv = sum(full[:, j:j + L] * conv_w[j] for j in range(CONV_W))
    conv = jax.nn.silu(conv).astype(jnp.float32).reshape(n, L, 3, B_HEADS, B_HEAD_DIM)
    q = l2_normalize(conv[:, :, 0]) * (B_HEAD_DIM ** -0.5)
    k = l2_normalize(conv[:, :, 1])
    v = conv[:, :, 2]
    beta = jax.nn.sigmoid(b_beta.astype(jnp.float32))
    g = -jnp.exp(a_log.astype(jnp.float32)) * jax.nn.softplus(b_alpha.astype(jnp.float32) + dt_bias.astype(jnp.float32))
    o, s_new = gated_delta_chunked(q, k, v, g, beta, s0)
    o = rms_norm(o, dn_g) * jax.nn.silu(b_z.astype(jnp.float32).reshape(n, L, B_HEADS, B_HEAD_DIM))
    return o.reshape(n, L, B_WIDTH), s_new, full[:, full.shape[1] - (CONV_W - 1):]


def token_mixers(h, dil_past, swa_past, conv_buf, s0, w_in, conv_w, a_log, dt_bias, dn_g, sinks):
    n, L, _ = h.shape
    z = jnp.einsum('nld,de->nle', h, w_in)
    a_qkv, b_qkv, b_z, b_beta, b_alpha, c_q, c_kv = jnp.split(z, IN_SPLITS, axis=-1)
    a_qkv = a_qkv.reshape(n, L, 3, A_HEADS, HEAD_DIM)
    a_q, a_kv = a_qkv[:, :, 0], a_qkv[:, :, 1:]
    c_q = c_q.reshape(n, L, C_HEADS, HEAD_DIM)
    c_kv = c_kv.reshape(n, L, 2, C_KV_HEADS, HEAD_DIM)
    if dil_past is None:
        a_out = dilated_prompt(a_q, a_kv[:, :, 0], a_kv[:, :, 1])
        c_o, c_lse = banded_attention(c_q, c_kv[:, :, 0], c_kv[:, :, 1], C_WINDOW, 1, alibi_slopes(C_HEADS))
        a_rows = a_kv[:, L - min(A_WINDOW_MAX, L):]
        c_rows = c_kv[:, L - min(C_WINDOW, L):]
    else:
        a_out = dilated_sample(a_q, dil_past, a_kv)
        c_o, c_lse = swa_sample(c_q, swa_past, c_kv)
        a_rows, c_rows = a_kv, c_kv
    c_out = c_o * jnp.exp(c_lse - jnp.logaddexp(c_lse, sinks.astype(jnp.float32)))[..., None]
    b_out, s_new, conv_new = delta_branch(b_qkv, b_z, b_beta, b_alpha, conv_buf, s0, conv_w, a_log, dt_bias, dn_g)
    mix = jnp.concatenate([a_out.reshape(n, L, A_WIDTH).astype(h.dtype), b_out.astype(h.dtype),
                           c_out.reshape(n, L, C_WIDTH).astype(h.dtype)], axis=-1)
    return mix, a_rows, c_rows, s_new.astype(s0.dtype), conv_new


def sandwich_residual(x, mix, w_out, g_post_mix, g_pre_mlp, w_up, w_down, g_post_mlp):
    x = x + rms_norm(jnp.einsum('nlm,md->nld', mix, w_out), g_post_mix)
    hm = rms_norm(x, g_pre_mlp)
    u = jnp.square(jax.nn.relu(jnp.einsum('nld,df->nlf', hm, w_up)))
    return x + rms_norm(jnp.einsum('nlf,fd->nld', u, w_down), g_post_mlp)


def setup_inputs(seed: int = 0) -> dict:
    key = jax.random.key(seed)
    ks = jax.random.split(key, 24)
    f32 = jnp.float32

    def nrm(k, shape, scale=1.0):
        return jax.random.normal(k, shape, f32) * scale

    la = min(A_WINDOW_MAX, PAST_LEN)
    lc = min(C_WINDOW, PAST_LEN)
    dt = jnp.exp(jax.random.uniform(ks[10], (DEPTH, B_HEADS), f32, float(np.log(1e-3)), float(np.log(1e-1))))
    return {
        'x_prompt': nrm(ks[0], (BATCH, SEQ, D_MODEL)),
        'x_sample': nrm(ks[1], (DEC_BATCH, DEC_SEQ, D_MODEL)),
        'cache_dilated_kv': nrm(ks[2], (DEPTH, DEC_BATCH, la, 2, A_HEADS, HEAD_DIM)),
        'cache_swa_kv': nrm(ks[3], (DEPTH, DEC_BATCH, lc, 2, C_KV_HEADS, HEAD_DIM)),
        'state_delta_s': nrm(ks[4], (DEPTH, DEC_BATCH, B_HEADS, B_HEAD_DIM, B_HEAD_DIM), 0.1),
        'state_delta_conv': nrm(ks[5], (DEPTH, DEC_BATCH, CONV_W - 1, 3 * B_WIDTH)),
        'g_pre_mix': 1.0 + nrm(ks[6], (DEPTH, D_MODEL), 0.05),
        'w_in': nrm(ks[7], (DEPTH, D_MODEL, IN_COLS), D_MODEL ** -0.5),
        'delta_conv_w': nrm(ks[8], (DEPTH, CONV_W, 3 * B_WIDTH), CONV_W ** -0.5),
        'delta_a_log': jnp.log(jax.random.uniform(ks[9], (DEPTH, B_HEADS), f32, 1.0, 16.0)),
        'delta_dt_bias': dt + jnp.log(-jnp.expm1(-dt)),
        'delta_norm_g': 1.0 + nrm(ks[11], (DEPTH, B_HEAD_DIM), 0.05),
        'swa_sinks': nrm(ks[12], (DEPTH, C_HEADS)),
        'w_out': nrm(ks[13], (DEPTH, MIX_WIDTH, D_MODEL), MIX_WIDTH ** -0.5),
        'g_post_mix': 1.0 + nrm(ks[14], (DEPTH, D_MODEL), 0.05),
        'g_pre_mlp': 1.0 + nrm(ks[15], (DEPTH, D_MODEL), 0.05),
        'w_up': nrm(ks[16], (DEPTH, D_MODEL, D_FF), D_MODEL ** -0.5),
        'w_down': nrm(ks[17], (DEPTH, D_FF, D_MODEL), D_FF ** -0.5),
        'g_post_mlp': 1.0 + nrm(ks[18], (DEPTH, D_MODEL), 0.05),
    }


def reference(x_prompt, x_sample, cache_dilated_kv, cache_swa_kv, state_delta_s, state_delta_conv,
              g_pre_mix, w_in, delta_conv_w, delta_a_log, delta_dt_bias, delta_norm_g, swa_sinks,
              w_out, g_post_mix, g_pre_mlp, w_up, w_down, g_post_mlp):
    yp, ys = x_prompt, x_sample
    n_p = x_prompt.shape[0]
    p_akv, p_ckv, p_s, p_conv = [], [], [], []
    s_akv, s_ckv, s_s, s_conv = [], [], [], []
    for l in range(DEPTH):
        mixer_w = (w_in[l], delta_conv_w[l], delta_a_log[l], delta_dt_bias[l], delta_norm_g[l], swa_sinks[l])
        block_w = (w_out[l], g_post_mix[l], g_pre_mlp[l], w_up[l], w_down[l], g_post_mlp[l])
        conv0 = jnp.zeros((n_p, CONV_W - 1, 3 * B_WIDTH), yp.dtype)
        s0 = jnp.zeros((n_p, B_HEADS, B_HEAD_DIM, B_HEAD_DIM), state_delta_s.dtype)
        mix, akv, ckv, bs, bc = token_mixers(rms_norm(yp, g_pre_mix[l]), None, None, conv0, s0, *mixer_w)
        yp = sandwich_residual(yp, mix, *block_w)
        p_akv.append(akv); p_ckv.append(ckv); p_s.append(bs); p_conv.append(bc)
        mix, akv, ckv, bs, bc = token_mixers(rms_norm(ys, g_pre_mix[l]), cache_dilated_kv[l], cache_swa_kv[l],
                                             state_delta_conv[l], state_delta_s[l], *mixer_w)
        ys = sandwich_residual(ys, mix, *block_w)
        s_akv.append(akv); s_ckv.append(ckv); s_s.append(bs); s_conv.append(bc)
    return (yp, ys, jnp.stack(p_akv), jnp.stack(p_ckv), jnp.stack(p_s), jnp.stack(p_conv),
            jnp.stack(s_akv), jnp.stack(s_ckv), jnp.stack(s_s), jnp.stack(s_conv))
```

```python
import numpy as np
import ml_dtypes
from collections import defaultdict
from contextlib import ExitStack
import concourse.bass as bass
import concourse.mybir as mybir
from concourse.bass_utils import run_bass_kernel_spmd

F32 = mybir.dt.float32
BF16 = mybir.dt.bfloat16
ALU = mybir.AluOpType
AF = mybir.ActivationFunctionType
AX = mybir.AxisListType

DEPTH = 4
D = 2048
NPR = 2048
NSS = 4
NS = 16
NT = NPR + NS
DFF = 8192
EPS = 1e-6
NFM = 37
WALL = NFM * 128 + 1024
TM_SEGS = [[(768, 512)], [(1280, 256), (4736, 256)], [(4992, 512)], [(4480, 256), (5504, 256)],
           [(1536, 512)], [(2048, 512)], [(2560, 512)]]
AQ, AK, BQ, BK, BV, BZ, BM, CQ, CK = 0, 6, 12, 16, 20, 24, 28, 29, 35
CPERM = [0, 3, 1, 4, 2, 5, 6, 9, 7, 10, 8, 11]
ATT_SCALE = 0.125
ENGS = ("pe", "act", "dve", "pool", "sp")


def alibi_slopes(n):
    return np.asarray([2.0 ** (-8.0 * (i + 1) / n) for i in range(n)], dtype=np.float64)


class Op:
    __slots__ = ("eng", "fn", "reads", "writes", "dma", "idx", "deps", "signal", "seq", "grp", "semkey", "bar")

    def __init__(self, eng, fn, reads, writes, dma, grp):
        self.eng = eng
        self.fn = fn
        self.reads = tuple(reads)
        self.writes = tuple(writes)
        self.dma = dma
        self.grp = grp
        self.deps = []
        self.signal = False
        self.seq = 0
        self.bar = False


class Prog:
    def __init__(self, nc):
        self.nc = nc
        self.ops = []
        self.enabled = True

    def add(self, eng, fn, reads=(), writes=(), dma=False, grp=None):
        op = Op(eng, fn, reads, writes, dma, grp)
        if self.enabled:
            self.ops.append(op)
        return op

    def pe(self, fn, r=(), w=()):
        return self.add("pe", fn, r, w)

    def act(self, fn, r=(), w=()):
        return self.add("act", fn, r, w)

    def dve(self, fn, r=(), w=()):
        return self.add("dve", fn, r, w)

    def pool(self, fn, r=(), w=()):
        return self.add("pool", fn, r, w)

    def dma(self, q, fn, r=(), w=(), grp=None):
        return self.add(q, fn, r, w, dma=True, grp=grp)

    def barrier(self):
        op = Op("sp", None, (), (), False, None)
        op.bar = True
        self.ops.append(op)

    def emit(self, stack):
        nc = self.nc
        ops = self.ops
        last_w = {}
        rds = defaultdict(list)
        last_eng = {}
        last_grp = {}
        bar_deps = []
        for i, op in enumerate(ops):
            op.idx = i
            if op.bar:
                bar_deps = list(last_eng.values()) + list(last_grp.values())
                for d in bar_deps:
                    d.signal = True
                last_w = {}
                rds = defaultdict(list)
                continue
            deps = {}
            for k in op.reads:
                w = last_w.get(k)
                if w is not None:
                    deps[w.idx] = "raw"
            for k in op.writes:
                w = last_w.get(k)
                if w is not None and deps.get(w.idx) != "raw":
                    deps[w.idx] = "waw"
                for r in rds[k]:
                    if r.idx not in deps:
                        deps[r.idx] = "war"
            for k in op.reads:
                if op.dma:
                    rds[k].append(op)
                else:
                    rds[k] = [r_ for r_ in rds[k] if r_.dma or r_.eng != op.eng] + [op]
            for k in op.writes:
                last_w[k] = op
                rds[k] = []
            keep = list(bar_deps)
            for j, kind in deps.items():
                d = ops[j]
                if d is op:
                    continue
                if (not d.dma) and (not op.dma) and d.eng == op.eng and kind != "raw" and op.eng == "pe":
                    continue
                keep.append(d)
                d.signal = True
            op.deps = keep
            if op.dma:
                op.signal = True
                last_grp[op.grp] = op
            else:
                last_eng[op.eng] = op
        cnt = defaultdict(int)
        for op in ops:
            if op.bar:
                continue
            if op.signal:
                key = ("g", op.grp) if op.dma else ("e", op.eng)
                cnt[key] += 16 if op.dma else 1
                op.seq = cnt[key]
                op.semkey = key
        sems = {}
        for n, key in enumerate(cnt):
            sems[key] = stack.enter_context(nc.semaphore("s%d" % n))
        per_eng = defaultdict(list)
        for op in ops:
            if not op.bar:
                per_eng[op.eng].append(op)
        block = stack.enter_context(nc.Block())
        totals = dict(cnt)
        self.n_waits = 0

        def run_engine(name, e):
            waited = defaultdict(int)
            for op in per_eng[name]:
                need = {}
                for d in op.deps:
                    k = d.semkey
                    if d.seq > need.get(k, 0):
                        need[k] = d.seq
                for k, v in need.items():
                    if waited[k] >= v:
                        continue
                    e.wait_ge(sems[k], v)
                    self.n_waits += 1
                    waited[k] = v
                ins = op.fn(e)
                if op.signal:
                    ins.then_inc(sems[op.semkey], 16 if op.dma else 1)
            if name == "sp":
                for k, v in totals.items():
                    if waited[k] < v:
                        e.wait_ge(sems[k], v)

        @block.sync
        def _(e):
            run_engine("sp", e)

        @block.scalar
        def _(e):
            run_engine("act", e)

        @block.vector
        def _(e):
            run_engine("dve", e)

        @block.gpsimd
        def _(e):
            run_engine("pool", e)

        @block.tensor
        def _(e):
            run_engine("pe", e)


class Arena:
    def __init__(self, ap, nbytes):
        self.ap = ap
        self.nbytes = nbytes
        self.off = 0

    def mark(self):
        return self.off

    def reset(self, m=0):
        self.off = m

    def tile(self, shape, dt, parts=None):
        n = int(np.prod(shape[1:]))
        nb = n * (2 if dt == BF16 else 4)
        nb = (nb + 63) // 64 * 64
        off = self.off
        self.off += nb
        assert self.off <= self.nbytes, "SBUF arena overflow %d" % self.off
        p = shape[0]
        if dt == BF16:
            a = self.ap[0:p, off // 4: off // 4 + nb // 4].bitcast(BF16)[:, 0:n]
        else:
            a = self.ap[0:p, off // 4: off // 4 + n]
        if len(shape) == 3:
            a = a.rearrange("p (a b) -> p a b", a=shape[1])
        elif len(shape) == 4:
            a = a.rearrange("p (a b c) -> p a b c", a=shape[1], b=shape[2])
        return a


ALL_PHASES = ("12", "3a", "3b", "3b2", "3bs", "4", "5")


def build_program(depth, phases=ALL_PHASES):
    nc = bass.Bass("TRN2", target_bir_lowering=False)

    def din(name, shape, dt=F32):
        return nc.dram_tensor(name, list(shape), dt, kind="ExternalInput").ap()

    def dout(name, shape, dt=F32):
        return nc.dram_tensor(name, list(shape), dt, kind="ExternalOutput").ap()

    def dscr(name, shape, dt):
        return nc.dram_tensor(name, list(shape), dt).ap()

    def din_if(cond, name, shape):
        return din(name, shape) if cond else nc.dram_tensor(name, list(shape), F32).ap()

    xin = din("xin", [NT, D])
    w_fm = din_if("12" in phases, "w_fm", [depth, D, WALL])
    w_out = din_if("4" in phases, "w_out", [depth, D, D])
    w_up = din_if("5" in phases, "w_up", [depth, D, DFF])
    w_down = din_if("5" in phases, "w_down", [depth, DFF, D])
    gpm = din("gpm", [depth, 128, 16])
    gpl = din("gpl", [depth, 128, 16])
    gom = din("gom", [depth, 1, D])
    gol = din("gol", [depth, 1, D])
    convw = din("convw", [depth, 128, 48])
    bpar = din("bpar", [depth, 36, 2])
    dng = din("dng", [depth, 1, 128])
    sinks = din("sinks", [depth, 65, 12])
    cdk = din_if("3a" in phases, "cdk", [depth, NSS, 1024, 1536])
    csw = din("csw", [depth, NSS, 128, 512])
    sds = din("sds", [depth, NSS, 4, 128, 128])
    sdc = din("sdc", [depth, NSS, 3, 1536])
    identf_d = din("identf", [128, 128])
    EA_d = din("EA", [128, 36 * 256], BF16)
    EC_d = din("EC", [128, 12 * 256], BF16)
    ESA_d = din("ESA", [128, 12 * 36], BF16)
    ESC_d = din("ESC", [128, 12 * 8], BF16)
    U64_d = din("U64", [64, 64])
    LS64_d = din("LS64", [64, 64])
    sel_d = din("sel", [65, 64])

    y = dout("y", [NT, D])
    akv = dout("akv", [depth, NT, 1536])
    ckv = dout("ckv", [depth, NT, 512])
    s_out = dout("s_out", [depth, 1 + NSS, 4, 128, 128])
    conv_out = dout("conv_out", [depth, 1 + NSS, 3, 1536])

    dbg = dout if "dbg" in phases else dscr
    xres = dbg("xres", [NT, D], F32)
    zT = dbg("zT", [NFM * 128, NT], BF16)
    mixT = dbg("mixT", [D, NT], BF16)

    P = Prog(nc)
    st = ExitStack()
    ARENA_F32 = 51600
    arena_t = nc.alloc_sbuf_tensor("arena", [128, ARENA_F32], F32)
    A = Arena(arena_t, ARENA_F32 * 4)
    PS = [st.enter_context(nc.psum_tensor("ps%d" % i, [128, 512], F32)) for i in range(8)]

    def psk(i):
        return "ps%d" % i

    uid = [0]

    def U(prefix):
        uid[0] += 1
        return "%s#%d" % (prefix, uid[0])

    def DMA(q, out, in_, r=(), w=(), grp=None):
        P.dma(q, lambda e: e.dma_start(out=out, in_=in_), r, w, grp)

    def MM(out, lhsT, rhs, start=True, stop=True, r=(), w=()):
        P.pe(lambda e: e.matmul(out, lhsT=lhsT, rhs=rhs, start=start, stop=stop), r, w)

    def TR(out, in_, ident, r=(), w=()):
        P.pe(lambda e: e.transpose(out=out, in_=in_, identity=ident), r, w)

    def ACTV(out, in_, func, r=(), w=(), bias=None, scale=None, accum=None):
        kw = {}
        if bias is not None:
            kw["bias"] = bias
        if scale is not None:
            kw["scale"] = scale
        if accum is not None:
            kw["accum_out"] = accum
        P.act(lambda e: e.activation(out=out, in_=in_, func=func, **kw), r, w)

    def TS(eng, out, in0, s1, s2, op0, op1=None, r=(), w=()):
        if op1 is None:
            P.add(eng, lambda e: e.tensor_scalar(out=out, in0=in0, scalar1=s1, scalar2=None, op0=op0), r, w)
        else:
            P.add(eng, lambda e: e.tensor_scalar(out=out, in0=in0, scalar1=s1, scalar2=s2, op0=op0, op1=op1), r, w)

    def TT(eng, out, in0, in1, op, r=(), w=()):
        P.add(eng, lambda e: e.tensor_tensor(out=out, in0=in0, in1=in1, op=op), r, w)

    def STT(eng, out, in0, scalar, in1, op0, op1, r=(), w=()):
        P.add(eng, lambda e: e.scalar_tensor_tensor(out=out, in0=in0, scalar=scalar, in1=in1, op0=op0, op1=op1), r, w)

    def CP(eng, out, in_, r=(), w=()):
        if eng == "act":
            P.act(lambda e: e.copy(out=out, in_=in_), r, w)
        else:
            P.add(eng, lambda e: e.tensor_copy(out=out, in_=in_), r, w)

    def MEMSET(eng, ap, val, w=()):
        P.add(eng, lambda e: e.memset(ap, val), (), w)

    identf = A.tile([128, 128], F32)
    onesf = A.tile([128, 128], F32)
    U64 = A.tile([64, 64], F32)
    LS64 = A.tile([64, 64], F32)
    sel = A.tile([65, 64], F32)
    eps_t = A.tile([128, 1], F32)
    one_t = A.tile([128, 1], F32)
    DMA("sp", identf, identf_d[:, :], w=["identf"], grp="c0")
    DMA("sp", U64, U64_d[:, :], w=["U64"], grp="c1")
    DMA("sp", LS64, LS64_d[:, :], w=["LS64"], grp="c2")
    DMA("sp", sel, sel_d[:, :], w=["sel"], grp="c3")
    MEMSET("pool", onesf, 1.0, w=["onesf"])
    MEMSET("pool", eps_t, EPS, w=["eps"])
    MEMSET("pool", one_t, 1.0, w=["one"])
    P.barrier()
    base_mark = A.mark()

    TOK_TILES = [(i * 128, 128) for i in range(16)] + [(NPR, NS)]
    TOK_GROUPS = [(i * 512, 512) for i in range(4)] + [(NPR, NS)]

    def rstd_from_ss(ss, n, D_, r, w):
        ACTV(ss, ss, AF.Ln, r=r + ["eps"], w=w, bias=eps_t[0:n, 0:1], scale=1.0 / D_)
        ACTV(ss, ss, AF.Exp, r=w, w=w, scale=-0.5)

    def norm_transpose(xsrc, tiles, hT, hkey, gT, gkey, col0, tag, ps_base=0):
        xt = [A.tile([128, D], F32) for _ in range(2)]
        xn = [A.tile([128, D], F32) for _ in range(2)]
        sq = A.tile([128, D], BF16)
        ss = [A.tile([128, 1], F32) for _ in range(2)]
        for ti, (r0, n) in enumerate(tiles):
            s = ti % 2
            kx, kn, ks = "%s_xt%d" % (tag, s), "%s_xn%d" % (tag, s), "%s_ss%d" % (tag, s)
            DMA("sp", xt[s][0:n, :], xsrc[r0:r0 + n, :], w=[kx], grp=kx)
            ACTV(sq[0:n, :], xt[s][0:n, :], AF.Square, r=[kx], w=[tag + "_sq", ks], accum=ss[s][0:n, 0:1])
            rstd_from_ss(ss[s][0:n, 0:1], n, D, [ks], [ks])
            TS("dve", xn[s][0:n, :], xt[s][0:n, :], ss[s][0:n, 0:1], None, ALU.mult, r=[kx, ks], w=[kn])
            c = col0 + (r0 if r0 < NPR else r0) - tiles[0][0] if False else None
            for qd in range(4):
                b = ps_base + (ti * 4 + qd) % 4
                for kk in range(4):
                    k = qd * 4 + kk
                    TR(PS[b][:, kk * 128: kk * 128 + n], xn[s][0:n, k * 128:(k + 1) * 128], identf[0:n, 0:n],
                       r=[kn, "identf"], w=[psk(b)])
                cc = col0 + ti * 128 if tiles[0][1] == 128 else col0
                cc = col0 + (r0 - tiles[0][0])
                TT("dve", hT[:, qd * 4:qd * 4 + 4, cc:cc + n],
                   PS[b][:, :].rearrange("p (a b) -> p a b", a=4)[:, :, 0:n],
                   gT[:, qd * 4:qd * 4 + 4].unsqueeze(2).to_broadcast([128, 4, n]), ALU.mult,
                   r=[psk(b), gkey], w=[hkey + "_%d" % (cc // 512)])

    def hkeys(hkey, c0, n):
        return [hkey + "_%d" % g for g in range(c0 // 512, (c0 + n - 1) // 512 + 1)]

    for l in range(depth):
        xsrc = xin if l == 0 else xres
        last = (l == depth - 1)
        xdst = y if last else xres

        A.reset(base_mark)
        P.enabled = "12" in phases
        hT = A.tile([128, 16, NT], BF16)
        gT = A.tile([128, 16], F32)
        DMA("sp", gT, gpm[l], w=["gT"], grp="gT")
        m1 = A.mark()
        norm_transpose(xsrc, TOK_TILES, hT, "hT", gT, "gT", 0, "p1")
        allh = hkeys("hT", 0, NT)
        wt = [A.tile([128, 16, 128], BF16) for _ in range(3)]
        zt = [A.tile([128, NT], BF16) for _ in range(2)]
        ev = 0
        for j in range(NFM):
            s = j % 3
            kw_ = "wfm%d" % s
            DMA("pool", wt[s], w_fm[l][:, j * 128:(j + 1) * 128].rearrange("(k p) n -> p k n", p=128), w=[kw_], grp=kw_)
            zs = j % 2
            kz = "zt%d" % zs
            for gi, (t0, n) in enumerate(TOK_GROUPS):
                b = (j * 5 + gi) % 4
                for k in range(16):
                    MM(PS[b][:, 0:n], wt[s][:, k, :], hT[:, k, t0:t0 + n], start=(k == 0), stop=(k == 15),
                       r=[kw_] + hkeys("hT", t0, n), w=[psk(b)])
                CP("act" if ev % 2 == 0 else "dve", zt[zs][:, t0:t0 + n], PS[b][:, 0:n], r=[psk(b)], w=[kz])
                ev += 1
            DMA("sp", zT[j * 128:(j + 1) * 128, :], zt[zs], r=[kz], w=["zT%d" % j], grp=kz + "o")
        wtm = [A.tile([128, 16, 512], BF16) for _ in range(2)]
        ot = [A.tile([128, 512], F32) for _ in range(3)]
        oi = 0
        for cg in range(7):
            s = cg % 2
            kw_ = "wtm%d" % s
            o_ = 0
            for (c_, n_) in TM_SEGS[cg]:
                DMA("pool", wtm[s][:, :, o_:o_ + n_], w_fm[l][:, c_:c_ + n_].rearrange("(k p) n -> p k n", p=128), w=[kw_], grp=kw_)
                o_ += n_
            if cg < 4:
                tl = TOK_TILES
            else:
                tl = [(NPR - 3, 3), (NPR, NS)]
            for (r0, n) in tl:
                b = 4 + oi % 4
                os_ = oi % 3
                ko = "ot%d" % os_
                for k in range(16):
                    MM(PS[b][0:n, :], hT[:, k, r0:r0 + n], wtm[s][:, k, :], start=(k == 0), stop=(k == 15),
                       r=[kw_] + hkeys("hT", r0, n), w=[psk(b)])
                CP("act" if oi % 2 == 0 else "dve", ot[os_][0:n, :], PS[b][0:n, :], r=[psk(b)], w=[ko])
                if cg < 3:
                    DMA("sp", akv[l][r0:r0 + n, cg * 512:(cg + 1) * 512], ot[os_][0:n, :], r=[ko], w=[U("akv")], grp=ko + "o")
                elif cg == 3:
                    DMA("sp", ckv[l][r0:r0 + n, :], ot[os_][0:n, :], r=[ko], w=[U("ckv")], grp=ko + "o")
                else:
                    c0 = (cg - 4) * 512
                    if r0 < NPR:
                        DMA("sp", conv_out[l, 0, :, c0:c0 + 512], ot[os_][0:3, :], r=[ko], w=["convo"], grp=ko + "o")
                    else:
                        for sq_ in range(NSS):
                            DMA("sp", conv_out[l, 1 + sq_, :, c0:c0 + 512], ot[os_][4 * sq_ + 1:4 * sq_ + 4, :], r=[ko],
                                w=["convo"], grp=ko + "o")
                oi += 1
        P.barrier()

        A.reset(base_mark)
        P.enabled = "3a" in phases
        EA = A.tile([128, 36, 256], BF16)
        EC = A.tile([128, 12, 256], BF16)
        ESA = A.tile([128, 12, 36], BF16)
        ESC = A.tile([128, 12, 8], BF16)
        esink = A.tile([65, 12], F32)
        DMA("sp", EA, EA_d[:, :].rearrange("p (a b) -> p a b", a=36), w=["EA"], grp="EA")
        DMA("sp", EC, EC_d[:, :].rearrange("p (a b) -> p a b", a=12), w=["EC"], grp="EC")
        DMA("sp", ESA, ESA_d[:, :].rearrange("p (a b) -> p a b", a=12), w=["ESA"], grp="ESA")
        DMA("sp", ESC, ESC_d[:, :].rearrange("p (a b) -> p a b", a=12), w=["ESC"], grp="ESC")
        DMA("sp", esink[64:65, :], sinks[l][64:65, :], w=["esink"], grp="esink")
        ACTV(esink[64:65, :], esink[64:65, :], AF.Exp, r=["esink"], w=["esink"])
        qT = [A.tile([128, NT], BF16) for _ in range(2)]
        kT = [A.tile([128, NT], BF16) for _ in range(2)]
        acc = [A.tile([65, NT], F32) for _ in range(2)]
        Vt = [A.tile([128, 16, 65], BF16) for _ in range(2)]
        Pt = [A.tile([128, 256], BF16) for _ in range(3)]
        mo = [A.tile([64, NT], BF16) for _ in range(2)]
        NPB_A, NPB_C = 8, 1
        kc = [A.tile([128, 768], F32) for _ in range(2)]
        kcT = A.tile([128, 6, NPB_A * 128], BF16)
        vc = A.tile([128, NPB_A, 12, 65], BF16)
        vnew = A.tile([4, 12, 65], BF16)
        Ps = [A.tile([128, 36], BF16) for _ in range(2)]
        for s in range(2):
            MEMSET("pool", Vt[s][:, :, 64:65], 1.0, w=["Vt%d" % s])
        MEMSET("pool", vc[:, :, :, 64:65], 1.0, w=["vc"])
        MEMSET("pool", vnew[:, :, 64:65], 1.0, w=["vnew"])
        kbctr = [0]
        hctr = [0]

        vsrc_key = {}
        akv_l = akv[l]
        ckv_l = ckv[l]
        vsrc_key[id(akv_l)] = "akv"
        vsrc_key[id(ckv_l)] = "ckv"

        def load_sample_cache_A(sq_):
            for blk in range(NPB_A):
                if blk < 4:
                    rows = cdk[l, sq_, 512 + 128 * blk:512 + 128 * blk + 128, :]
                else:
                    rows = cdk[l, sq_, (blk - 4) * 128:(blk - 4) * 128 + 128, :]
                s = blk % 2
                kkc = "kc%d" % s
                DMA("sp", kc[s], rows[:, 0:768], w=[kkc], grp=kkc)
                DMA("pool", vc[:, blk, :, 0:64], rows[:, 768:1536].rearrange("p (h e) -> p h e", h=12), w=["cAv"], grp="vc")
                for pr in range(6):
                    b = 4 + (blk * 6 + pr) % 4
                    TR(PS[b][:, 0:128], kc[s][:, pr * 128:(pr + 1) * 128], identf, r=[kkc, "identf"], w=[psk(b)])
                    CP("act" if pr % 2 == 0 else "dve", kcT[:, pr, blk * 128:(blk + 1) * 128], PS[b][:, 0:128], r=[psk(b)], w=["cAk"])
            DMA("pool", vnew[:, :, 0:64], akv_l[NPR + 4 * sq_:NPR + 4 * sq_ + 4, 768:1536].rearrange("p (h e) -> p h e", h=12),
                r=["akv"], w=["cAn"], grp="vnew")

        accS = A.tile([65, 24, NS], F32)

        def sample_attn(sq_, hslot, qap, knew, nblk, kblk_fn, vblk_fn, vnew_ap, E_, rkeys):
            i = kbctr[0]
            kbctr[0] += 1
            bS, bO, pslot = 4 + i % 2, 6 + i % 2, i % 2
            kp = "Ps%d" % pslot
            for blk in range(nblk):
                MM(PS[bS][:, blk * 4:blk * 4 + 4], kblk_fn(blk), qap, r=rkeys, w=[psk(bS)])
            MM(PS[bS][0:4, nblk * 4:nblk * 4 + 4], knew, qap, r=rkeys, w=[psk(bS)])
            ACTV(Ps[pslot][:, 0:nblk * 4], PS[bS][:, 0:nblk * 4], AF.Exp, r=[psk(bS)], w=[kp], scale=ATT_SCALE)
            ACTV(Ps[pslot][0:4, nblk * 4:nblk * 4 + 4], PS[bS][0:4, nblk * 4:nblk * 4 + 4], AF.Exp, r=[psk(bS)], w=[kp], scale=ATT_SCALE)
            TT("dve", Ps[pslot][:, 0:nblk * 4], Ps[pslot][:, 0:nblk * 4], E_[:, 0:nblk * 4], ALU.mult, r=[kp, "ESA", "ESC"], w=[kp])
            TT("dve", Ps[pslot][0:4, nblk * 4:nblk * 4 + 4], Ps[pslot][0:4, nblk * 4:nblk * 4 + 4], E_[0:4, nblk * 4:nblk * 4 + 4],
               ALU.mult, r=[kp, "ESA", "ESC"], w=[kp])
            for blk in range(nblk):
                MM(PS[bO][0:65, 0:4], vblk_fn(blk), Ps[pslot][:, blk * 4:blk * 4 + 4], start=(blk == 0), stop=False,
                   r=[kp] + rkeys, w=[psk(bO)])
            MM(PS[bO][0:65, 0:4], vnew_ap, Ps[pslot][0:4, nblk * 4:nblk * 4 + 4], start=False, stop=True, r=[kp] + rkeys, w=[psk(bO)])
            CP("act", accS[:, hslot, 4 * sq_:4 * sq_ + 4], PS[bO][0:65, 0:4], r=[psk(bO)], w=["accS"])

        qsA = A.tile([128, 6, NS], BF16)
        ksA = A.tile([128, 6, NS], BF16)
        qsC = A.tile([128, 6, NS], BF16)
        ksC = A.tile([128, 2, NS], BF16)
        DMA("sp", qsA, zT[AQ * 128:(AQ + 6) * 128, NPR:NT].rearrange("(c p) t -> p c t", p=128), r=["zT%d" % j for j in range(AQ, AQ + 6)], w=["qsA"], grp="qsA")
        DMA("sp", ksA, zT[AK * 128:(AK + 6) * 128, NPR:NT].rearrange("(c p) t -> p c t", p=128), r=["zT%d" % j for j in range(AK, AK + 6)], w=["ksA"], grp="ksA")
        DMA("sp", qsC, zT[CQ * 128:(CQ + 6) * 128, NPR:NT].rearrange("(c p) t -> p c t", p=128), r=["zT%d" % j for j in range(CQ, CQ + 6)], w=["qsC"], grp="qsC")
        DMA("sp", ksC, zT[CK * 128:(CK + 2) * 128, NPR:NT].rearrange("(c p) t -> p c t", p=128), r=["zT%d" % j for j in range(CK, CK + 2)], w=["ksC"], grp="ksC")
        kcc = A.tile([128, 256], F32)
        kccT = A.tile([128, 2, 128], BF16)
        vcc = A.tile([128, 4, 65], BF16)
        vnewc = A.tile([4, 4, 65], BF16)
        MEMSET("pool", vcc[:, :, 64:65], 1.0, w=["cCv"])
        MEMSET("pool", vnewc[:, :, 64:65], 1.0, w=["cCn"])
        for sq_ in range(NSS):
            load_sample_cache_A(sq_)
            rk = ["cAk", "cAv", "cAn", "qsA", "ksA"]
            for h in range(12):
                pr, pb = h // 2, (h % 2) * 64
                sample_attn(sq_, h, qsA[pb:pb + 64, pr, 4 * sq_:4 * sq_ + 4], ksA[pb:pb + 64, pr, 4 * sq_:4 * sq_ + 4], NPB_A,
                            lambda blk, pr=pr, pb=pb: kcT[pb:pb + 64, pr, blk * 128:(blk + 1) * 128],
                            lambda blk, h=h: vc[:, blk, h, :], vnew[:, h, :], ESA[:, h, :], rk)
            DMA("sp", kcc, csw[l, sq_, :, 0:256], w=["kcc"], grp="kcc")
            DMA("pool", vcc[:, :, 0:64], csw[l, sq_, :, 256:512].rearrange("p (h e) -> p h e", h=4), w=["cCv"], grp="vcc")
            DMA("pool", vnewc[:, :, 0:64], ckv_l[NPR + 4 * sq_:NPR + 4 * sq_ + 4, 256:512].rearrange("p (h e) -> p h e", h=4),
                r=["ckv"], w=["cCn"], grp="vnewc")
            for pr in range(2):
                b = 4 + pr
                TR(PS[b][:, 0:128], kcc[:, pr * 128:(pr + 1) * 128], identf, r=["kcc", "identf"], w=[psk(b)])
                CP("act", kccT[:, pr, :], PS[b][:, 0:128], r=[psk(b)], w=["cC"])
            rk = ["cC", "cCv", "cCn", "qsC", "ksC"]
            for pos in range(12):
                n_kv = CPERM[pos] // 3
                pr, pb = pos // 2, (pos % 2) * 64
                sample_attn(sq_, 12 + pos, qsC[pb:pb + 64, pr, 4 * sq_:4 * sq_ + 4], ksC[pb:pb + 64, n_kv // 2, 4 * sq_:4 * sq_ + 4], NPB_C,
                            lambda blk, n_kv=n_kv, pb=pb: kccT[pb:pb + 64, n_kv // 2, :],
                            lambda blk, n_kv=n_kv: vcc[:, n_kv, :], vnewc[:, n_kv, :], ESC[:, pos, :], rk)

        def attn_prompt_head(qchunk, kchunk, half, branches, Eap_fn, vsrc, vkey, vcol, mix_row, sink_col, hslot, load_q, load_k, ps_q, ps_k):
            hs = hctr[0] % 2
            hctr[0] += 1
            kq, kk_, ka = "qT%d" % ps_q, "kT%d" % ps_k, "acc%d" % hs
            if load_q:
                DMA("sp", qT[ps_q], zT[qchunk * 128:(qchunk + 1) * 128, :], r=["zT%d" % qchunk], w=[kq], grp=kq)
            if load_k:
                DMA("sp", kT[ps_k], zT[kchunk * 128:(kchunk + 1) * 128, :], r=["zT%d" % kchunk], w=[kk_], grp=kk_)
            pb = half * 64
            MEMSET("pool", acc[hs][:, 0:NPR], 0.0, w=[ka])
            CP("pool", acc[hs][:, NPR:NT], accS[:, hslot, :], r=["accS"], w=[ka])
            pend = None
            for bi, dil in enumerate(branches):
                nblk = (NPR // dil) // 128
                vs = kbctr[0] % 2
                kbctr[0] += 1
                kv_ = "Vt%d" % vs
                vv = vsrc[0:NPR, vcol:vcol + 64].rearrange("(b p r) e -> r p b e", p=128, r=dil)
                for r_ in range(dil):
                    DMA("pool", Vt[vs][:, r_ * nblk:(r_ + 1) * nblk, 0:64], vv[r_], r=[vkey], w=[kv_ + "_%d" % r_] + ([kv_ + "_%d" % x for x in range(1, 16)] if r_ == 0 else []), grp=kv_)
                vkeys = [kv_ + "_%d" % r_ for r_ in range(dil)] + [kv_]
                E_ = Eap_fn(bi)
                for r_ in range(dil):
                    for blk in range(nblk):
                        nq = 256 if blk < nblk - 1 else 128
                        t0 = r_ + dil * 128 * blk
                        kap = kT[ps_k][pb:pb + 64, t0:t0 + dil * 127 + 1:dil]
                        qap = qT[ps_q][pb:pb + 64, t0:t0 + dil * (nq - 1) + 1:dil]
                        i = kbctr[0]
                        kbctr[0] += 1
                        bS, bO, pslot = i % 2, 2 + i % 2, i % 3
                        MM(PS[bS][:, 0:nq], kap, qap, r=[kq, kk_], w=[psk(bS)])
                        if pend is not None:
                            pend()
                        kp = "Pt%d" % pslot
                        ACTV(Pt[pslot][:, 0:nq], PS[bS][:, 0:nq], AF.Exp, r=[psk(bS)], w=[kp], scale=ATT_SCALE)
                        TT("dve", Pt[pslot][:, 0:nq], Pt[pslot][:, 0:nq], E_[:, 0:nq], ALU.mult, r=[kp, "EA", "EC"], w=[kp])

                        def fin(bO=bO, pslot=pslot, kp=kp, vs=vs, vkeys=vkeys, idx=r_ * nblk + blk, nq=nq, t0=t0, dil=dil):
                            MM(PS[bO][0:65, 0:nq], Vt[vs][:, idx, :], Pt[pslot][:, 0:nq], r=[kp] + vkeys, w=[psk(bO)])
                            av = acc[hs][:, t0:t0 + dil * (nq - 1) + 1:dil]
                            TT("dve", av, PS[bO][0:65, 0:nq], av, ALU.add, r=[psk(bO), ka], w=[ka])
                        pend = fin
            if pend is not None:
                pend()
            if sink_col is not None:
                TS("dve", acc[hs][64:65, :], acc[hs][64:65, :], esink[64:65, sink_col:sink_col + 1], None, ALU.add,
                   r=[ka, "esink"], w=[ka])
            P.dve(lambda e: e.reciprocal(out=acc[hs][64:65, :], in_=acc[hs][64:65, :]), [ka], [ka])
            km = "mo%d" % hs
            for gi, (t0, n) in enumerate(TOK_GROUPS):
                b = 4 + gi % 4
                MM(PS[b][0:64, 0:n], sel, acc[hs][:, t0:t0 + n], r=[ka, "sel"], w=[psk(b)])
                TT("dve", mo[hs][:, t0:t0 + n], acc[hs][0:64, t0:t0 + n], PS[b][0:64, 0:n], ALU.mult, r=[psk(b), ka], w=[km])
            DMA("sp", mixT[mix_row:mix_row + 64, :], mo[hs], r=[km], w=[U("mixT")], grp=km + "o")

        for h in range(12):
            pr, half = h // 2, h % 2
            attn_prompt_head(AQ + pr, AK + pr, half, [1, 4, 16], lambda bi, h=h: EA[:, h * 3 + bi, :], akv_l, "akv", 768 + h * 64,
                             h * 64, None, h, half == 0, half == 0, pr % 2, pr % 2)
        for pos in range(12):
            pr, half = pos // 2, pos % 2
            n_kv = CPERM[pos] // 3
            attn_prompt_head(CQ + pr, CK + n_kv // 2, half, [1], lambda bi, pos=pos: EC[:, pos, :], ckv_l, "ckv", 256 + n_kv * 64,
                             1280 + pos * 64, pos, 12 + pos, half == 0, pos % 6 == 0, pr % 2, (pos // 6) % 2)
        P.barrier()

        A.reset(base_mark)
        P.enabled = "3b" in phases
        QN = A.tile([128, 4, NPR], F32)
        KN = A.tile([128, 4, NPR], F32)
        VN = A.tile([128, 4, NPR], F32)
        ZS = A.tile([128, 4, NPR], BF16)
        X = A.tile([128, 3 + NPR], F32)
        SQ = A.tile([128, NPR], F32)
        MT = A.tile([36, NPR], F32)
        cw = A.tile([128, 48], F32)
        bp = A.tile([36, 2], F32)
        dgb = A.tile([64, 128], F32)
        Sst = A.tile([128, 4, 128], F32)
        tmpn = A.tile([128, 512], F32)
        DMA("sp", cw, convw[l], w=["cw"], grp="cw")
        DMA("sp", bp[32:36, :], bpar[l][32:36, :], w=["bp"], grp="bp")
        DMA("sp", dgb, dng[l][0:1, :].partition_broadcast(64), w=["dgb"], grp="dgb")
        ACTV(bp[32:36, 0:1], bp[32:36, 0:1], AF.Exp, r=["bp"], w=["bp"])
        TS("dve", bp[32:36, 0:1], bp[32:36, 0:1], -1.0, None, ALU.mult, r=["bp"], w=["bp"])
        U8 = 8
        bg = A.tile([64, 2, 36], F32)
        gcl = A.tile([64, 2, 8], F32)
        egc = A.tile([64, 2, 4], F32)
        ekd = A.tile([64, 2, 4], F32)
        nbeta = A.tile([64, 2, 4], F32)
        cbe = A.tile([64, 2, 4], F32)
        gcr = A.tile([128, U8, 64], F32)
        egr = A.tile([128, U8, 64], F32)
        tU = A.tile([64, U8, 64], F32)
        gU = A.tile([64, U8, 64], F32)
        tL = A.tile([64, U8, 64], F32)
        Nm = [A.tile([64, U8, 64], F32) for _ in range(2)]
        Pm = [A.tile([64, U8, 64], F32) for _ in range(2)]
        TTm = A.tile([64, U8, 64], F32)
        attnT = A.tile([64, U8, 64], F32)
        kbe = A.tile([64, U8, 128], F32)
        kd = A.tile([64, U8, 128], F32)
        vb = A.tile([64, U8, 128], F32)
        nwT = A.tile([128, U8, 64], F32)
        qdT = A.tile([128, U8, 64], F32)
        vn = A.tile([64, 4, 128], F32)
        osq = A.tile([64, 4, 128], F32)
        on = A.tile([64, 4, 128], F32)
        oss = A.tile([64, 4], F32)

        def delta_seq(L, Lpad, col0, conv_src, s0_src, oseq):
            nchunk = Lpad // 64
            sk = U("seq")
            en_seq = ("3b" in phases) and (conv_src is None or "3bs" in phases)
            P.enabled = en_seq
            if L < Lpad:
                for t_, nm in ((QN, "QN"), (KN, "KN"), (VN, "VN")):
                    MEMSET("pool", t_[:, :, 0:Lpad], 0.0, w=[nm])
                MEMSET("pool", MT[:, 0:Lpad], 0.0, w=["MT"])
            for kind, (tile_, nm, chunk0) in enumerate(((QN, "QN", BQ), (KN, "KN", BK), (VN, "VN", BV))):
                for hb in range(4):
                    ch = chunk0 + hb
                    wc = (kind * 4 + hb) * 4
                    if conv_src is None:
                        MEMSET("pool", X[:, 0:3], 0.0, w=["X"])
                    else:
                        P.dma("pool", lambda e, hb=hb, kind=kind: e.dma_start(
                            out=X[:, 0:3], in_=conv_src[:, (kind * 4 + hb) * 128:(kind * 4 + hb + 1) * 128].rearrange("t c -> c t"),
                            allow_slow_non_contiguous=True), [], ["X"], "Xc")
                    DMA("pool", X[:, 3:3 + L], zT[ch * 128:(ch + 1) * 128, col0:col0 + L], r=["zT%d" % ch], w=["X"], grp="X")
                    tg = tile_[:, hb, 0:L]
                    e0 = "dve"
                    TS(e0, tg, X[:, 0:L], cw[:, wc:wc + 1], None, ALU.mult, r=["X", "cw"], w=[nm])
                    for j in range(1, 4):
                        STT(e0, tg, X[:, j:j + L], cw[:, wc + j:wc + j + 1], tg, ALU.mult, ALU.add, r=["X", "cw", nm], w=[nm])
                    ACTV(tg, tg, AF.Silu, r=[nm], w=[nm])
                    if kind < 2:
                        TT("pool", SQ[:, 0:L], tg, tg, ALU.mult, r=[nm], w=["SQ"])
                        for g0 in range(0, L, 512):
                            n = min(512, L - g0)
                            b = (g0 // 512) % 4
                            MM(PS[b][:, 0:n], onesf, SQ[:, g0:g0 + n], r=["SQ", "onesf"], w=[psk(b)])
                            ACTV(tmpn[:, 0:n], PS[b][:, 0:n], AF.Ln, r=[psk(b), "eps"], w=["tmpn"], bias=eps_t[:, 0:1])
                            ACTV(tmpn[:, 0:n], tmpn[:, 0:n], AF.Exp, r=["tmpn"], w=["tmpn"], scale=-0.5)
                            if kind == 0:
                                STT("dve", tile_[:, hb, g0:g0 + n], tile_[:, hb, g0:g0 + n], float(128 ** -0.5), tmpn[:, 0:n], ALU.mult, ALU.mult,
                                    r=[nm, "tmpn"], w=[nm])
                            else:
                                TT("dve", tile_[:, hb, g0:g0 + n], tile_[:, hb, g0:g0 + n], tmpn[:, 0:n], ALU.mult, r=[nm, "tmpn"], w=[nm])
            for hb in range(4):
                ch = BZ + hb
                DMA("sp", ZS[:, hb, 0:L], zT[ch * 128:(ch + 1) * 128, col0:col0 + L], r=["zT%d" % ch], w=["ZS"], grp="ZS")
            ACTV(ZS[:, :, 0:L], ZS[:, :, 0:L], AF.Silu, r=["ZS"], w=["ZS"])
            DMA("pool", MT[0:36, 0:L], zT[BM * 128:BM * 128 + 36, col0:col0 + L], r=["zT%d" % BM], w=["MT"], grp="MT")
            ACTV(MT[0:4, 0:L], MT[0:4, 0:L], AF.Sigmoid, r=["MT"], w=["MT"])
            ACTV(MT[32:36, 0:L], MT[32:36, 0:L], AF.Exp, r=["MT", "bp"], w=["MT"], bias=bp[32:36, 1:2])
            ACTV(MT[32:36, 0:L], MT[32:36, 0:L], AF.Ln, r=["MT", "one"], w=["MT"], bias=one_t[32:36, 0:1])
            TS("dve", MT[32:36, 0:L], MT[32:36, 0:L], bp[32:36, 0:1], None, ALU.mult, r=["MT", "bp"], w=["MT"])
            if s0_src is None:
                MEMSET("pool", Sst, 0.0, w=["S"])
            else:
                DMA("sp", Sst, s0_src.rearrange("h k v -> k h v"), w=["S"], grp="S0")
            b2stop = max([int(p[1:]) for p in phases if p.startswith("s") and p[1:].isdigit()] + [0]) or 99

            def G(n):
                P.enabled = en_seq and ("3b2" in phases) and n <= b2stop
            G(0)
            for c0 in range(0, nchunk, 2):
                ncb = min(2, nchunk - c0)
                nu = ncb * 4
                cols = [slice(64 * (c0 + ci), 64 * (c0 + ci) + 64) for ci in range(ncb)]
                G(1)
                for ci in range(ncb):
                    TR(PS[0][0:64, ci * 36:(ci + 1) * 36], MT[0:36, cols[ci]], identf[0:36, 0:36], r=["MT", "identf"], w=[psk(0)])
                CP("dve", bg[:, 0:ncb, :], PS[0][0:64, 0:ncb * 36].rearrange("p (a b) -> p a b", a=ncb), r=[psk(0)], w=["bg"])
                for ci in range(ncb):
                    MM(PS[1][0:64, ci * 8:ci * 8 + 4], U64, bg[:, ci, 32:36], r=["bg", "U64"], w=[psk(1)])
                    MM(PS[1][0:64, ci * 8 + 4:ci * 8 + 8], onesf[0:64, 0:64], bg[:, ci, 32:36], r=["bg", "onesf"], w=[psk(1)])
                CP("dve", gcl[:, 0:ncb, :], PS[1][0:64, 0:ncb * 8].rearrange("p (a b) -> p a b", a=ncb), r=[psk(1)], w=["gcl"])
                ACTV(egc[:, 0:ncb, :], gcl[:, 0:ncb, 0:4], AF.Exp, r=["gcl"], w=["egc"])
                TT("dve", ekd[:, 0:ncb, :], gcl[:, 0:ncb, 4:8], gcl[:, 0:ncb, 0:4], ALU.subtract, r=["gcl"], w=["ekd"])
                ACTV(ekd[:, 0:ncb, :], ekd[:, 0:ncb, :], AF.Exp, r=["ekd"], w=["ekd"])
                TS("dve", nbeta[:, 0:ncb, :], bg[:, 0:ncb, 0:4], -1.0, None, ALU.mult, r=["bg"], w=["nbeta"])
                TT("dve", cbe[:, 0:ncb, :], bg[:, 0:ncb, 0:4], egc[:, 0:ncb, :], ALU.mult, r=["bg", "egc"], w=["cbe"])
                G(3)
                for ci in range(ncb):
                    for hb in range(4):
                        u = ci * 4 + hb
                        TS("dve", gU[:, u, :], U64, bg[:, ci, 32 + hb:33 + hb], None, ALU.mult, r=["bg", "U64"], w=["gU"])
                        MM(PS[2][:, u * 64:(u + 1) * 64], onesf[0:64, :], gU[:, u, :], r=["gU", "onesf"], w=[psk(2)])
                psv2 = PS[2][:, 0:nu * 64].rearrange("p (a b) -> p a b", a=nu)
                CP("dve", gcr[:, 0:nu, :], psv2, r=[psk(2)], w=["gcr"])
                ACTV(egr[:, 0:nu, :], gcr[:, 0:nu, :], AF.Exp, r=["gcr"], w=["egr"])
                G(4)
                for ci in range(ncb):
                    for hb in range(4):
                        u = ci * 4 + hb
                        TS("dve", tU[:, u, :], gcr[0:64, u, :], gcl[:, ci, hb:hb + 1], 0.0, ALU.subtract, ALU.min, r=["gcr", "gcl"], w=["tU"])
                        TS("dve", tL[:, u, :], gcr[0:64, u, :], gcl[:, ci, hb:hb + 1], 0.0, ALU.subtract, ALU.max, r=["gcr", "gcl"], w=["tL"])
                ACTV(tU[:, 0:nu, :], tU[:, 0:nu, :], AF.Exp, r=["tU"], w=["tU"])
                ACTV(tL[:, 0:nu, :], tL[:, 0:nu, :], AF.Exp, r=["tL"], w=["tL"], scale=-1.0)
                TT("dve", tU[:, 0:nu, :], tU[:, 0:nu, :], U64.unsqueeze(1).to_broadcast([64, nu, 64]), ALU.mult, r=["tU", "U64"], w=["tU"])
                TT("pool", tL[:, 0:nu, :], tL[:, 0:nu, :], LS64.unsqueeze(1).to_broadcast([64, nu, 64]), ALU.mult, r=["tL", "LS64"], w=["tL"])
                G(5)
                for ci in range(ncb):
                    for hb in range(4):
                        u = ci * 4 + hb
                        MM(PS[3][0:64, u * 64:(u + 1) * 64], KN[:, hb, cols[ci]], KN[:, hb, cols[ci]], r=["KN"], w=[psk(3)])
                        MM(PS[4][0:64, u * 64:(u + 1) * 64], KN[:, hb, cols[ci]], QN[:, hb, cols[ci]], r=["KN", "QN"], w=[psk(4)])
                for ci in range(ncb):
                    for hb in range(4):
                        u = ci * 4 + hb
                        STT("dve", Nm[0][:, u, :], PS[3][0:64, u * 64:(u + 1) * 64], nbeta[:, ci, hb:hb + 1], tL[:, u, :], ALU.mult, ALU.mult,
                            r=[psk(3), "nbeta", "tL"], w=["N0"])
                TT("dve", attnT[:, 0:nu, :], PS[4][0:64, 0:nu * 64].rearrange("p (a b) -> p a b", a=nu), tU[:, 0:nu, :], ALU.mult,
                   r=[psk(4), "tU"], w=["attnT"])
                G(6)
                for u in range(nu):
                    TR(PS[5][0:64, u * 64:(u + 1) * 64], Nm[0][:, u, :], identf[0:64, 0:64], r=["N0", "identf"], w=[psk(5)])
                psv5 = PS[5][0:64, 0:nu * 64].rearrange("p (a b) -> p a b", a=nu)
                CP("act", Pm[0][:, 0:nu, :], psv5, r=[psk(5)], w=["P0"])
                TT("dve", TTm[:, 0:nu, :], Pm[0][:, 0:nu, :], identf[0:64, 0:64].unsqueeze(1).to_broadcast([64, nu, 64]), ALU.add, r=["P0", "identf"], w=["TT"])
                G(7)
                cur = 0
                for step in range(1, 6):
                    nx = 1 - cur
                    for u in range(nu):
                        MM(PS[6][0:64, u * 64:(u + 1) * 64], Pm[cur][:, u, :], Nm[cur][:, u, :], r=["P%d" % cur, "N%d" % cur], w=[psk(6)])
                    CP("act", Nm[nx][:, 0:nu, :], PS[6][0:64, 0:nu * 64].rearrange("p (a b) -> p a b", a=nu), r=[psk(6)], w=["N%d" % nx])
                    if step < 5:
                        for u in range(nu):
                            MM(PS[7][0:64, u * 64:(u + 1) * 64], Nm[cur][:, u, :], Pm[cur][:, u, :], r=["P%d" % cur, "N%d" % cur], w=[psk(7)])
                        CP("dve", Pm[nx][:, 0:nu, :], PS[7][0:64, 0:nu * 64].rearrange("p (a b) -> p a b", a=nu), r=[psk(7)], w=["P%d" % nx])
                    for u in range(nu):
                        MM(PS[5][0:64, u * 64:(u + 1) * 64], Nm[nx][:, u, :], TTm[:, u, :], r=["N%d" % nx, "TT"], w=[psk(5)])
                    TT("dve", TTm[:, 0:nu, :], TTm[:, 0:nu, :], PS[5][0:64, 0:nu * 64].rearrange("p (a b) -> p a b", a=nu), ALU.add,
                       r=[psk(5), "TT"], w=["TT"])
                    cur = nx
                G(8)
                for which, (src, nm) in enumerate(((KN, "KN"), (VN, "VN"))):
                    for ci in range(ncb):
                        for hb in range(4):
                            u = ci * 4 + hb
                            b = 0 + (u // 4) + 2 * which
                            TR(PS[b][0:64, (u % 4) * 128:(u % 4 + 1) * 128], src[:, hb, cols[ci]], identf, r=[nm, "identf"], w=[psk(b)])
                    for ci in range(ncb):
                        for hb in range(4):
                            u = ci * 4 + hb
                            b = 0 + (u // 4) + 2 * which
                            pv = PS[b][0:64, (u % 4) * 128:(u % 4 + 1) * 128]
                            if which == 0:
                                ACTV(kbe[:, u, :], pv, AF.Copy, r=[psk(b), "cbe"], w=["kbe"], scale=cbe[:, ci, hb:hb + 1])
                                ACTV(kd[:, u, :], pv, AF.Copy, r=[psk(b), "ekd"], w=["kd"], scale=ekd[:, ci, hb:hb + 1])
                            else:
                                ACTV(vb[:, u, :], pv, AF.Copy, r=[psk(b), "bg"], w=["vb"], scale=bg[:, ci, hb:hb + 1])
                G(9)
                for u in range(nu):
                    MM(PS[6][:, u * 64:(u + 1) * 64], kbe[:, u, :], TTm[:, u, :], r=["kbe", "TT"], w=[psk(6)])
                TS("dve", nwT[:, 0:nu, :], PS[6][:, 0:nu * 64].rearrange("p (a b) -> p a b", a=nu), -1.0, None, ALU.mult, r=[psk(6)], w=["nwT"])
                for ci in range(ncb):
                    TT("pool", qdT[:, ci * 4:ci * 4 + 4, :], QN[:, :, cols[ci]], egr[:, ci * 4:ci * 4 + 4, :], ALU.mult, r=["QN", "egr"], w=["qdT"])
                G(10)
                for ci in range(ncb):
                    for hb in range(4):
                        u = ci * 4 + hb
                        MM(PS[7][0:64, hb * 128:(hb + 1) * 128], TTm[:, u, :], vb[:, u, :], start=True, stop=False, r=["TT", "vb"], w=[psk(7)])
                        MM(PS[7][0:64, hb * 128:(hb + 1) * 128], nwT[:, u, :], Sst[:, hb, :], start=False, stop=True, r=["nwT", "S"], w=[psk(7)])
                    CP("act", vn, PS[7][0:64, :].rearrange("p (a b) -> p a b", a=4), r=[psk(7)], w=["vn"])
                    for hb in range(4):
                        u = ci * 4 + hb
                        MM(PS[4][0:64, hb * 128:(hb + 1) * 128], qdT[:, u, :], Sst[:, hb, :], start=True, stop=False, r=["qdT", "S"], w=[psk(4)])
                        MM(PS[4][0:64, hb * 128:(hb + 1) * 128], attnT[:, u, :], vn[:, hb, :], start=False, stop=True, r=["attnT", "vn"], w=[psk(4)])
                    for hb in range(4):
                        u = ci * 4 + hb
                        MM(PS[3][:, hb * 128:(hb + 1) * 128], kd[:, u, :], vn[:, hb, :], r=["kd", "vn"], w=[psk(3)])
                    TT("dve", Sst, Sst, egr[:, ci * 4:ci * 4 + 4, 63:64].to_broadcast([128, 4, 128]), ALU.mult, r=["S", "egr"], w=["S"])
                    TT("dve", Sst, Sst, PS[3][:, :].rearrange("p (a b) -> p a b", a=4), ALU.add, r=["S", psk(3)], w=["S"])
                    psv4 = PS[4][0:64, :].rearrange("p (a b) -> p a b", a=4)
                    ACTV(osq, psv4, AF.Square, r=[psk(4)], w=["osq"])
                    P.dve(lambda e: e.tensor_reduce(out=oss, in_=osq, axis=AX.X, op=ALU.add), ["osq"], ["oss"])
                    ACTV(oss, oss, AF.Ln, r=["oss", "eps"], w=["oss"], bias=eps_t[0:64, 0:1], scale=1.0 / 128)
                    ACTV(oss, oss, AF.Exp, r=["oss"], w=["oss"], scale=-0.5)
                    TT("dve", on, psv4, oss.unsqueeze(2).to_broadcast([64, 4, 128]), ALU.mult, r=[psk(4), "oss"], w=["on"])
                    TT("pool", on, on, dgb.unsqueeze(1).to_broadcast([64, 4, 128]), ALU.mult, r=["on", "dgb"], w=["on"])
                    for hb in range(4):
                        TR(PS[2][:, hb * 64:(hb + 1) * 64], on[:, hb, :], identf[0:64, 0:64], r=["on", "identf"], w=[psk(2)])
                    TT("dve", ZS[:, :, cols[ci]], ZS[:, :, cols[ci]], PS[2][:, 0:256].rearrange("p (a b) -> p a b", a=4), ALU.mult,
                       r=["ZS", psk(2)], w=["ZS"])
            P.enabled = en_seq
            for hb in range(4):
                DMA("sp", mixT[768 + hb * 128:768 + (hb + 1) * 128, col0:col0 + L], ZS[:, hb, 0:L], r=["ZS"], w=[U("mixT")], grp="zso")
            DMA("sp", s_out[l, oseq].rearrange("h k v -> k h v"), Sst, r=["S"], w=[U("sout")], grp="so")

        delta_seq(NPR, NPR, 0, None, None, 0)
        for sq_ in range(NSS):
            delta_seq(4, 64, NPR + 4 * sq_, sdc[l, sq_], sds[l, sq_], 1 + sq_)
        P.barrier()

        A.reset(base_mark)
        P.enabled = "4" in phases
        wo = A.tile([128, 16, D], BF16)
        gb = A.tile([128, D], F32)
        for q4 in range(4):
            DMA("pool", wo[:, q4 * 4:(q4 + 1) * 4, :], w_out[l][q4 * 512:(q4 + 1) * 512, :].rearrange("(k p) n -> p k n", p=128), w=["wo"], grp="wo%d" % q4)
        DMA("sp", gb, gom[l][0:1, :].partition_broadcast(128), w=["gb"], grp="gb")
        mt_ = [A.tile([128, 16, 128], BF16) for _ in range(2)]
        xt4 = [A.tile([128, D], F32) for _ in range(2)]
        yo = [A.tile([128, D], F32) for _ in range(2)]
        sqj = A.tile([128, 512], BF16)
        ss4 = [A.tile([128, 4], F32) for _ in range(2)]
        ss4r = [A.tile([128, 1], F32) for _ in range(2)]
        for ti, (r0, n) in enumerate(TOK_TILES):
            s = ti % 2
            km_, kx, ky, ks = "mt%d" % s, "xt4%d" % s, "yo%d" % s, "ss4%d" % s
            DMA("sp", mt_[s][:, :, 0:n], mixT[:, r0:r0 + n].rearrange("(k p) t -> p k t", p=128), r=["mixT"], w=[km_], grp=km_)
            DMA("sp", xt4[s][0:n, :], xsrc[r0:r0 + n, :], w=[kx], grp=kx)
            for cgp in range(4):
                b = (ti % 2) * 4 + cgp
                for k in range(16):
                    MM(PS[b][0:n, :], mt_[s][:, k, 0:n], wo[:, k, cgp * 512:(cgp + 1) * 512], start=(k == 0), stop=(k == 15),
                       r=[km_, "wo"], w=[psk(b)])
                ACTV(sqj[0:n, :], PS[b][0:n, :], AF.Square, r=[psk(b)], w=["sqj", ks + "_%d" % cgp], accum=ss4[s][0:n, cgp:cgp + 1])
            P.dve(lambda e, s=s, n=n: e.tensor_reduce(out=ss4r[s][0:n, 0:1], in_=ss4[s][0:n, :], axis=AX.X, op=ALU.add),
                  [ks + "_%d" % c for c in range(4)], [ks])
            rstd_from_ss(ss4r[s][0:n, 0:1], n, D, [ks], [ks])
            for cgp in range(4):
                b = (ti % 2) * 4 + cgp
                STT("dve", yo[s][0:n, cgp * 512:(cgp + 1) * 512], PS[b][0:n, :], ss4r[s][0:n, 0:1], gb[0:n, cgp * 512:(cgp + 1) * 512],
                    ALU.mult, ALU.mult, r=[psk(b), ks, "gb"], w=[ky])
            TT("pool", yo[s][0:n, :], yo[s][0:n, :], xt4[s][0:n, :], ALU.add, r=[ky, kx], w=[ky])
            DMA("sp", xres[r0:r0 + n, :], yo[s][0:n, :], r=[ky], w=[U("xres")], grp=ky + "o")
        P.barrier()

        A.reset(base_mark)
        P.enabled = "5" in phases
        gl = A.tile([128, 16], F32)
        gb5 = A.tile([128, D], F32)
        DMA("sp", gl, gpl[l], w=["gl"], grp="gl")
        DMA("sp", gb5, gol[l][0:1, :].partition_broadcast(128), w=["gb5"], grp="gb5")
        hm = A.tile([128, 16, 528], BF16)
        yacc = A.tile([128, 5, D], F32)
        uT = [A.tile([128, 8, 528], BF16) for _ in range(2)]
        wu = [A.tile([128, 16, 128], BF16) for _ in range(3)]
        wd = [A.tile([128, D], BF16) for _ in range(10)]
        rl = [A.tile([128, 512], F32) for _ in range(2)]
        m5 = A.mark()
        groups = [[(g * 512 + i * 128, 128) for i in range(4)] for g in range(4)]
        groups[3].append((NPR, NS))
        fctr = 0
        for g, tiles in enumerate(groups):
            segs = [(g * 512, 512, 0)] + ([(NPR, NS, 512)] if g == 3 else [])
            A.reset(m5)
            norm_transpose(xres, tiles, hm, "hm", gl, "gl", 0, "p5", ps_base=0)
            allhm = hkeys("hm", 0, 528)
            for fb in range(8):
                us = fb % 2
                ku = "uT%d" % us
                for fl in range(8):
                    f = fb * 8 + fl
                    s = fctr % 3
                    kwu = "wu%d" % s
                    DMA("pool", wu[s], w_up[l][:, f * 128:(f + 1) * 128].rearrange("(k p) n -> p k n", p=128), w=[kwu], grp=kwu)
                    sd = fctr % 10
                    kwd = "wd%d" % sd
                    DMA("pool", wd[sd], w_down[l][f * 128:(f + 1) * 128, :], w=[kwd], grp=kwd)
                    for si, (t0, n, c0) in enumerate(segs):
                        b = (fctr * 2 + si) % 2
                        for k in range(16):
                            MM(PS[b][:, 0:n], wu[s][:, k, :], hm[:, k, c0:c0 + n], start=(k == 0), stop=(k == 15), r=[kwu] + allhm, w=[psk(b)])
                        rs_ = (fctr + si) % 2
                        kr = "rl%d" % rs_
                        ACTV(rl[rs_][:, 0:n], PS[b][:, 0:n], AF.Relu, r=[psk(b)], w=[kr])
                        TT("pool", uT[us][:, fl, c0:c0 + n], rl[rs_][:, 0:n], rl[rs_][:, 0:n], ALU.mult, r=[kr], w=[ku])
                    fctr += 1
                for ti, (r0, n) in enumerate(tiles):
                    c0 = ti * 128 if r0 < NPR else 512
                    for cgp in range(4):
                        b = 2 + (ti * 4 + cgp) % 6
                        for fl in range(8):
                            sd = (fctr - 8 + fl) % 10
                            MM(PS[b][0:n, :], uT[us][:, fl, c0:c0 + n], wd[sd][:, cgp * 512:(cgp + 1) * 512], start=(fl == 0), stop=(fl == 7),
                               r=[ku, "wd%d" % sd], w=[psk(b)])
                        ya = yacc[0:n, ti, cgp * 512:(cgp + 1) * 512]
                        if fb == 0:
                            CP("act", ya, PS[b][0:n, :], r=[psk(b)], w=["yacc%d" % ti])
                        else:
                            TT("dve", ya, ya, PS[b][0:n, :], ALU.add, r=[psk(b), "yacc%d" % ti], w=["yacc%d" % ti])
            xt5 = [A.tile([128, D], F32) for _ in range(2)]
            sq5 = A.tile([128, D], BF16)
            ss5 = [A.tile([128, 1], F32) for _ in range(2)]
            for ti, (r0, n) in enumerate(tiles):
                s = ti % 2
                kx, ks, kya = "xt5%d" % s, "ss5%d" % s, "yacc%d" % ti
                DMA("sp", xt5[s][0:n, :], xres[r0:r0 + n, :], r=["xres"], w=[kx], grp=kx)
                ACTV(sq5[0:n, :], yacc[0:n, ti, :], AF.Square, r=[kya], w=["sq5", ks], accum=ss5[s][0:n, 0:1])
                rstd_from_ss(ss5[s][0:n, 0:1], n, D, [ks], [ks])
                STT("dve", yacc[0:n, ti, :], yacc[0:n, ti, :], ss5[s][0:n, 0:1], gb5[0:n, :], ALU.mult, ALU.mult, r=[kya, ks, "gb5"], w=[kya])
                TT("pool", xt5[s][0:n, :], xt5[s][0:n, :], yacc[0:n, ti, :], ALU.add, r=[kx, kya], w=[kx])
                DMA("sp", xdst[r0:r0 + n, :], xt5[s][0:n, :], r=[kx], w=[U("xdst")], grp=kx + "o")
        P.barrier()

    P.emit(st)
    st.close()
    return nc, P


def _constants():
    bf = ml_dtypes.bfloat16
    sa = alibi_slopes(12)
    p = np.arange(128)[:, None].astype(np.float64)
    q = np.arange(256)[None, :].astype(np.float64)
    dist = q - p
    valid = (dist >= 0) & (dist <= 128)
    EA = np.zeros((128, 36, 256), np.float64)
    for h in range(12):
        for bi, dil in enumerate((1, 4, 16)):
            EA[:, h * 3 + bi, :] = np.where(valid, np.exp(-sa[h] * dil * np.maximum(dist, 0)), 0.0)
    EC = np.zeros((128, 12, 256), np.float64)
    for pos in range(12):
        EC[:, pos, :] = np.where(valid, np.exp(-sa[CPERM[pos]] * np.maximum(dist, 0)), 0.0)
    ESA = np.zeros((128, 12, 36), np.float64)
    pp = np.arange(128)
    for h in range(12):
        for qi in range(4):
            for c in range(4):
                rho = 1536 + 128 * c + pp
                d_ = 2048 + qi - rho
                w = np.where(d_ <= 128, np.exp(-sa[h] * d_), 0.0)
                w = w + np.where((d_ % 4 == 0) & (d_ <= 512), np.exp(-sa[h] * d_), 0.0)
                ESA[:, h, c * 4 + qi] = w
            for r in range(4):
                rho = r + 16 * pp
                d_ = 2048 + qi - rho
                ESA[:, h, (4 + r) * 4 + qi] = np.where(r == qi, np.exp(-sa[h] * d_), 0.0)
            for j in range(4):
                w = 0.0
                if j < qi:
                    w = np.exp(-sa[h] * (qi - j))
                elif j == qi:
                    w = 3.0
                ESA[j, h, 32 + qi] = w
    ESC = np.zeros((128, 12, 8), np.float64)
    for pos in range(12):
        s_ = sa[CPERM[pos]]
        for qi in range(4):
            d_ = 128 + qi - pp
            ESC[:, pos, qi] = np.where(d_ <= 128, np.exp(-s_ * d_), 0.0)
            for j in range(4):
                if j <= qi:
                    ESC[j, pos, 4 + qi] = np.exp(-s_ * (qi - j))
    i64 = np.arange(64)
    U64 = (i64[None, :] >= i64[:, None]).astype(np.float32)
    LS64 = (i64[None, :] < i64[:, None]).astype(np.float32)
    sel = np.zeros((65, 64), np.float32)
    sel[64, :] = 1.0
    return dict(identf=np.eye(128, dtype=np.float32), EA=EA.reshape(128, -1).astype(bf), EC=EC.reshape(128, -1).astype(bf),
                ESA=ESA.reshape(128, -1).astype(bf), ESC=ESC.reshape(128, -1).astype(bf), U64=U64, LS64=LS64, sel=sel)


def _fm_columns():
    cols = []
    cols += list(range(0, 768))
    cols += list(range(768, 1536))
    cols += list(range(2304, 2304 + 1536))
    cols += list(range(3840, 3840 + 512))
    misc = [-1] * 128
    for h in range(4):
        misc[h] = 4352 + h
        misc[32 + h] = 4356 + h
    cols += misc
    cq0 = 4360
    for pos in range(12):
        cols += list(range(cq0 + CPERM[pos] * 64, cq0 + CPERM[pos] * 64 + 64))
    ck0 = 4360 + 768
    cols += list(range(ck0, ck0 + 256))
    assert len(cols) == NFM * 128
    return np.asarray(cols)


_CACHE = {}
_CDK_ROWS = np.concatenate([np.arange(r, 2048, 16) for r in range(4)] + [np.arange(1536, 2048)])


def _prepare_weights(depth, w_in, w_out):
    fm = _fm_columns()
    w_in = np.asarray(w_in)
    w_fm = np.zeros((depth, D, WALL), np.float32)
    ok = np.nonzero(fm >= 0)[0]
    w_fm[:, :, ok] = w_in[:depth][:, :, fm[ok]]
    w_fm[:, :, NFM * 128:NFM * 128 + 768] = w_in[:depth][:, :, 1536:2304]
    w_fm[:, :, NFM * 128 + 768:] = w_in[:depth][:, :, 5128 + 256:5640]
    rows = list(range(0, 1280))
    for pos in range(12):
        rows += list(range(1280 + CPERM[pos] * 64, 1280 + CPERM[pos] * 64 + 64))
    w_out_p = np.ascontiguousarray(np.asarray(w_out)[:depth][:, np.asarray(rows), :])
    return w_fm, w_out_p


def kernel(x_prompt, x_sample, cache_dilated_kv, cache_swa_kv, state_delta_s, state_delta_conv,
           g_pre_mix, w_in, delta_conv_w, delta_a_log, delta_dt_bias, delta_norm_g, swa_sinks,
           w_out, g_post_mix, g_pre_mlp, w_up, w_down, g_post_mlp, _depth=None, _phases=ALL_PHASES, _ncores=8):
    depth = DEPTH if _depth is None else _depth
    f32 = np.float32
    x_prompt = np.asarray(x_prompt, f32)
    x_sample = np.asarray(x_sample, f32)
    if "nc" not in _CACHE or _CACHE.get("depth") != (depth, tuple(_phases)):
        _CACHE["nc"] = build_program(depth, tuple(_phases))[0]
        _CACHE["depth"] = (depth, tuple(_phases))
    nc = _CACHE["nc"]
    consts = _constants()
    w_fm, w_out_p = _prepare_weights(depth, w_in, w_out)
    gT = lambda g: np.ascontiguousarray(np.asarray(g, f32)[:depth].reshape(depth, 16, 128).transpose(0, 2, 1))
    convw = np.asarray(delta_conv_w, f32)[:depth]
    convw = np.ascontiguousarray(convw.reshape(depth, 4, 12, 128).transpose(0, 3, 2, 1).reshape(depth, 128, 48))
    bpar = np.zeros((depth, 36, 2), f32)
    bpar[:, 32:36, 0] = np.asarray(delta_a_log, f32)[:depth]
    bpar[:, 32:36, 1] = np.asarray(delta_dt_bias, f32)[:depth]
    sk = np.zeros((depth, 65, 12), f32)
    sk[:, 64, :] = np.asarray(swa_sinks, f32)[:depth][:, CPERM]
    shared = dict(w_fm=w_fm, w_out=w_out_p, w_up=np.asarray(w_up, f32)[:depth], w_down=np.asarray(w_down, f32)[:depth],
                  gpm=gT(g_pre_mix), gpl=gT(g_pre_mlp), gom=np.asarray(g_post_mix, f32)[:depth].reshape(depth, 1, D),
                  gol=np.asarray(g_post_mlp, f32)[:depth].reshape(depth, 1, D), convw=convw, bpar=bpar,
                  dng=np.asarray(delta_norm_g, f32)[:depth].reshape(depth, 1, 128), sinks=sk, **consts)
    cdk_all = np.asarray(cache_dilated_kv, f32)
    csw_all = np.asarray(cache_swa_kv, f32)
    sds_all = np.asarray(state_delta_s, f32)
    sdc_all = np.asarray(state_delta_conv, f32)
    if "12" not in _phases:
        del shared["w_fm"]
    if "4" not in _phases:
        del shared["w_out"]
    if "5" not in _phases:
        del shared["w_up"], shared["w_down"]
    in_maps = []
    for c in range(_ncores):
        b = c % 4
        ss = slice(NSS * c, NSS * c + NSS)
        m = dict(shared)
        m["xin"] = np.concatenate([x_prompt[b], x_sample[ss].reshape(NS, D)], axis=0)
        if "3a" in _phases:
            m["cdk"] = np.ascontiguousarray(cdk_all[:depth, ss][:, :, _CDK_ROWS].reshape(depth, NSS, 1024, 1536))
        m["csw"] = np.ascontiguousarray(csw_all[:depth, ss].reshape(depth, NSS, 128, 512))
        m["sds"] = np.ascontiguousarray(sds_all[:depth, ss])
        m["sdc"] = np.ascontiguousarray(sdc_all[:depth, ss])
        in_maps.append(m)
    res = run_bass_kernel_spmd(nc, in_maps, core_ids=list(range(_ncores)))
    R = list(res.results)
    R = R + [R[i % _ncores] for i in range(len(R), 8)]
    if "dbg" in _phases:
        _CACHE["dbg"] = [{k: np.asarray(r[k]) for k in ("xres", "zT", "mixT")} for r in R]
    B = x_prompt.shape[0]
    yp = np.stack([R[b]["y"][:NPR] for b in range(B)])
    ys = np.concatenate([R[c]["y"][NPR:].reshape(NSS, 4, D) for c in range(8)])
    p_akv = np.stack([R[b]["akv"][:, :NPR].reshape(depth, NPR, 2, 12, 64) for b in range(B)], axis=1)
    p_ckv = np.stack([R[b]["ckv"][:, NPR - 128:NPR].reshape(depth, 128, 2, 4, 64) for b in range(B)], axis=1)
    p_s = np.stack([R[b]["s_out"][:, 0] for b in range(B)], axis=1)
    p_conv = np.stack([R[b]["conv_out"][:, 0] for b in range(B)], axis=1)
    s_akv = np.concatenate([R[c]["akv"][:, NPR:].reshape(depth, NSS, 4, 2, 12, 64) for c in range(8)], axis=1)
    s_ckv = np.concatenate([R[c]["ckv"][:, NPR:].reshape(depth, NSS, 4, 2, 4, 64) for c in range(8)], axis=1)
    s_s = np.concatenate([R[c]["s_out"][:, 1:] for c in range(8)], axis=1)
    s_conv = np.concatenate([R[c]["conv_out"][:, 1:] for c in range(8)], axis=1)
    outs = (yp, ys, p_akv, p_ckv, p_s, p_conv, s_akv, s_ckv, s_s, s_conv)
    return tuple(np.ascontiguousarray(o, dtype=np.float32) for o in outs)
```

```python
import numpy as np
import ml_dtypes
from collections import defaultdict
from contextlib import ExitStack
import concourse.bass as bass
import concourse.mybir as mybir
from concourse.bass_utils import run_bass_kernel_spmd

F32 = mybir.dt.float32
BF16 = mybir.dt.bfloat16
ALU = mybir.AluOpType
AF = mybir.ActivationFunctionType
AX = mybir.AxisListType

DEPTH = 4
D = 2048
NPR = 2048
NSS = 4
NS = 16
NT = NPR + NS
DFF = 8192
EPS = 1e-6
NFM = 37
WALL = NFM * 128 + 1024
TM_SEGS = [[(768, 512)], [(1280, 256), (4736, 256)], [(4992, 512)], [(4480, 256), (5504, 256)],
           [(1536, 512)], [(2048, 512)], [(2560, 512)]]
AQ, AK, BQ, BK, BV, BZ, BM, CQ, CK = 0, 6, 12, 16, 20, 24, 28, 29, 35
CPERM = [0, 3, 1, 4, 2, 5, 6, 9, 7, 10, 8, 11]
ATT_SCALE = 0.125
ENGS = ("pe", "act", "dve", "pool", "sp")


def alibi_slopes(n):
    return np.asarray([2.0 ** (-8.0 * (i + 1) / n) for i in range(n)], dtype=np.float64)


class Op:
    __slots__ = ("eng", "fn", "reads", "writes", "dma", "idx", "deps", "signal", "seq", "grp", "semkey", "bar")

    def __init__(self, eng, fn, reads, writes, dma, grp):
        self.eng = eng
        self.fn = fn
        self.reads = tuple(reads)
        self.writes = tuple(writes)
        self.dma = dma
        self.grp = grp
        self.deps = []
        self.signal = False
        self.seq = 0
        self.bar = False


class Prog:
    def __init__(self, nc):
        self.nc = nc
        self.ops = []
        self.enabled = True

    def add(self, eng, fn, reads=(), writes=(), dma=False, grp=None):
        op = Op(eng, fn, reads, writes, dma, grp)
        if self.enabled:
            self.ops.append(op)
        return op

    def pe(self, fn, r=(), w=()):
        return self.add("pe", fn, r, w)

    def act(self, fn, r=(), w=()):
        return self.add("act", fn, r, w)

    def dve(self, fn, r=(), w=()):
        return self.add("dve", fn, r, w)

    def pool(self, fn, r=(), w=()):
        return self.add("pool", fn, r, w)

    def dma(self, q, fn, r=(), w=(), grp=None):
        return self.add(q, fn, r, w, dma=True, grp=grp)

    def barrier(self):
        op = Op("sp", None, (), (), False, None)
        op.bar = True
        self.ops.append(op)

    def emit(self, stack):
        nc = self.nc
        ops = self.ops
        last_w = {}
        rds = defaultdict(list)
        last_eng = {}
        last_grp = {}
        bar_deps = []
        for i, op in enumerate(ops):
            op.idx = i
            if op.bar:
                bar_deps = list(last_eng.values()) + list(last_grp.values())
                for d in bar_deps:
                    d.signal = True
                last_w = {}
                rds = defaultdict(list)
                continue
            deps = {}
            for k in op.reads:
                w = last_w.get(k)
                if w is not None:
                    deps[w.idx] = "raw"
            for k in op.writes:
                w = last_w.get(k)
                if w is not None and deps.get(w.idx) != "raw":
                    deps[w.idx] = "waw"
                for r in rds[k]:
                    if r.idx not in deps:
                        deps[r.idx] = "war"
            for k in op.reads:
                if op.dma:
                    rds[k].append(op)
                else:
                    rds[k] = [r_ for r_ in rds[k] if r_.dma or r_.eng != op.eng] + [op]
            for k in op.writes:
                last_w[k] = op
                rds[k] = []
            keep = list(bar_deps)
            for j, kind in deps.items():
                d = ops[j]
                if d is op:
                    continue
                if (not d.dma) and (not op.dma) and d.eng == op.eng and kind != "raw" and op.eng == "pe":
                    continue
                keep.append(d)
                d.signal = True
            op.deps = keep
            if op.dma:
                op.signal = True
                last_grp[op.grp] = op
            else:
                last_eng[op.eng] = op
        cnt = defaultdict(int)
        for op in ops:
            if op.bar:
                continue
            if op.signal:
                key = ("g", op.grp) if op.dma else ("e", op.eng)
                cnt[key] += 16 if op.dma else 1
                op.seq = cnt[key]
                op.semkey = key
        sems = {}
        for n, key in enumerate(cnt):
            sems[key] = stack.enter_context(nc.semaphore("s%d" % n))
        per_eng = defaultdict(list)
        for op in ops:
            if not op.bar:
                per_eng[op.eng].append(op)
        block = stack.enter_context(nc.Block())
        totals = dict(cnt)
        self.n_waits = 0

        def run_engine(name, e):
            waited = defaultdict(int)
            for op in per_eng[name]:
                need = {}
                for d in op.deps:
                    k = d.semkey
                    if d.seq > need.get(k, 0):
                        need[k] = d.seq
                for k, v in need.items():
                    if waited[k] >= v:
                        continue
                    e.wait_ge(sems[k], v)
                    self.n_waits += 1
                    waited[k] = v
                ins = op.fn(e)
                if op.signal:
                    ins.then_inc(sems[op.semkey], 16 if op.dma else 1)
            if name == "sp":
                for k, v in totals.items():
                    if waited[k] < v:
                        e.wait_ge(sems[k], v)

        @block.sync
        def _(e):
            run_engine("sp", e)

        @block.scalar
        def _(e):
            run_engine("act", e)

        @block.vector
        def _(e):
            run_engine("dve", e)

        @block.gpsimd
        def _(e):
            run_engine("pool", e)

        @block.tensor
        def _(e):
            run_engine("pe", e)


class Arena:
    def __init__(self, ap, nbytes):
        self.ap = ap
        self.nbytes = nbytes
        self.off = 0

    def mark(self):
        return self.off

    def reset(self, m=0):
        self.off = m

    def tile(self, shape, dt, parts=None):
        n = int(np.prod(shape[1:]))
        nb = n * (2 if dt == BF16 else 4)
        nb = (nb + 63) // 64 * 64
        off = self.off
        self.off += nb
        assert self.off <= self.nbytes, "SBUF arena overflow %d" % self.off
        p = shape[0]
        if dt == BF16:
            a = self.ap[0:p, off // 4: off // 4 + nb // 4].bitcast(BF16)[:, 0:n]
        else:
            a = self.ap[0:p, off // 4: off // 4 + n]
        if len(shape) == 3:
            a = a.rearrange("p (a b) -> p a b", a=shape[1])
        elif len(shape) == 4:
            a = a.rearrange("p (a b c) -> p a b c", a=shape[1], b=shape[2])
        return a


ALL_PHASES = ("12", "3a", "3b", "3b2", "3bs", "4", "5")


def build_program(depth, phases=ALL_PHASES):
    nc = bass.Bass("TRN2", target_bir_lowering=False)

    def din(name, shape, dt=F32):
        return nc.dram_tensor(name, list(shape), dt, kind="ExternalInput").ap()

    def dout(name, shape, dt=F32):
        return nc.dram_tensor(name, list(shape), dt, kind="ExternalOutput").ap()

    def dscr(name, shape, dt):
        return nc.dram_tensor(name, list(shape), dt).ap()

    def din_if(cond, name, shape):
        return din(name, shape) if cond else nc.dram_tensor(name, list(shape), F32).ap()

    xin = din("xin", [NT, D])
    w_fm = din_if("12" in phases, "w_fm", [depth, D, WALL])
    w_out = din_if("4" in phases, "w_out", [depth, D, D])
    w_up = din_if("5" in phases, "w_up", [depth, D, DFF])
    w_down = din_if("5" in phases, "w_down", [depth, DFF, D])
    gpm = din("gpm", [depth, 128, 16])
    gpl = din("gpl", [depth, 128, 16])
    gom = din("gom", [depth, 1, D])
    gol = din("gol", [depth, 1, D])
    convw = din("convw", [depth, 128, 48])
    bpar = din("bpar", [depth, 36, 2])
    dng = din("dng", [depth, 1, 128])
    sinks = din("sinks", [depth, 65, 12])
    cdk = din_if("3a" in phases, "cdk", [depth, NSS, 1024, 1536])
    csw = din("csw", [depth, NSS, 128, 512])
    sds = din("sds", [depth, NSS, 4, 128, 128])
    sdc = din("sdc", [depth, NSS, 3, 1536])
    identf_d = din("identf", [128, 128])
    EA_d = din("EA", [128, 36 * 256], BF16)
    EC_d = din("EC", [128, 12 * 256], BF16)
    ESA_d = din("ESA", [128, 12 * 36], BF16)
    ESC_d = din("ESC", [128, 12 * 8], BF16)
    U64_d = din("U64", [64, 64])
    LS64_d = din("LS64", [64, 64])
    sel_d = din("sel", [65, 64])

    y = dout("y", [NT, D])
    akv = dout("akv", [depth, NT, 1536])
    ckv = dout("ckv", [depth, NT, 512])
    s_out = dout("s_out", [depth, 1 + NSS, 4, 128, 128])
    conv_out = dout("conv_out", [depth, 1 + NSS, 3, 1536])

    dbg = dout if "dbg" in phases else dscr
    xres = dbg("xres", [NT, D], F32)
    zT = dbg("zT", [NFM * 128, NT], BF16)
    mixT = dbg("mixT", [D, NT], BF16)

    P = Prog(nc)
    st = ExitStack()
    ARENA_F32 = 51600
    arena_t = nc.alloc_sbuf_tensor("arena", [128, ARENA_F32], F32)
    A = Arena(arena_t, ARENA_F32 * 4)
    PS = [st.enter_context(nc.psum_tensor("ps%d" % i, [128, 512], F32)) for i in range(8)]

    def psk(i):
        return "ps%d" % i

    uid = [0]

    def U(prefix):
        uid[0] += 1
        return "%s#%d" % (prefix, uid[0])

    def DMA(q, out, in_, r=(), w=(), grp=None):
        P.dma(q, lambda e: e.dma_start(out=out, in_=in_), r, w, grp)

    def MM(out, lhsT, rhs, start=True, stop=True, r=(), w=()):
        P.pe(lambda e: e.matmul(out, lhsT=lhsT, rhs=rhs, start=start, stop=stop), r, w)

    def TR(out, in_, ident, r=(), w=()):
        P.pe(lambda e: e.transpose(out=out, in_=in_, identity=ident), r, w)

    def ACTV(out, in_, func, r=(), w=(), bias=None, scale=None, accum=None):
        kw = {}
        if bias is not None:
            kw["bias"] = bias
        if scale is not None:
            kw["scale"] = scale
        if accum is not None:
            kw["accum_out"] = accum
        P.act(lambda e: e.activation(out=out, in_=in_, func=func, **kw), r, w)

    def TS(eng, out, in0, s1, s2, op0, op1=None, r=(), w=()):
        if op1 is None:
            P.add(eng, lambda e: e.tensor_scalar(out=out, in0=in0, scalar1=s1, scalar2=None, op0=op0), r, w)
        else:
            P.add(eng, lambda e: e.tensor_scalar(out=out, in0=in0, scalar1=s1, scalar2=s2, op0=op0, op1=op1), r, w)

    def TT(eng, out, in0, in1, op, r=(), w=()):
        P.add(eng, lambda e: e.tensor_tensor(out=out, in0=in0, in1=in1, op=op), r, w)

    def STT(eng, out, in0, scalar, in1, op0, op1, r=(), w=()):
        P.add(eng, lambda e: e.scalar_tensor_tensor(out=out, in0=in0, scalar=scalar, in1=in1, op0=op0, op1=op1), r, w)

    def CP(eng, out, in_, r=(), w=()):
        if eng == "act":
            P.act(lambda e: e.copy(out=out, in_=in_), r, w)
        else:
            P.add(eng, lambda e: e.tensor_copy(out=out, in_=in_), r, w)

    def MEMSET(eng, ap, val, w=()):
        P.add(eng, lambda e: e.memset(ap, val), (), w)

    identf = A.tile([128, 128], F32)
    onesf = A.tile([128, 128], F32)
    U64 = A.tile([64, 64], F32)
    LS64 = A.tile([64, 64], F32)
    sel = A.tile([65, 64], F32)
    eps_t = A.tile([128, 1], F32)
    one_t = A.tile([128, 1], F32)
    DMA("sp", identf, identf_d[:, :], w=["identf"], grp="c0")
    DMA("sp", U64, U64_d[:, :], w=["U64"], grp="c1")
    DMA("sp", LS64, LS64_d[:, :], w=["LS64"], grp="c2")
    DMA("sp", sel, sel_d[:, :], w=["sel"], grp="c3")
    MEMSET("pool", onesf, 1.0, w=["onesf"])
    MEMSET("pool", eps_t, EPS, w=["eps"])
    MEMSET("pool", one_t, 1.0, w=["one"])
    P.barrier()
    base_mark = A.mark()

    TOK_TILES = [(i * 128, 128) for i in range(16)] + [(NPR, NS)]
    TOK_GROUPS = [(i * 512, 512) for i in range(4)] + [(NPR, NS)]

    def rstd_from_ss(ss, n, D_, r, w):
        ACTV(ss, ss, AF.Ln, r=r + ["eps"], w=w, bias=eps_t[0:n, 0:1], scale=1.0 / D_)
        ACTV(ss, ss, AF.Exp, r=w, w=w, scale=-0.5)

    def nt_alloc():
        return ([A.tile([128, D], F32) for _ in range(2)], [A.tile([128, D], F32) for _ in range(2)],
                A.tile([128, D], BF16), [A.tile([128, 1], F32) for _ in range(2)])

    def norm_transpose(xsrc, tiles, hT, hkey, gT, gkey, col0, tag, ps_base=0, bufs=None):
        xt, xn, sq, ss = bufs if bufs is not None else nt_alloc()
        for ti, (r0, n) in enumerate(tiles):
            s = ti % 2
            kx, kn, ks = "%s_xt%d" % (tag, s), "%s_xn%d" % (tag, s), "%s_ss%d" % (tag, s)
            DMA("sp", xt[s][0:n, :], xsrc[r0:r0 + n, :], w=[kx], grp=kx)
            ACTV(sq[0:n, :], xt[s][0:n, :], AF.Square, r=[kx], w=[tag + "_sq", ks], accum=ss[s][0:n, 0:1])
            rstd_from_ss(ss[s][0:n, 0:1], n, D, [ks], [ks])
            TS("dve", xn[s][0:n, :], xt[s][0:n, :], ss[s][0:n, 0:1], None, ALU.mult, r=[kx, ks], w=[kn])
            c = col0 + (r0 if r0 < NPR else r0) - tiles[0][0] if False else None
            for qd in range(4):
                b = ps_base + (ti * 4 + qd) % 4
                for kk in range(4):
                    k = qd * 4 + kk
                    TR(PS[b][:, kk * 128: kk * 128 + n], xn[s][0:n, k * 128:(k + 1) * 128], identf[0:n, 0:n],
                       r=[kn, "identf"], w=[psk(b)])
                cc = col0 + ti * 128 if tiles[0][1] == 128 else col0
                cc = col0 + (r0 - tiles[0][0])
                TT("dve", hT[:, qd * 4:qd * 4 + 4, cc:cc + n],
                   PS[b][:, :].rearrange("p (a b) -> p a b", a=4)[:, :, 0:n],
                   gT[:, qd * 4:qd * 4 + 4].unsqueeze(2).to_broadcast([128, 4, n]), ALU.mult,
                   r=[psk(b), gkey], w=[hkey + "_%d" % (cc // 512)])

    def hkeys(hkey, c0, n):
        return [hkey + "_%d" % g for g in range(c0 // 512, (c0 + n - 1) // 512 + 1)]

    for l in range(depth):
        xsrc = xin if l == 0 else xres
        last = (l == depth - 1)
        xdst = y if last else xres

        A.reset(base_mark)
        P.enabled = "12" in phases
        hT = A.tile([128, 16, NT], BF16)
        gT = A.tile([128, 16], F32)
        DMA("sp", gT, gpm[l], w=["gT"], grp="gT")
        m1 = A.mark()
        norm_transpose(xsrc, TOK_TILES, hT, "hT", gT, "gT", 0, "p1")
        allh = hkeys("hT", 0, NT)
        wt = [A.tile([128, 16, 128], BF16) for _ in range(3)]
        zt = [A.tile([128, NT], BF16) for _ in range(2)]
        ev = 0
        for j in range(NFM):
            s = j % 3
            kw_ = "wfm%d" % s
            DMA("pool", wt[s], w_fm[l][:, j * 128:(j + 1) * 128].rearrange("(k p) n -> p k n", p=128), w=[kw_], grp=kw_)
            zs = j % 2
            kz = "zt%d" % zs
            for gi, (t0, n) in enumerate(TOK_GROUPS):
                b = (j * 5 + gi) % 4
                for k in range(16):
                    MM(PS[b][:, 0:n], wt[s][:, k, :], hT[:, k, t0:t0 + n], start=(k == 0), stop=(k == 15),
                       r=[kw_] + hkeys("hT", t0, n), w=[psk(b)])
                CP("act" if ev % 2 == 0 else "dve", zt[zs][:, t0:t0 + n], PS[b][:, 0:n], r=[psk(b)], w=[kz])
                ev += 1
            DMA("sp", zT[j * 128:(j + 1) * 128, :], zt[zs], r=[kz], w=["zT%d" % j], grp=kz + "o")
        wtm = [A.tile([128, 16, 512], BF16) for _ in range(2)]
        ot = [A.tile([128, 512], F32) for _ in range(3)]
        oi = 0
        for cg in range(7):
            s = cg % 2
            kw_ = "wtm%d" % s
            o_ = 0
            for (c_, n_) in TM_SEGS[cg]:
                DMA("pool", wtm[s][:, :, o_:o_ + n_], w_fm[l][:, c_:c_ + n_].rearrange("(k p) n -> p k n", p=128), w=[kw_], grp=kw_)
                o_ += n_
            if cg < 4:
                tl = TOK_TILES
            else:
                tl = [(NPR - 3, 3), (NPR, NS)]
            for (r0, n) in tl:
                b = 4 + oi % 4
                os_ = oi % 3
                ko = "ot%d" % os_
                for k in range(16):
                    MM(PS[b][0:n, :], hT[:, k, r0:r0 + n], wtm[s][:, k, :], start=(k == 0), stop=(k == 15),
                       r=[kw_] + hkeys("hT", r0, n), w=[psk(b)])
                CP("act" if oi % 2 == 0 else "dve", ot[os_][0:n, :], PS[b][0:n, :], r=[psk(b)], w=[ko])
                if cg < 3:
                    DMA("sp", akv[l][r0:r0 + n, cg * 512:(cg + 1) * 512], ot[os_][0:n, :], r=[ko], w=[U("akv")], grp=ko + "o")
                elif cg == 3:
                    DMA("sp", ckv[l][r0:r0 + n, :], ot[os_][0:n, :], r=[ko], w=[U("ckv")], grp=ko + "o")
                else:
                    c0 = (cg - 4) * 512
                    if r0 < NPR:
                        DMA("sp", conv_out[l, 0, :, c0:c0 + 512], ot[os_][0:3, :], r=[ko], w=["convo"], grp=ko + "o")
                    else:
                        for sq_ in range(NSS):
                            DMA("sp", conv_out[l, 1 + sq_, :, c0:c0 + 512], ot[os_][4 * sq_ + 1:4 * sq_ + 4, :], r=[ko],
                                w=["convo"], grp=ko + "o")
                oi += 1
        P.barrier()

        A.reset(base_mark)
        P.enabled = "3a" in phases
        EA = A.tile([128, 36, 256], BF16)
        EC = A.tile([128, 12, 256], BF16)
        ESA = A.tile([128, 12, 36], BF16)
        ESC = A.tile([128, 12, 8], BF16)
        esink = A.tile([65, 12], F32)
        DMA("sp", EA, EA_d[:, :].rearrange("p (a b) -> p a b", a=36), w=["EA"], grp="EA")
        DMA("sp", EC, EC_d[:, :].rearrange("p (a b) -> p a b", a=12), w=["EC"], grp="EC")
        DMA("sp", ESA, ESA_d[:, :].rearrange("p (a b) -> p a b", a=12), w=["ESA"], grp="ESA")
        DMA("sp", ESC, ESC_d[:, :].rearrange("p (a b) -> p a b", a=12), w=["ESC"], grp="ESC")
        DMA("sp", esink[64:65, :], sinks[l][64:65, :], w=["esink"], grp="esink")
        ACTV(esink[64:65, :], esink[64:65, :], AF.Exp, r=["esink"], w=["esink"])
        qT = [A.tile([128, NT], BF16) for _ in range(2)]
        kT = [A.tile([128, NT], BF16) for _ in range(2)]
        acc = [A.tile([65, NT], F32) for _ in range(2)]
        Vt = [A.tile([128, 16, 65], BF16) for _ in range(2)]
        Pt = [A.tile([128, 256], BF16) for _ in range(3)]
        mo = [A.tile([64, NT], BF16) for _ in range(2)]
        NPB_A, NPB_C = 8, 1
        kc = [A.tile([128, 768], F32) for _ in range(2)]
        kcT = A.tile([128, 6, NPB_A * 128], BF16)
        vc = A.tile([128, NPB_A, 12, 65], BF16)
        vnew = A.tile([4, 12, 65], BF16)
        Ps = [A.tile([128, 36], BF16) for _ in range(2)]
        for s in range(2):
            MEMSET("pool", Vt[s][:, :, 64:65], 1.0, w=["Vt%d" % s])
        MEMSET("pool", vc[:, :, :, 64:65], 1.0, w=["vc"])
        MEMSET("pool", vnew[:, :, 64:65], 1.0, w=["vnew"])
        kbctr = [0]
        hctr = [0]

        vsrc_key = {}
        akv_l = akv[l]
        ckv_l = ckv[l]
        vsrc_key[id(akv_l)] = "akv"
        vsrc_key[id(ckv_l)] = "ckv"

        def load_sample_cache_A(sq_):
            for blk in range(NPB_A):
                if blk < 4:
                    rows = cdk[l, sq_, 512 + 128 * blk:512 + 128 * blk + 128, :]
                else:
                    rows = cdk[l, sq_, (blk - 4) * 128:(blk - 4) * 128 + 128, :]
                s = blk % 2
                kkc = "kc%d" % s
                DMA("sp", kc[s], rows[:, 0:768], w=[kkc], grp=kkc)
                DMA("pool", vc[:, blk, :, 0:64], rows[:, 768:1536].rearrange("p (h e) -> p h e", h=12), w=["cAv"], grp="vc")
                for pr in range(6):
                    b = 4 + (blk * 6 + pr) % 4
                    TR(PS[b][:, 0:128], kc[s][:, pr * 128:(pr + 1) * 128], identf, r=[kkc, "identf"], w=[psk(b)])
                    CP("act" if pr % 2 == 0 else "dve", kcT[:, pr, blk * 128:(blk + 1) * 128], PS[b][:, 0:128], r=[psk(b)], w=["cAk"])
            DMA("pool", vnew[:, :, 0:64], akv_l[NPR + 4 * sq_:NPR + 4 * sq_ + 4, 768:1536].rearrange("p (h e) -> p h e", h=12),
                r=["akv"], w=["cAn"], grp="vnew")

        accS = A.tile([65, 24, NS], F32)

        def sample_attn(sq_, hslot, qap, knew, nblk, kblk_fn, vblk_fn, vnew_ap, E_, rkeys):
            i = kbctr[0]
            kbctr[0] += 1
            bS, bO, pslot = 4 + i % 2, 6 + i % 2, i % 2
            kp = "Ps%d" % pslot
            for blk in range(nblk):
                MM(PS[bS][:, blk * 4:blk * 4 + 4], kblk_fn(blk), qap, r=rkeys, w=[psk(bS)])
            MM(PS[bS][0:4, nblk * 4:nblk * 4 + 4], knew, qap, r=rkeys, w=[psk(bS)])
            ACTV(Ps[pslot][:, 0:nblk * 4], PS[bS][:, 0:nblk * 4], AF.Exp, r=[psk(bS)], w=[kp], scale=ATT_SCALE)
            ACTV(Ps[pslot][0:4, nblk * 4:nblk * 4 + 4], PS[bS][0:4, nblk * 4:nblk * 4 + 4], AF.Exp, r=[psk(bS)], w=[kp], scale=ATT_SCALE)
            TT("dve", Ps[pslot][:, 0:nblk * 4], Ps[pslot][:, 0:nblk * 4], E_[:, 0:nblk * 4], ALU.mult, r=[kp, "ESA", "ESC"], w=[kp])
            TT("dve", Ps[pslot][0:4, nblk * 4:nblk * 4 + 4], Ps[pslot][0:4, nblk * 4:nblk * 4 + 4], E_[0:4, nblk * 4:nblk * 4 + 4],
               ALU.mult, r=[kp, "ESA", "ESC"], w=[kp])
            for blk in range(nblk):
                MM(PS[bO][0:65, 0:4], vblk_fn(blk), Ps[pslot][:, blk * 4:blk * 4 + 4], start=(blk == 0), stop=False,
                   r=[kp] + rkeys, w=[psk(bO)])
            MM(PS[bO][0:65, 0:4], vnew_ap, Ps[pslot][0:4, nblk * 4:nblk * 4 + 4], start=False, stop=True, r=[kp] + rkeys, w=[psk(bO)])
            CP("act", accS[:, hslot, 4 * sq_:4 * sq_ + 4], PS[bO][0:65, 0:4], r=[psk(bO)], w=["accS"])

        qsA = A.tile([128, 6, NS], BF16)
        ksA = A.tile([128, 6, NS], BF16)
        qsC = A.tile([128, 6, NS], BF16)
        ksC = A.tile([128, 2, NS], BF16)
        DMA("sp", qsA, zT[AQ * 128:(AQ + 6) * 128, NPR:NT].rearrange("(c p) t -> p c t", p=128), r=["zT%d" % j for j in range(AQ, AQ + 6)], w=["qsA"], grp="qsA")
        DMA("sp", ksA, zT[AK * 128:(AK + 6) * 128, NPR:NT].rearrange("(c p) t -> p c t", p=128), r=["zT%d" % j for j in range(AK, AK + 6)], w=["ksA"], grp="ksA")
        DMA("sp", qsC, zT[CQ * 128:(CQ + 6) * 128, NPR:NT].rearrange("(c p) t -> p c t", p=128), r=["zT%d" % j for j in range(CQ, CQ + 6)], w=["qsC"], grp="qsC")
        DMA("sp", ksC, zT[CK * 128:(CK + 2) * 128, NPR:NT].rearrange("(c p) t -> p c t", p=128), r=["zT%d" % j for j in range(CK, CK + 2)], w=["ksC"], grp="ksC")
        kcc = A.tile([128, 256], F32)
        kccT = A.tile([128, 2, 128], BF16)
        vcc = A.tile([128, 4, 65], BF16)
        vnewc = A.tile([4, 4, 65], BF16)
        MEMSET("pool", vcc[:, :, 64:65], 1.0, w=["cCv"])
        MEMSET("pool", vnewc[:, :, 64:65], 1.0, w=["cCn"])
        for sq_ in range(NSS):
            load_sample_cache_A(sq_)
            rk = ["cAk", "cAv", "cAn", "qsA", "ksA"]
            for h in range(12):
                pr, pb = h // 2, (h % 2) * 64
                sample_attn(sq_, h, qsA[pb:pb + 64, pr, 4 * sq_:4 * sq_ + 4], ksA[pb:pb + 64, pr, 4 * sq_:4 * sq_ + 4], NPB_A,
                            lambda blk, pr=pr, pb=pb: kcT[pb:pb + 64, pr, blk * 128:(blk + 1) * 128],
                            lambda blk, h=h: vc[:, blk, h, :], vnew[:, h, :], ESA[:, h, :], rk)
            DMA("sp", kcc, csw[l, sq_, :, 0:256], w=["kcc"], grp="kcc")
            DMA("pool", vcc[:, :, 0:64], csw[l, sq_, :, 256:512].rearrange("p (h e) -> p h e", h=4), w=["cCv"], grp="vcc")
            DMA("pool", vnewc[:, :, 0:64], ckv_l[NPR + 4 * sq_:NPR + 4 * sq_ + 4, 256:512].rearrange("p (h e) -> p h e", h=4),
                r=["ckv"], w=["cCn"], grp="vnewc")
            for pr in range(2):
                b = 4 + pr
                TR(PS[b][:, 0:128], kcc[:, pr * 128:(pr + 1) * 128], identf, r=["kcc", "identf"], w=[psk(b)])
                CP("act", kccT[:, pr, :], PS[b][:, 0:128], r=[psk(b)], w=["cC"])
            rk = ["cC", "cCv", "cCn", "qsC", "ksC"]
            for pos in range(12):
                n_kv = CPERM[pos] // 3
                pr, pb = pos // 2, (pos % 2) * 64
                sample_attn(sq_, 12 + pos, qsC[pb:pb + 64, pr, 4 * sq_:4 * sq_ + 4], ksC[pb:pb + 64, n_kv // 2, 4 * sq_:4 * sq_ + 4], NPB_C,
                            lambda blk, n_kv=n_kv, pb=pb: kccT[pb:pb + 64, n_kv // 2, :],
                            lambda blk, n_kv=n_kv: vcc[:, n_kv, :], vnewc[:, n_kv, :], ESC[:, pos, :], rk)

        def attn_prompt_head(qchunk, kchunk, half, branches, Eap_fn, vsrc, vkey, vcol, mix_row, sink_col, hslot, load_q, load_k, ps_q, ps_k):
            hs = hctr[0] % 2
            hctr[0] += 1
            kq, kk_, ka = "qT%d" % ps_q, "kT%d" % ps_k, "acc%d" % hs
            if load_q:
                DMA("sp", qT[ps_q], zT[qchunk * 128:(qchunk + 1) * 128, :], r=["zT%d" % qchunk], w=[kq], grp=kq)
            if load_k:
                DMA("sp", kT[ps_k], zT[kchunk * 128:(kchunk + 1) * 128, :], r=["zT%d" % kchunk], w=[kk_], grp=kk_)
            pb = half * 64
            MEMSET("pool", acc[hs][:, 0:NPR], 0.0, w=[ka])
            CP("pool", acc[hs][:, NPR:NT], accS[:, hslot, :], r=["accS"], w=[ka])
            pend = None
            for bi, dil in enumerate(branches):
                nblk = (NPR // dil) // 128
                vs = kbctr[0] % 2
                kbctr[0] += 1
                kv_ = "Vt%d" % vs
                vv = vsrc[0:NPR, vcol:vcol + 64].rearrange("(b p r) e -> r p b e", p=128, r=dil)
                for r_ in range(dil):
                    DMA("pool", Vt[vs][:, r_ * nblk:(r_ + 1) * nblk, 0:64], vv[r_], r=[vkey], w=[kv_ + "_%d" % r_] + ([kv_ + "_%d" % x for x in range(1, 16)] if r_ == 0 else []), grp=kv_)
                vkeys = [kv_ + "_%d" % r_ for r_ in range(dil)] + [kv_]
                E_ = Eap_fn(bi)
                for r_ in range(dil):
                    for blk in range(nblk):
                        nq = 256 if blk < nblk - 1 else 128
                        t0 = r_ + dil * 128 * blk
                        kap = kT[ps_k][pb:pb + 64, t0:t0 + dil * 127 + 1:dil]
                        qap = qT[ps_q][pb:pb + 64, t0:t0 + dil * (nq - 1) + 1:dil]
                        i = kbctr[0]
                        kbctr[0] += 1
                        bS, bO, pslot = i % 2, 2 + i % 2, i % 3
                        MM(PS[bS][:, 0:nq], kap, qap, r=[kq, kk_], w=[psk(bS)])
                        if pend is not None:
                            pend()
                        kp = "Pt%d" % pslot
                        ACTV(Pt[pslot][:, 0:nq], PS[bS][:, 0:nq], AF.Exp, r=[psk(bS)], w=[kp], scale=ATT_SCALE)
                        TT("dve", Pt[pslot][:, 0:nq], Pt[pslot][:, 0:nq], E_[:, 0:nq], ALU.mult, r=[kp, "EA", "EC"], w=[kp])

                        def fin(bO=bO, pslot=pslot, kp=kp, vs=vs, vkeys=vkeys, idx=r_ * nblk + blk, nq=nq, t0=t0, dil=dil):
                            MM(PS[bO][0:65, 0:nq], Vt[vs][:, idx, :], Pt[pslot][:, 0:nq], r=[kp] + vkeys, w=[psk(bO)])
                            av = acc[hs][:, t0:t0 + dil * (nq - 1) + 1:dil]
                            TT("dve", av, PS[bO][0:65, 0:nq], av, ALU.add, r=[psk(bO), ka], w=[ka])
                        pend = fin
            if pend is not None:
                pend()
            if sink_col is not None:
                TS("dve", acc[hs][64:65, :], acc[hs][64:65, :], esink[64:65, sink_col:sink_col + 1], None, ALU.add,
                   r=[ka, "esink"], w=[ka])
            P.dve(lambda e: e.reciprocal(out=acc[hs][64:65, :], in_=acc[hs][64:65, :]), [ka], [ka])
            km = "mo%d" % hs
            for gi, (t0, n) in enumerate(TOK_GROUPS):
                b = 4 + gi % 4
                MM(PS[b][0:64, 0:n], sel, acc[hs][:, t0:t0 + n], r=[ka, "sel"], w=[psk(b)])
                TT("dve", mo[hs][:, t0:t0 + n], acc[hs][0:64, t0:t0 + n], PS[b][0:64, 0:n], ALU.mult, r=[psk(b), ka], w=[km])
            DMA("sp", mixT[mix_row:mix_row + 64, :], mo[hs], r=[km], w=[U("mixT")], grp=km + "o")

        for h in range(12):
            pr, half = h // 2, h % 2
            attn_prompt_head(AQ + pr, AK + pr, half, [1, 4, 16], lambda bi, h=h: EA[:, h * 3 + bi, :], akv_l, "akv", 768 + h * 64,
                             h * 64, None, h, half == 0, half == 0, pr % 2, pr % 2)
        for pos in range(12):
            pr, half = pos // 2, pos % 2
            n_kv = CPERM[pos] // 3
            attn_prompt_head(CQ + pr, CK + n_kv // 2, half, [1], lambda bi, pos=pos: EC[:, pos, :], ckv_l, "ckv", 256 + n_kv * 64,
                             1280 + pos * 64, pos, 12 + pos, half == 0, pos % 6 == 0, pr % 2, (pos // 6) % 2)
        P.barrier()

        A.reset(base_mark)
        P.enabled = "3b" in phases
        QN = A.tile([128, 4, NPR], F32)
        KN = A.tile([128, 4, NPR], F32)
        VN = A.tile([128, 4, NPR], F32)
        ZS = A.tile([128, 4, NPR], BF16)
        X = A.tile([128, 3 + NPR], F32)
        SQ = A.tile([128, NPR], F32)
        MT = A.tile([36, NPR], F32)
        cw = A.tile([128, 48], F32)
        bp = A.tile([36, 2], F32)
        dgb = A.tile([64, 128], F32)
        Sst = A.tile([128, 4, 128], F32)
        tmpn = A.tile([128, 512], F32)
        DMA("sp", cw, convw[l], w=["cw"], grp="cw")
        DMA("sp", bp[32:36, :], bpar[l][32:36, :], w=["bp"], grp="bp")
        DMA("sp", dgb, dng[l][0:1, :].partition_broadcast(64), w=["dgb"], grp="dgb")
        ACTV(bp[32:36, 0:1], bp[32:36, 0:1], AF.Exp, r=["bp"], w=["bp"])
        TS("dve", bp[32:36, 0:1], bp[32:36, 0:1], -1.0, None, ALU.mult, r=["bp"], w=["bp"])
        U8 = 8
        bg = A.tile([64, 2, 36], F32)
        gcl = A.tile([64, 2, 8], F32)
        egc = A.tile([64, 2, 4], F32)
        ekd = A.tile([64, 2, 4], F32)
        nbeta = A.tile([64, 2, 4], F32)
        cbe = A.tile([64, 2, 4], F32)
        gcr = A.tile([128, U8, 64], F32)
        egr = A.tile([128, U8, 64], F32)
        tU = A.tile([64, U8, 64], F32)
        gU = A.tile([64, U8, 64], F32)
        tL = A.tile([64, U8, 64], F32)
        Nm = [A.tile([64, U8, 64], F32) for _ in range(2)]
        Pm = [A.tile([64, U8, 64], F32) for _ in range(2)]
        TTm = A.tile([64, U8, 64], F32)
        attnT = A.tile([64, U8, 64], F32)
        kbe = A.tile([64, U8, 128], F32)
        kd = A.tile([64, U8, 128], F32)
        vb = A.tile([64, U8, 128], F32)
        nwT = A.tile([128, U8, 64], F32)
        qdT = A.tile([128, U8, 64], F32)
        vn = A.tile([64, 4, 128], F32)
        osq = A.tile([64, 4, 128], F32)
        on = A.tile([64, 4, 128], F32)
        oss = A.tile([64, 4], F32)

        def delta_seq(L, Lpad, col0, conv_src, s0_src, oseq):
            nchunk = Lpad // 64
            sk = U("seq")
            en_seq = ("3b" in phases) and (conv_src is None or "3bs" in phases)
            P.enabled = en_seq
            if L < Lpad:
                for t_, nm in ((QN, "QN"), (KN, "KN"), (VN, "VN")):
                    MEMSET("pool", t_[:, :, 0:Lpad], 0.0, w=[nm])
                MEMSET("pool", MT[:, 0:Lpad], 0.0, w=["MT"])
            for kind, (tile_, nm, chunk0) in enumerate(((QN, "QN", BQ), (KN, "KN", BK), (VN, "VN", BV))):
                for hb in range(4):
                    ch = chunk0 + hb
                    wc = (kind * 4 + hb) * 4
                    if conv_src is None:
                        MEMSET("pool", X[:, 0:3], 0.0, w=["X"])
                    else:
                        P.dma("pool", lambda e, hb=hb, kind=kind: e.dma_start(
                            out=X[:, 0:3], in_=conv_src[:, (kind * 4 + hb) * 128:(kind * 4 + hb + 1) * 128].rearrange("t c -> c t"),
                            allow_slow_non_contiguous=True), [], ["X"], "Xc")
                    DMA("pool", X[:, 3:3 + L], zT[ch * 128:(ch + 1) * 128, col0:col0 + L], r=["zT%d" % ch], w=["X"], grp="X")
                    tg = tile_[:, hb, 0:L]
                    e0 = "dve"
                    TS(e0, tg, X[:, 0:L], cw[:, wc:wc + 1], None, ALU.mult, r=["X", "cw"], w=[nm])
                    for j in range(1, 4):
                        STT(e0, tg, X[:, j:j + L], cw[:, wc + j:wc + j + 1], tg, ALU.mult, ALU.add, r=["X", "cw", nm], w=[nm])
                    ACTV(tg, tg, AF.Silu, r=[nm], w=[nm])
                    if kind < 2:
                        TT("pool", SQ[:, 0:L], tg, tg, ALU.mult, r=[nm], w=["SQ"])
                        for g0 in range(0, L, 512):
                            n = min(512, L - g0)
                            b = (g0 // 512) % 4
                            MM(PS[b][:, 0:n], onesf, SQ[:, g0:g0 + n], r=["SQ", "onesf"], w=[psk(b)])
                            ACTV(tmpn[:, 0:n], PS[b][:, 0:n], AF.Ln, r=[psk(b), "eps"], w=["tmpn"], bias=eps_t[:, 0:1])
                            ACTV(tmpn[:, 0:n], tmpn[:, 0:n], AF.Exp, r=["tmpn"], w=["tmpn"], scale=-0.5)
                            if kind == 0:
                                STT("dve", tile_[:, hb, g0:g0 + n], tile_[:, hb, g0:g0 + n], float(128 ** -0.5), tmpn[:, 0:n], ALU.mult, ALU.mult,
                                    r=[nm, "tmpn"], w=[nm])
                            else:
                                TT("dve", tile_[:, hb, g0:g0 + n], tile_[:, hb, g0:g0 + n], tmpn[:, 0:n], ALU.mult, r=[nm, "tmpn"], w=[nm])
            for hb in range(4):
                ch = BZ + hb
                DMA("sp", ZS[:, hb, 0:L], zT[ch * 128:(ch + 1) * 128, col0:col0 + L], r=["zT%d" % ch], w=["ZS"], grp="ZS")
            ACTV(ZS[:, :, 0:L], ZS[:, :, 0:L], AF.Silu, r=["ZS"], w=["ZS"])
            DMA("pool", MT[0:36, 0:L], zT[BM * 128:BM * 128 + 36, col0:col0 + L], r=["zT%d" % BM], w=["MT"], grp="MT")
            ACTV(MT[0:4, 0:L], MT[0:4, 0:L], AF.Sigmoid, r=["MT"], w=["MT"])
            ACTV(MT[32:36, 0:L], MT[32:36, 0:L], AF.Exp, r=["MT", "bp"], w=["MT"], bias=bp[32:36, 1:2])
            ACTV(MT[32:36, 0:L], MT[32:36, 0:L], AF.Ln, r=["MT", "one"], w=["MT"], bias=one_t[32:36, 0:1])
            TS("dve", MT[32:36, 0:L], MT[32:36, 0:L], bp[32:36, 0:1], None, ALU.mult, r=["MT", "bp"], w=["MT"])
            if s0_src is None:
                MEMSET("pool", Sst, 0.0, w=["S"])
            else:
                DMA("sp", Sst, s0_src.rearrange("h k v -> k h v"), w=["S"], grp="S0")
            b2stop = max([int(p[1:]) for p in phases if p.startswith("s") and p[1:].isdigit()] + [0]) or 99

            def G(n):
                P.enabled = en_seq and ("3b2" in phases) and n <= b2stop
            G(0)
            for c0 in range(0, nchunk, 2):
                ncb = min(2, nchunk - c0)
                nu = ncb * 4
                cols = [slice(64 * (c0 + ci), 64 * (c0 + ci) + 64) for ci in range(ncb)]
                G(1)
                for ci in range(ncb):
                    TR(PS[0][0:64, ci * 36:(ci + 1) * 36], MT[0:36, cols[ci]], identf[0:36, 0:36], r=["MT", "identf"], w=[psk(0)])
                CP("dve", bg[:, 0:ncb, :], PS[0][0:64, 0:ncb * 36].rearrange("p (a b) -> p a b", a=ncb), r=[psk(0)], w=["bg"])
                for ci in range(ncb):
                    MM(PS[1][0:64, ci * 8:ci * 8 + 4], U64, bg[:, ci, 32:36], r=["bg", "U64"], w=[psk(1)])
                    MM(PS[1][0:64, ci * 8 + 4:ci * 8 + 8], onesf[0:64, 0:64], bg[:, ci, 32:36], r=["bg", "onesf"], w=[psk(1)])
                CP("dve", gcl[:, 0:ncb, :], PS[1][0:64, 0:ncb * 8].rearrange("p (a b) -> p a b", a=ncb), r=[psk(1)], w=["gcl"])
                ACTV(egc[:, 0:ncb, :], gcl[:, 0:ncb, 0:4], AF.Exp, r=["gcl"], w=["egc"])
                TT("dve", ekd[:, 0:ncb, :], gcl[:, 0:ncb, 4:8], gcl[:, 0:ncb, 0:4], ALU.subtract, r=["gcl"], w=["ekd"])
                ACTV(ekd[:, 0:ncb, :], ekd[:, 0:ncb, :], AF.Exp, r=["ekd"], w=["ekd"])
                TS("dve", nbeta[:, 0:ncb, :], bg[:, 0:ncb, 0:4], -1.0, None, ALU.mult, r=["bg"], w=["nbeta"])
                TT("dve", cbe[:, 0:ncb, :], bg[:, 0:ncb, 0:4], egc[:, 0:ncb, :], ALU.mult, r=["bg", "egc"], w=["cbe"])
                G(3)
                for ci in range(ncb):
                    for hb in range(4):
                        u = ci * 4 + hb
                        TS("dve", gU[:, u, :], U64, bg[:, ci, 32 + hb:33 + hb], None, ALU.mult, r=["bg", "U64"], w=["gU"])
                        MM(PS[2][:, u * 64:(u + 1) * 64], onesf[0:64, :], gU[:, u, :], r=["gU", "onesf"], w=[psk(2)])
                psv2 = PS[2][:, 0:nu * 64].rearrange("p (a b) -> p a b", a=nu)
                CP("dve", gcr[:, 0:nu, :], psv2, r=[psk(2)], w=["gcr"])
                ACTV(egr[:, 0:nu, :], gcr[:, 0:nu, :], AF.Exp, r=["gcr"], w=["egr"])
                G(4)
                for ci in range(ncb):
                    for hb in range(4):
                        u = ci * 4 + hb
                        TS("dve", tU[:, u, :], gcr[0:64, u, :], gcl[:, ci, hb:hb + 1], 0.0, ALU.subtract, ALU.min, r=["gcr", "gcl"], w=["tU"])
                        TS("dve", tL[:, u, :], gcr[0:64, u, :], gcl[:, ci, hb:hb + 1], 0.0, ALU.subtract, ALU.max, r=["gcr", "gcl"], w=["tL"])
                ACTV(tU[:, 0:nu, :], tU[:, 0:nu, :], AF.Exp, r=["tU"], w=["tU"])
                ACTV(tL[:, 0:nu, :], tL[:, 0:nu, :], AF.Exp, r=["tL"], w=["tL"], scale=-1.0)
                TT("dve", tU[:, 0:nu, :], tU[:, 0:nu, :], U64.unsqueeze(1).to_broadcast([64, nu, 64]), ALU.mult, r=["tU", "U64"], w=["tU"])
                TT("pool", tL[:, 0:nu, :], tL[:, 0:nu, :], LS64.unsqueeze(1).to_broadcast([64, nu, 64]), ALU.mult, r=["tL", "LS64"], w=["tL"])
                G(5)
                for ci in range(ncb):
                    for hb in range(4):
                        u = ci * 4 + hb
                        MM(PS[3][0:64, u * 64:(u + 1) * 64], KN[:, hb, cols[ci]], KN[:, hb, cols[ci]], r=["KN"], w=[psk(3)])
                        MM(PS[4][0:64, u * 64:(u + 1) * 64], KN[:, hb, cols[ci]], QN[:, hb, cols[ci]], r=["KN", "QN"], w=[psk(4)])
                for ci in range(ncb):
                    for hb in range(4):
                        u = ci * 4 + hb
                        STT("dve", Nm[0][:, u, :], PS[3][0:64, u * 64:(u + 1) * 64], nbeta[:, ci, hb:hb + 1], tL[:, u, :], ALU.mult, ALU.mult,
                            r=[psk(3), "nbeta", "tL"], w=["N0"])
                TT("dve", attnT[:, 0:nu, :], PS[4][0:64, 0:nu * 64].rearrange("p (a b) -> p a b", a=nu), tU[:, 0:nu, :], ALU.mult,
                   r=[psk(4), "tU"], w=["attnT"])
                G(6)
                for u in range(nu):
                    TR(PS[5][0:64, u * 64:(u + 1) * 64], Nm[0][:, u, :], identf[0:64, 0:64], r=["N0", "identf"], w=[psk(5)])
                psv5 = PS[5][0:64, 0:nu * 64].rearrange("p (a b) -> p a b", a=nu)
                CP("act", Pm[0][:, 0:nu, :], psv5, r=[psk(5)], w=["P0"])
                TT("dve", TTm[:, 0:nu, :], Pm[0][:, 0:nu, :], identf[0:64, 0:64].unsqueeze(1).to_broadcast([64, nu, 64]), ALU.add, r=["P0", "identf"], w=["TT"])
                G(7)
                cur = 0
                for step in range(1, 6):
                    nx = 1 - cur
                    for u in range(nu):
                        MM(PS[6][0:64, u * 64:(u + 1) * 64], Pm[cur][:, u, :], Nm[cur][:, u, :], r=["P%d" % cur, "N%d" % cur], w=[psk(6)])
                    CP("act", Nm[nx][:, 0:nu, :], PS[6][0:64, 0:nu * 64].rearrange("p (a b) -> p a b", a=nu), r=[psk(6)], w=["N%d" % nx])
                    if step < 5:
                        for u in range(nu):
                            MM(PS[7][0:64, u * 64:(u + 1) * 64], Nm[cur][:, u, :], Pm[cur][:, u, :], r=["P%d" % cur, "N%d" % cur], w=[psk(7)])
                        CP("dve", Pm[nx][:, 0:nu, :], PS[7][0:64, 0:nu * 64].rearrange("p (a b) -> p a b", a=nu), r=[psk(7)], w=["P%d" % nx])
                    for u in range(nu):
                        MM(PS[5][0:64, u * 64:(u + 1) * 64], Nm[nx][:, u, :], TTm[:, u, :], r=["N%d" % nx, "TT"], w=[psk(5)])
                    TT("dve", TTm[:, 0:nu, :], TTm[:, 0:nu, :], PS[5][0:64, 0:nu * 64].rearrange("p (a b) -> p a b", a=nu), ALU.add,
                       r=[psk(5), "TT"], w=["TT"])
                    cur = nx
                G(8)
                for which, (src, nm) in enumerate(((KN, "KN"), (VN, "VN"))):
                    for ci in range(ncb):
                        for hb in range(4):
                            u = ci * 4 + hb
                            b = 0 + (u // 4) + 2 * which
                            TR(PS[b][0:64, (u % 4) * 128:(u % 4 + 1) * 128], src[:, hb, cols[ci]], identf, r=[nm, "identf"], w=[psk(b)])
                    for ci in range(ncb):
                        for hb in range(4):
                            u = ci * 4 + hb
                            b = 0 + (u // 4) + 2 * which
                            pv = PS[b][0:64, (u % 4) * 128:(u % 4 + 1) * 128]
                            if which == 0:
                                ACTV(kbe[:, u, :], pv, AF.Copy, r=[psk(b), "cbe"], w=["kbe"], scale=cbe[:, ci, hb:hb + 1])
                                ACTV(kd[:, u, :], pv, AF.Copy, r=[psk(b), "ekd"], w=["kd"], scale=ekd[:, ci, hb:hb + 1])
                            else:
                                ACTV(vb[:, u, :], pv, AF.Copy, r=[psk(b), "bg"], w=["vb"], scale=bg[:, ci, hb:hb + 1])
                G(9)
                for u in range(nu):
                    MM(PS[6][:, u * 64:(u + 1) * 64], kbe[:, u, :], TTm[:, u, :], r=["kbe", "TT"], w=[psk(6)])
                TS("dve", nwT[:, 0:nu, :], PS[6][:, 0:nu * 64].rearrange("p (a b) -> p a b", a=nu), -1.0, None, ALU.mult, r=[psk(6)], w=["nwT"])
                for ci in range(ncb):
                    TT("pool", qdT[:, ci * 4:ci * 4 + 4, :], QN[:, :, cols[ci]], egr[:, ci * 4:ci * 4 + 4, :], ALU.mult, r=["QN", "egr"], w=["qdT"])
                G(10)
                for ci in range(ncb):
                    for hb in range(4):
                        u = ci * 4 + hb
                        MM(PS[7][0:64, hb * 128:(hb + 1) * 128], TTm[:, u, :], vb[:, u, :], start=True, stop=False, r=["TT", "vb"], w=[psk(7)])
                        MM(PS[7][0:64, hb * 128:(hb + 1) * 128], nwT[:, u, :], Sst[:, hb, :], start=False, stop=True, r=["nwT", "S"], w=[psk(7)])
                    CP("act", vn, PS[7][0:64, :].rearrange("p (a b) -> p a b", a=4), r=[psk(7)], w=["vn"])
                    for hb in range(4):
                        u = ci * 4 + hb
                        MM(PS[4][0:64, hb * 128:(hb + 1) * 128], qdT[:, u, :], Sst[:, hb, :], start=True, stop=False, r=["qdT", "S"], w=[psk(4)])
                        MM(PS[4][0:64, hb * 128:(hb + 1) * 128], attnT[:, u, :], vn[:, hb, :], start=False, stop=True, r=["attnT", "vn"], w=[psk(4)])
                    for hb in range(4):
                        u = ci * 4 + hb
                        MM(PS[3][:, hb * 128:(hb + 1) * 128], kd[:, u, :], vn[:, hb, :], r=["kd", "vn"], w=[psk(3)])
                    TT("dve", Sst, Sst, egr[:, ci * 4:ci * 4 + 4, 63:64].to_broadcast([128, 4, 128]), ALU.mult, r=["S", "egr"], w=["S"])
                    TT("dve", Sst, Sst, PS[3][:, :].rearrange("p (a b) -> p a b", a=4), ALU.add, r=["S", psk(3)], w=["S"])
                    psv4 = PS[4][0:64, :].rearrange("p (a b) -> p a b", a=4)
                    ACTV(osq, psv4, AF.Square, r=[psk(4)], w=["osq"])
                    P.dve(lambda e: e.tensor_reduce(out=oss, in_=osq, axis=AX.X, op=ALU.add), ["osq"], ["oss"])
                    ACTV(oss, oss, AF.Ln, r=["oss", "eps"], w=["oss"], bias=eps_t[0:64, 0:1], scale=1.0 / 128)
                    ACTV(oss, oss, AF.Exp, r=["oss"], w=["oss"], scale=-0.5)
                    TT("dve", on, psv4, oss.unsqueeze(2).to_broadcast([64, 4, 128]), ALU.mult, r=[psk(4), "oss"], w=["on"])
                    TT("pool", on, on, dgb.unsqueeze(1).to_broadcast([64, 4, 128]), ALU.mult, r=["on", "dgb"], w=["on"])
                    for hb in range(4):
                        TR(PS[2][:, hb * 64:(hb + 1) * 64], on[:, hb, :], identf[0:64, 0:64], r=["on", "identf"], w=[psk(2)])
                    TT("dve", ZS[:, :, cols[ci]], ZS[:, :, cols[ci]], PS[2][:, 0:256].rearrange("p (a b) -> p a b", a=4), ALU.mult,
                       r=["ZS", psk(2)], w=["ZS"])
            P.enabled = en_seq
            for hb in range(4):
                DMA("sp", mixT[768 + hb * 128:768 + (hb + 1) * 128, col0:col0 + L], ZS[:, hb, 0:L], r=["ZS"], w=[U("mixT")], grp="zso")
            DMA("sp", s_out[l, oseq].rearrange("h k v -> k h v"), Sst, r=["S"], w=[U("sout")], grp="so")

        delta_seq(NPR, NPR, 0, None, None, 0)
        for sq_ in range(NSS):
            delta_seq(4, 64, NPR + 4 * sq_, sdc[l, sq_], sds[l, sq_], 1 + sq_)
        P.barrier()

        A.reset(base_mark)
        P.enabled = "4" in phases
        wo = A.tile([128, 16, D], BF16)
        gb = A.tile([128, D], F32)
        for q4 in range(4):
            DMA("pool", wo[:, q4 * 4:(q4 + 1) * 4, :], w_out[l][q4 * 512:(q4 + 1) * 512, :].rearrange("(k p) n -> p k n", p=128), w=["wo"], grp="wo%d" % q4)
        DMA("sp", gb, gom[l][0:1, :].partition_broadcast(128), w=["gb"], grp="gb")
        mt_ = [A.tile([128, 16, 128], BF16) for _ in range(2)]
        xt4 = [A.tile([128, D], F32) for _ in range(2)]
        yo = [A.tile([128, D], F32) for _ in range(2)]
        sqj = A.tile([128, 512], BF16)
        ss4 = [A.tile([128, 4], F32) for _ in range(2)]
        ss4r = [A.tile([128, 1], F32) for _ in range(2)]
        for ti, (r0, n) in enumerate(TOK_TILES):
            s = ti % 2
            km_, kx, ky, ks = "mt%d" % s, "xt4%d" % s, "yo%d" % s, "ss4%d" % s
            DMA("sp", mt_[s][:, :, 0:n], mixT[:, r0:r0 + n].rearrange("(k p) t -> p k t", p=128), r=["mixT"], w=[km_], grp=km_)
            DMA("sp", xt4[s][0:n, :], xsrc[r0:r0 + n, :], w=[kx], grp=kx)
            for cgp in range(4):
                b = (ti % 2) * 4 + cgp
                for k in range(16):
                    MM(PS[b][0:n, :], mt_[s][:, k, 0:n], wo[:, k, cgp * 512:(cgp + 1) * 512], start=(k == 0), stop=(k == 15),
                       r=[km_, "wo"], w=[psk(b)])
                ACTV(sqj[0:n, :], PS[b][0:n, :], AF.Square, r=[psk(b)], w=["sqj", ks + "_%d" % cgp], accum=ss4[s][0:n, cgp:cgp + 1])
            P.dve(lambda e, s=s, n=n: e.tensor_reduce(out=ss4r[s][0:n, 0:1], in_=ss4[s][0:n, :], axis=AX.X, op=ALU.add),
                  [ks + "_%d" % c for c in range(4)], [ks])
            rstd_from_ss(ss4r[s][0:n, 0:1], n, D, [ks], [ks])
            for cgp in range(4):
                b = (ti % 2) * 4 + cgp
                STT("dve", yo[s][0:n, cgp * 512:(cgp + 1) * 512], PS[b][0:n, :], ss4r[s][0:n, 0:1], gb[0:n, cgp * 512:(cgp + 1) * 512],
                    ALU.mult, ALU.mult, r=[psk(b), ks, "gb"], w=[ky])
            TT("pool", yo[s][0:n, :], yo[s][0:n, :], xt4[s][0:n, :], ALU.add, r=[ky, kx], w=[ky])
            DMA("sp", xres[r0:r0 + n, :], yo[s][0:n, :], r=[ky], w=[U("xres")], grp=ky + "o")
        P.barrier()

        A.reset(base_mark)
        P.enabled = "5" in phases
        gl = A.tile([128, 16], F32)
        gb5 = A.tile([128, D], F32)
        DMA("sp", gl, gpl[l], w=["gl"], grp="gl")
        DMA("sp", gb5, gol[l][0:1, :].partition_broadcast(128), w=["gb5"], grp="gb5")
        hm = A.tile([128, 16, 528], BF16)
        yacc = A.tile([128, 5, D], F32)
        uT = [A.tile([128, 8, 528], BF16) for _ in range(2)]
        wu = [A.tile([128, 16, 512], BF16) for _ in range(2)]
        wd = [A.tile([128, D], BF16) for _ in range(10)]
        rl = [A.tile([128, 512], F32) for _ in range(2)]
        nt5 = nt_alloc()
        m5 = A.mark()
        groups = [[(g * 512 + i * 128, 128) for i in range(4)] for g in range(4)]
        groups[3].append((NPR, NS))
        fctr = 0
        for g, tiles in enumerate(groups):
            segs = [(g * 512, 512, 0)] + ([(NPR, NS, 512)] if g == 3 else [])
            A.reset(m5)
            norm_transpose(xres, tiles, hm, "hm", gl, "gl", 0, "p5", ps_base=0, bufs=nt5)
            allhm = hkeys("hm", 0, 528)
            for fb in range(8):
                us = fb % 2
                ku = "uT%d" % us
                for fl in range(8):
                    f = fb * 8 + fl
                    s = (fctr // 4) % 2
                    kwu = "wu%d" % s
                    fo = (fl % 4) * 128
                    if fl % 4 == 0 and "nowu" not in _EXP:
                        DMA("pool", wu[s], w_up[l][:, f * 128:(f + 4) * 128].rearrange("(k p) n -> p k n", p=128), w=[kwu], grp=kwu)
                    sd = fctr % 10
                    kwd = "wd%d" % sd
                    if "nowd" not in _EXP:
                        DMA("pool", wd[sd], w_down[l][f * 128:(f + 1) * 128, :], w=[kwd], grp=kwd)
                    for si, (t0, n, c0) in enumerate(segs):
                        b = (fctr * 2 + si) % 2
                        for k in range(16):
                            MM(PS[b][:, 0:n], wu[s][:, k, fo:fo + 128], hm[:, k, c0:c0 + n], start=(k == 0), stop=(k == 15), r=[kwu] + allhm, w=[psk(b)])
                        rs_ = (fctr + si) % 2
                        kr = "rl%d" % rs_
                        ACTV(rl[rs_][:, 0:n], PS[b][:, 0:n], AF.Relu, r=[psk(b)], w=[kr])
                        TT("pool", uT[us][:, fl, c0:c0 + n], rl[rs_][:, 0:n], rl[rs_][:, 0:n], ALU.mult, r=[kr], w=[ku])
                    fctr += 1
                for ti, (r0, n) in enumerate(tiles):
                    c0 = ti * 128 if r0 < NPR else 512
                    for cgp in range(4):
                        b = 2 + (ti * 4 + cgp) % 6
                        for fl in range(8):
                            sd = (fctr - 8 + fl) % 10
                            MM(PS[b][0:n, :], uT[us][:, fl, c0:c0 + n], wd[sd][:, cgp * 512:(cgp + 1) * 512], start=(fl == 0), stop=(fl == 7),
                               r=[ku, "wd%d" % sd], w=[psk(b)])
                        ya = yacc[0:n, ti, cgp * 512:(cgp + 1) * 512]
                        if fb == 0:
                            CP("act", ya, PS[b][0:n, :], r=[psk(b)], w=["yacc%d" % ti])
                        else:
                            TT("dve", ya, ya, PS[b][0:n, :], ALU.add, r=[psk(b), "yacc%d" % ti], w=["yacc%d" % ti])
            xt5, _, sq5, ss5 = nt5
            for ti, (r0, n) in enumerate(tiles):
                s = ti % 2
                kx, ks, kya = "p5_xt%d" % s, "p5_ss%d" % s, "yacc%d" % ti
                DMA("sp", xt5[s][0:n, :], xres[r0:r0 + n, :], r=["xres"], w=[kx], grp=kx)
                ACTV(sq5[0:n, :], yacc[0:n, ti, :], AF.Square, r=[kya], w=["p5_sq", ks], accum=ss5[s][0:n, 0:1])
                rstd_from_ss(ss5[s][0:n, 0:1], n, D, [ks], [ks])
                STT("dve", yacc[0:n, ti, :], yacc[0:n, ti, :], ss5[s][0:n, 0:1], gb5[0:n, :], ALU.mult, ALU.mult, r=[kya, ks, "gb5"], w=[kya])
                TT("pool", xt5[s][0:n, :], xt5[s][0:n, :], yacc[0:n, ti, :], ALU.add, r=[kx, kya], w=[kx])
                DMA("sp", xdst[r0:r0 + n, :], xt5[s][0:n, :], r=[kx], w=[U("xdst")], grp=kx + "o")
        P.barrier()

    P.emit(st)
    st.close()
    return nc, P


def _constants():
    bf = ml_dtypes.bfloat16
    sa = alibi_slopes(12)
    p = np.arange(128)[:, None].astype(np.float64)
    q = np.arange(256)[None, :].astype(np.float64)
    dist = q - p
    valid = (dist >= 0) & (dist <= 128)
    EA = np.zeros((128, 36, 256), np.float64)
    for h in range(12):
        for bi, dil in enumerate((1, 4, 16)):
            EA[:, h * 3 + bi, :] = np.where(valid, np.exp(-sa[h] * dil * np.maximum(dist, 0)), 0.0)
    EC = np.zeros((128, 12, 256), np.float64)
    for pos in range(12):
        EC[:, pos, :] = np.where(valid, np.exp(-sa[CPERM[pos]] * np.maximum(dist, 0)), 0.0)
    ESA = np.zeros((128, 12, 36), np.float64)
    pp = np.arange(128)
    for h in range(12):
        for qi in range(4):
            for c in range(4):
                rho = 1536 + 128 * c + pp
                d_ = 2048 + qi - rho
                w = np.where(d_ <= 128, np.exp(-sa[h] * d_), 0.0)
                w = w + np.where((d_ % 4 == 0) & (d_ <= 512), np.exp(-sa[h] * d_), 0.0)
                ESA[:, h, c * 4 + qi] = w
            for r in range(4):
                rho = r + 16 * pp
                d_ = 2048 + qi - rho
                ESA[:, h, (4 + r) * 4 + qi] = np.where(r == qi, np.exp(-sa[h] * d_), 0.0)
            for j in range(4):
                w = 0.0
                if j < qi:
                    w = np.exp(-sa[h] * (qi - j))
                elif j == qi:
                    w = 3.0
                ESA[j, h, 32 + qi] = w
    ESC = np.zeros((128, 12, 8), np.float64)
    for pos in range(12):
        s_ = sa[CPERM[pos]]
        for qi in range(4):
            d_ = 128 + qi - pp
            ESC[:, pos, qi] = np.where(d_ <= 128, np.exp(-s_ * d_), 0.0)
            for j in range(4):
                if j <= qi:
                    ESC[j, pos, 4 + qi] = np.exp(-s_ * (qi - j))
    i64 = np.arange(64)
    U64 = (i64[None, :] >= i64[:, None]).astype(np.float32)
    LS64 = (i64[None, :] < i64[:, None]).astype(np.float32)
    sel = np.zeros((65, 64), np.float32)
    sel[64, :] = 1.0
    return dict(identf=np.eye(128, dtype=np.float32), EA=EA.reshape(128, -1).astype(bf), EC=EC.reshape(128, -1).astype(bf),
                ESA=ESA.reshape(128, -1).astype(bf), ESC=ESC.reshape(128, -1).astype(bf), U64=U64, LS64=LS64, sel=sel)


def _fm_columns():
    cols = []
    cols += list(range(0, 768))
    cols += list(range(768, 1536))
    cols += list(range(2304, 2304 + 1536))
    cols += list(range(3840, 3840 + 512))
    misc = [-1] * 128
    for h in range(4):
        misc[h] = 4352 + h
        misc[32 + h] = 4356 + h
    cols += misc
    cq0 = 4360
    for pos in range(12):
        cols += list(range(cq0 + CPERM[pos] * 64, cq0 + CPERM[pos] * 64 + 64))
    ck0 = 4360 + 768
    cols += list(range(ck0, ck0 + 256))
    assert len(cols) == NFM * 128
    return np.asarray(cols)


_CACHE = {}
_EXP = set()
_CDK_ROWS = np.concatenate([np.arange(r, 2048, 16) for r in range(4)] + [np.arange(1536, 2048)])


def _prepare_weights(depth, w_in, w_out):
    fm = _fm_columns()
    w_in = np.asarray(w_in)
    w_fm = np.zeros((depth, D, WALL), np.float32)
    ok = np.nonzero(fm >= 0)[0]
    w_fm[:, :, ok] = w_in[:depth][:, :, fm[ok]]
    w_fm[:, :, NFM * 128:NFM * 128 + 768] = w_in[:depth][:, :, 1536:2304]
    w_fm[:, :, NFM * 128 + 768:] = w_in[:depth][:, :, 5128 + 256:5640]
    rows = list(range(0, 1280))
    for pos in range(12):
        rows += list(range(1280 + CPERM[pos] * 64, 1280 + CPERM[pos] * 64 + 64))
    w_out_p = np.ascontiguousarray(np.asarray(w_out)[:depth][:, np.asarray(rows), :])
    return w_fm, w_out_p


def kernel(x_prompt, x_sample, cache_dilated_kv, cache_swa_kv, state_delta_s, state_delta_conv,
           g_pre_mix, w_in, delta_conv_w, delta_a_log, delta_dt_bias, delta_norm_g, swa_sinks,
           w_out, g_post_mix, g_pre_mlp, w_up, w_down, g_post_mlp, _depth=None, _phases=ALL_PHASES, _ncores=8):
    depth = DEPTH if _depth is None else _depth
    f32 = np.float32
    x_prompt = np.asarray(x_prompt, f32)
    x_sample = np.asarray(x_sample, f32)
    if "nc" not in _CACHE or _CACHE.get("depth") != (depth, tuple(_phases)):
        _CACHE["nc"] = build_program(depth, tuple(_phases))[0]
        _CACHE["depth"] = (depth, tuple(_phases))
    nc = _CACHE["nc"]
    consts = _constants()
    w_fm, w_out_p = _prepare_weights(depth, w_in, w_out)
    gT = lambda g: np.ascontiguousarray(np.asarray(g, f32)[:depth].reshape(depth, 16, 128).transpose(0, 2, 1))
    convw = np.asarray(delta_conv_w, f32)[:depth]
    convw = np.ascontiguousarray(convw.reshape(depth, 4, 12, 128).transpose(0, 3, 2, 1).reshape(depth, 128, 48))
    bpar = np.zeros((depth, 36, 2), f32)
    bpar[:, 32:36, 0] = np.asarray(delta_a_log, f32)[:depth]
    bpar[:, 32:36, 1] = np.asarray(delta_dt_bias, f32)[:depth]
    sk = np.zeros((depth, 65, 12), f32)
    sk[:, 64, :] = np.asarray(swa_sinks, f32)[:depth][:, CPERM]
    shared = dict(w_fm=w_fm, w_out=w_out_p, w_up=np.asarray(w_up, f32)[:depth], w_down=np.asarray(w_down, f32)[:depth],
                  gpm=gT(g_pre_mix), gpl=gT(g_pre_mlp), gom=np.asarray(g_post_mix, f32)[:depth].reshape(depth, 1, D),
                  gol=np.asarray(g_post_mlp, f32)[:depth].reshape(depth, 1, D), convw=convw, bpar=bpar,
                  dng=np.asarray(delta_norm_g, f32)[:depth].reshape(depth, 1, 128), sinks=sk, **consts)
    cdk_all = np.asarray(cache_dilated_kv, f32)
    csw_all = np.asarray(cache_swa_kv, f32)
    sds_all = np.asarray(state_delta_s, f32)
    sdc_all = np.asarray(state_delta_conv, f32)
    if "12" not in _phases:
        del shared["w_fm"]
    if "4" not in _phases:
        del shared["w_out"]
    if "5" not in _phases:
        del shared["w_up"], shared["w_down"]
    in_maps = []
    for c in range(_ncores):
        b = c % 4
        ss = slice(NSS * c, NSS * c + NSS)
        m = dict(shared)
        m["xin"] = np.concatenate([x_prompt[b], x_sample[ss].reshape(NS, D)], axis=0)
        if "3a" in _phases:
            m["cdk"] = np.ascontiguousarray(cdk_all[:depth, ss][:, :, _CDK_ROWS].reshape(depth, NSS, 1024, 1536))
        m["csw"] = np.ascontiguousarray(csw_all[:depth, ss].reshape(depth, NSS, 128, 512))
        m["sds"] = np.ascontiguousarray(sds_all[:depth, ss])
        m["sdc"] = np.ascontiguousarray(sdc_all[:depth, ss])
        in_maps.append(m)
    res = run_bass_kernel_spmd(nc, in_maps, core_ids=list(range(_ncores)))
    R = list(res.results)
    R = R + [R[i % _ncores] for i in range(len(R), 8)]
    if "dbg" in _phases:
        _CACHE["dbg"] = [{k: np.asarray(r[k]) for k in ("xres", "zT", "mixT")} for r in R]
    B = x_prompt.shape[0]
    yp = np.stack([R[b]["y"][:NPR] for b in range(B)])
    ys = np.concatenate([R[c]["y"][NPR:].reshape(NSS, 4, D) for c in range(8)])
    p_akv = np.stack([R[b]["akv"][:, :NPR].reshape(depth, NPR, 2, 12, 64) for b in range(B)], axis=1)
    p_ckv = np.stack([R[b]["ckv"][:, NPR - 128:NPR].reshape(depth, 128, 2, 4, 64) for b in range(B)], axis=1)
    p_s = np.stack([R[b]["s_out"][:, 0] for b in range(B)], axis=1)
    p_conv = np.stack([R[b]["conv_out"][:, 0] for b in range(B)], axis=1)
    s_akv = np.concatenate([R[c]["akv"][:, NPR:].reshape(depth, NSS, 4, 2, 12, 64) for c in range(8)], axis=1)
    s_ckv = np.concatenate([R[c]["ckv"][:, NPR:].reshape(depth, NSS, 4, 2, 4, 64) for c in range(8)], axis=1)
    s_s = np.concatenate([R[c]["s_out"][:, 1:] for c in range(8)], axis=1)
    s_conv = np.concatenate([R[c]["conv_out"][:, 1:] for c in range(8)], axis=1)
    outs = (yp, ys, p_akv, p_ckv, p_s, p_conv, s_akv, s_ckv, s_s, s_conv)
    return tuple(np.ascontiguousarray(o, dtype=np.float32) for o in outs)
```

```python
import numpy as np
import ml_dtypes
from collections import defaultdict
from contextlib import ExitStack
import concourse.bass as bass
import concourse.mybir as mybir
from concourse.bass_utils import run_bass_kernel_spmd

F32 = mybir.dt.float32
BF16 = mybir.dt.bfloat16
ALU = mybir.AluOpType
AF = mybir.ActivationFunctionType
AX = mybir.AxisListType

DEPTH = 4
D = 2048
NPR = 2048
NSS = 4
NS = 16
NT = NPR + NS
DFF = 8192
EPS = 1e-6
NFM = 37
WALL = NFM * 128 + 1024
TM_SEGS = [[(768, 512)], [(1280, 256), (4736, 256)], [(4992, 512)], [(4480, 256), (5504, 256)],
           [(1536, 512)], [(2048, 512)], [(2560, 512)]]
AQ, AK, BQ, BK, BV, BZ, BM, CQ, CK = 0, 6, 12, 16, 20, 24, 28, 29, 35
CPERM = [0, 3, 1, 4, 2, 5, 6, 9, 7, 10, 8, 11]
ATT_SCALE = 0.125
ENGS = ("pe", "act", "dve", "pool", "sp")


def alibi_slopes(n):
    return np.asarray([2.0 ** (-8.0 * (i + 1) / n) for i in range(n)], dtype=np.float64)


class Op:
    __slots__ = ("eng", "fn", "reads", "writes", "dma", "idx", "deps", "signal", "seq", "grp", "semkey", "bar")

    def __init__(self, eng, fn, reads, writes, dma, grp):
        self.eng = eng
        self.fn = fn
        self.reads = tuple(reads)
        self.writes = tuple(writes)
        self.dma = dma
        self.grp = grp
        self.deps = []
        self.signal = False
        self.seq = 0
        self.bar = False


class Prog:
    def __init__(self, nc):
        self.nc = nc
        self.ops = []
        self.enabled = True

    def add(self, eng, fn, reads=(), writes=(), dma=False, grp=None):
        op = Op(eng, fn, reads, writes, dma, grp)
        if self.enabled:
            self.ops.append(op)
        return op

    def pe(self, fn, r=(), w=()):
        return self.add("pe", fn, r, w)

    def act(self, fn, r=(), w=()):
        return self.add("act", fn, r, w)

    def dve(self, fn, r=(), w=()):
        return self.add("dve", fn, r, w)

    def pool(self, fn, r=(), w=()):
        return self.add("pool", fn, r, w)

    def dma(self, q, fn, r=(), w=(), grp=None):
        return self.add(q, fn, r, w, dma=True, grp=grp)

    def barrier(self):
        op = Op("sp", None, (), (), False, None)
        op.bar = True
        self.ops.append(op)

    def emit(self, stack):
        nc = self.nc
        ops = self.ops
        last_w = {}
        rds = defaultdict(list)
        last_eng = {}
        last_grp = {}
        bar_deps = []
        for i, op in enumerate(ops):
            op.idx = i
            if op.bar:
                bar_deps = list(last_eng.values()) + list(last_grp.values())
                for d in bar_deps:
                    d.signal = True
                last_w = {}
                rds = defaultdict(list)
                continue
            deps = {}
            for k in op.reads:
                w = last_w.get(k)
                if w is not None:
                    deps[w.idx] = "raw"
            for k in op.writes:
                w = last_w.get(k)
                if w is not None and deps.get(w.idx) != "raw":
                    deps[w.idx] = "waw"
                for r in rds[k]:
                    if r.idx not in deps:
                        deps[r.idx] = "war"
            for k in op.reads:
                if op.dma:
                    rds[k].append(op)
                else:
                    rds[k] = [r_ for r_ in rds[k] if r_.dma or r_.eng != op.eng] + [op]
            for k in op.writes:
                last_w[k] = op
                rds[k] = []
            keep = list(bar_deps)
            for j, kind in deps.items():
                d = ops[j]
                if d is op:
                    continue
                if (not d.dma) and (not op.dma) and d.eng == op.eng and kind != "raw" and op.eng == "pe":
                    continue
                keep.append(d)
                d.signal = True
            op.deps = keep
            if op.dma:
                op.signal = True
                last_grp[op.grp] = op
            else:
                last_eng[op.eng] = op
        cnt = defaultdict(int)
        for op in ops:
            if op.bar:
                continue
            if op.signal:
                key = ("g", op.grp) if op.dma else ("e", op.eng)
                cnt[key] += 16 if op.dma else 1
                op.seq = cnt[key]
                op.semkey = key
        sems = {}
        for n, key in enumerate(cnt):
            sems[key] = stack.enter_context(nc.semaphore("s%d" % n))
        per_eng = defaultdict(list)
        for op in ops:
            if not op.bar:
                per_eng[op.eng].append(op)
        block = stack.enter_context(nc.Block())
        totals = dict(cnt)
        self.n_waits = 0

        def run_engine(name, e):
            waited = defaultdict(int)
            for op in per_eng[name]:
                need = {}
                for d in op.deps:
                    k = d.semkey
                    if d.seq > need.get(k, 0):
                        need[k] = d.seq
                for k, v in need.items():
                    if waited[k] >= v:
                        continue
                    e.wait_ge(sems[k], v)
                    self.n_waits += 1
                    waited[k] = v
                ins = op.fn(e)
                if op.signal:
                    ins.then_inc(sems[op.semkey], 16 if op.dma else 1)
            if name == "sp":
                for k, v in totals.items():
                    if waited[k] < v:
                        e.wait_ge(sems[k], v)

        @block.sync
        def _(e):
            run_engine("sp", e)

        @block.scalar
        def _(e):
            run_engine("act", e)

        @block.vector
        def _(e):
            run_engine("dve", e)

        @block.gpsimd
        def _(e):
            run_engine("pool", e)

        @block.tensor
        def _(e):
            run_engine("pe", e)


class Arena:
    def __init__(self, ap, nbytes):
        self.ap = ap
        self.nbytes = nbytes
        self.off = 0

    def mark(self):
        return self.off

    def reset(self, m=0):
        self.off = m

    def tile(self, shape, dt, parts=None):
        n = int(np.prod(shape[1:]))
        nb = n * (2 if dt == BF16 else 4)
        nb = (nb + 63) // 64 * 64
        off = self.off
        self.off += nb
        assert self.off <= self.nbytes, "SBUF arena overflow %d" % self.off
        p = shape[0]
        if dt == BF16:
            a = self.ap[0:p, off // 4: off // 4 + nb // 4].bitcast(BF16)[:, 0:n]
        else:
            a = self.ap[0:p, off // 4: off // 4 + n]
        if len(shape) == 3:
            a = a.rearrange("p (a b) -> p a b", a=shape[1])
        elif len(shape) == 4:
            a = a.rearrange("p (a b c) -> p a b c", a=shape[1], b=shape[2])
        return a


ALL_PHASES = ("12", "3a", "3b", "3b2", "3bs", "4", "5")


def build_program(depth, phases=ALL_PHASES):
    nc = bass.Bass("TRN2", target_bir_lowering=False)

    def din(name, shape, dt=F32):
        return nc.dram_tensor(name, list(shape), dt, kind="ExternalInput").ap()

    def dout(name, shape, dt=F32):
        return nc.dram_tensor(name, list(shape), dt, kind="ExternalOutput").ap()

    def dscr(name, shape, dt):
        return nc.dram_tensor(name, list(shape), dt).ap()

    def din_if(cond, name, shape):
        return din(name, shape) if cond else nc.dram_tensor(name, list(shape), F32).ap()

    xin = din("xin", [NT, D])
    w_fm = din_if("12" in phases, "w_fm", [depth, D, WALL])
    w_out = din_if("4" in phases, "w_out", [depth, D, D])
    w_up = din_if("5" in phases, "w_up", [depth, D, DFF])
    w_down = din_if("5" in phases, "w_down", [depth, DFF, D])
    gpm = din("gpm", [depth, 128, 16])
    gpl = din("gpl", [depth, 128, 16])
    gom = din("gom", [depth, 1, D])
    gol = din("gol", [depth, 1, D])
    convw = din("convw", [depth, 128, 48])
    bpar = din("bpar", [depth, 36, 2])
    dng = din("dng", [depth, 1, 128])
    sinks = din("sinks", [depth, 65, 12])
    cdk = din_if("3a" in phases, "cdk", [depth, NSS, 1024, 1536])
    csw = din("csw", [depth, NSS, 128, 512])
    sds = din("sds", [depth, NSS, 4, 128, 128])
    sdc = din("sdc", [depth, NSS, 3, 1536])
    identf_d = din("identf", [128, 128])
    EA_d = din("EA", [128, 36 * 256], BF16)
    EC_d = din("EC", [128, 12 * 256], BF16)
    ESA_d = din("ESA", [128, 12 * 36], BF16)
    ESC_d = din("ESC", [128, 12 * 8], BF16)
    U64_d = din("U64", [64, 64])
    LS64_d = din("LS64", [64, 64])
    sel_d = din("sel", [65, 64])

    y = dout("y", [NT, D])
    akv = dout("akv", [depth, NT, 1536])
    ckv = dout("ckv", [depth, NT, 512])
    s_out = dout("s_out", [depth, 1 + NSS, 4, 128, 128])
    conv_out = dout("conv_out", [depth, 1 + NSS, 3, 1536])

    dbg = dout if "dbg" in phases else dscr
    xres = dbg("xres", [NT, D], F32)
    zT = dbg("zT", [NFM * 128, NT], BF16)
    mixT = dbg("mixT", [D, NT], BF16)

    P = Prog(nc)
    st = ExitStack()
    ARENA_F32 = 51600
    arena_t = nc.alloc_sbuf_tensor("arena", [128, ARENA_F32], F32)
    A = Arena(arena_t, ARENA_F32 * 4)
    PS = [st.enter_context(nc.psum_tensor("ps%d" % i, [128, 512], F32)) for i in range(8)]

    def psk(i):
        return "ps%d" % i

    uid = [0]

    def U(prefix):
        uid[0] += 1
        return "%s#%d" % (prefix, uid[0])

    def DMA(q, out, in_, r=(), w=(), grp=None):
        P.dma(q, lambda e: e.dma_start(out=out, in_=in_), r, w, grp)

    def MM(out, lhsT, rhs, start=True, stop=True, r=(), w=()):
        P.pe(lambda e: e.matmul(out, lhsT=lhsT, rhs=rhs, start=start, stop=stop), r, w)

    def TR(out, in_, ident, r=(), w=()):
        P.pe(lambda e: e.transpose(out=out, in_=in_, identity=ident), r, w)

    def ACTV(out, in_, func, r=(), w=(), bias=None, scale=None, accum=None):
        kw = {}
        if bias is not None:
            kw["bias"] = bias
        if scale is not None:
            kw["scale"] = scale
        if accum is not None:
            kw["accum_out"] = accum
        P.act(lambda e: e.activation(out=out, in_=in_, func=func, **kw), r, w)

    def TS(eng, out, in0, s1, s2, op0, op1=None, r=(), w=()):
        if op1 is None:
            P.add(eng, lambda e: e.tensor_scalar(out=out, in0=in0, scalar1=s1, scalar2=None, op0=op0), r, w)
        else:
            P.add(eng, lambda e: e.tensor_scalar(out=out, in0=in0, scalar1=s1, scalar2=s2, op0=op0, op1=op1), r, w)

    def TT(eng, out, in0, in1, op, r=(), w=()):
        P.add(eng, lambda e: e.tensor_tensor(out=out, in0=in0, in1=in1, op=op), r, w)

    def STT(eng, out, in0, scalar, in1, op0, op1, r=(), w=()):
        P.add(eng, lambda e: e.scalar_tensor_tensor(out=out, in0=in0, scalar=scalar, in1=in1, op0=op0, op1=op1), r, w)

    def CP(eng, out, in_, r=(), w=()):
        if eng == "act":
            P.act(lambda e: e.copy(out=out, in_=in_), r, w)
        else:
            P.add(eng, lambda e: e.tensor_copy(out=out, in_=in_), r, w)

    def MEMSET(eng, ap, val, w=()):
        P.add(eng, lambda e: e.memset(ap, val), (), w)

    identf = A.tile([128, 128], F32)
    onesf = A.tile([128, 128], F32)
    U64 = A.tile([64, 64], F32)
    LS64 = A.tile([64, 64], F32)
    sel = A.tile([65, 64], F32)
    eps_t = A.tile([128, 1], F32)
    one_t = A.tile([128, 1], F32)
    DMA("sp", identf, identf_d[:, :], w=["identf"], grp="c0")
    DMA("sp", U64, U64_d[:, :], w=["U64"], grp="c1")
    DMA("sp", LS64, LS64_d[:, :], w=["LS64"], grp="c2")
    DMA("sp", sel, sel_d[:, :], w=["sel"], grp="c3")
    MEMSET("pool", onesf, 1.0, w=["onesf"])
    MEMSET("pool", eps_t, EPS, w=["eps"])
    MEMSET("pool", one_t, 1.0, w=["one"])
    P.barrier()
    base_mark = A.mark()

    TOK_TILES = [(i * 128, 128) for i in range(16)] + [(NPR, NS)]
    TOK_GROUPS = [(i * 512, 512) for i in range(4)] + [(NPR, NS)]

    def rstd_from_ss(ss, n, D_, r, w):
        ACTV(ss, ss, AF.Ln, r=r + ["eps"], w=w, bias=eps_t[0:n, 0:1], scale=1.0 / D_)
        ACTV(ss, ss, AF.Exp, r=w, w=w, scale=-0.5)

    def nt_alloc():
        return ([A.tile([128, D], F32) for _ in range(2)], [A.tile([128, D], F32) for _ in range(2)],
                A.tile([128, D], BF16), [A.tile([128, 1], F32) for _ in range(2)])

    def norm_transpose(xsrc, tiles, hT, hkey, gT, gkey, col0, tag, ps_base=0, bufs=None):
        xt, xn, sq, ss = bufs if bufs is not None else nt_alloc()
        for ti, (r0, n) in enumerate(tiles):
            s = ti % 2
            kx, kn, ks = "%s_xt%d" % (tag, s), "%s_xn%d" % (tag, s), "%s_ss%d" % (tag, s)
            DMA("sp", xt[s][0:n, :], xsrc[r0:r0 + n, :], w=[kx], grp=kx)
            ACTV(sq[0:n, :], xt[s][0:n, :], AF.Square, r=[kx], w=[tag + "_sq", ks], accum=ss[s][0:n, 0:1])
            rstd_from_ss(ss[s][0:n, 0:1], n, D, [ks], [ks])
            TS("dve", xn[s][0:n, :], xt[s][0:n, :], ss[s][0:n, 0:1], None, ALU.mult, r=[kx, ks], w=[kn])
            c = col0 + (r0 if r0 < NPR else r0) - tiles[0][0] if False else None
            for qd in range(4):
                b = ps_base + (ti * 4 + qd) % 4
                for kk in range(4):
                    k = qd * 4 + kk
                    TR(PS[b][:, kk * 128: kk * 128 + n], xn[s][0:n, k * 128:(k + 1) * 128], identf[0:n, 0:n],
                       r=[kn, "identf"], w=[psk(b)])
                cc = col0 + ti * 128 if tiles[0][1] == 128 else col0
                cc = col0 + (r0 - tiles[0][0])
                TT("dve", hT[:, qd * 4:qd * 4 + 4, cc:cc + n],
                   PS[b][:, :].rearrange("p (a b) -> p a b", a=4)[:, :, 0:n],
                   gT[:, qd * 4:qd * 4 + 4].unsqueeze(2).to_broadcast([128, 4, n]), ALU.mult,
                   r=[psk(b), gkey], w=[hkey + "_%d" % (cc // 512)])

    def hkeys(hkey, c0, n):
        return [hkey + "_%d" % g for g in range(c0 // 512, (c0 + n - 1) // 512 + 1)]

    for l in range(depth):
        xsrc = xin if l == 0 else xres
        last = (l == depth - 1)
        xdst = y if last else xres

        A.reset(base_mark)
        P.enabled = "12" in phases
        hT = A.tile([128, 16, NT], BF16)
        gT = A.tile([128, 16], F32)
        DMA("sp", gT, gpm[l], w=["gT"], grp="gT")
        m1 = A.mark()
        norm_transpose(xsrc, TOK_TILES, hT, "hT", gT, "gT", 0, "p1")
        allh = hkeys("hT", 0, NT)
        wt = [A.tile([128, 16, 128], BF16) for _ in range(3)]
        zt = [A.tile([128, NT], BF16) for _ in range(2)]
        ev = 0
        for j in range(NFM):
            s = j % 3
            kw_ = "wfm%d" % s
            DMA("pool", wt[s], w_fm[l][:, j * 128:(j + 1) * 128].rearrange("(k p) n -> p k n", p=128), w=[kw_], grp=kw_)
            zs = j % 2
            kz = "zt%d" % zs
            for gi, (t0, n) in enumerate(TOK_GROUPS):
                b = (j * 5 + gi) % 4
                for k in range(16):
                    MM(PS[b][:, 0:n], wt[s][:, k, :], hT[:, k, t0:t0 + n], start=(k == 0), stop=(k == 15),
                       r=[kw_] + hkeys("hT", t0, n), w=[psk(b)])
                CP("act" if ev % 2 == 0 else "dve", zt[zs][:, t0:t0 + n], PS[b][:, 0:n], r=[psk(b)], w=[kz])
                ev += 1
            DMA("sp", zT[j * 128:(j + 1) * 128, :], zt[zs], r=[kz], w=["zT%d" % j], grp=kz + "o")
        wtm = [A.tile([128, 16, 512], BF16) for _ in range(2)]
        ot = [A.tile([128, 512], F32) for _ in range(3)]
        oi = 0
        for cg in range(7):
            s = cg % 2
            kw_ = "wtm%d" % s
            o_ = 0
            for (c_, n_) in TM_SEGS[cg]:
                DMA("pool", wtm[s][:, :, o_:o_ + n_], w_fm[l][:, c_:c_ + n_].rearrange("(k p) n -> p k n", p=128), w=[kw_], grp=kw_)
                o_ += n_
            if cg < 4:
                tl = TOK_TILES
            else:
                tl = [(NPR - 3, 3), (NPR, NS)]
            for (r0, n) in tl:
                b = 4 + oi % 4
                os_ = oi % 3
                ko = "ot%d" % os_
                for k in range(16):
                    MM(PS[b][0:n, :], hT[:, k, r0:r0 + n], wtm[s][:, k, :], start=(k == 0), stop=(k == 15),
                       r=[kw_] + hkeys("hT", r0, n), w=[psk(b)])
                CP("act" if oi % 2 == 0 else "dve", ot[os_][0:n, :], PS[b][0:n, :], r=[psk(b)], w=[ko])
                if cg < 3:
                    DMA("sp", akv[l][r0:r0 + n, cg * 512:(cg + 1) * 512], ot[os_][0:n, :], r=[ko], w=[U("akv")], grp=ko + "o")
                elif cg == 3:
                    DMA("sp", ckv[l][r0:r0 + n, :], ot[os_][0:n, :], r=[ko], w=[U("ckv")], grp=ko + "o")
                else:
                    c0 = (cg - 4) * 512
                    if r0 < NPR:
                        DMA("sp", conv_out[l, 0, :, c0:c0 + 512], ot[os_][0:3, :], r=[ko], w=["convo"], grp=ko + "o")
                    else:
                        for sq_ in range(NSS):
                            DMA("sp", conv_out[l, 1 + sq_, :, c0:c0 + 512], ot[os_][4 * sq_ + 1:4 * sq_ + 4, :], r=[ko],
                                w=["convo"], grp=ko + "o")
                oi += 1
        P.barrier()

        A.reset(base_mark)
        P.enabled = "3a" in phases
        EA = A.tile([128, 36, 256], BF16)
        EC = A.tile([128, 12, 256], BF16)
        ESA = A.tile([128, 12, 36], BF16)
        ESC = A.tile([128, 12, 8], BF16)
        esink = A.tile([65, 12], F32)
        DMA("sp", EA, EA_d[:, :].rearrange("p (a b) -> p a b", a=36), w=["EA"], grp="EA")
        DMA("sp", EC, EC_d[:, :].rearrange("p (a b) -> p a b", a=12), w=["EC"], grp="EC")
        DMA("sp", ESA, ESA_d[:, :].rearrange("p (a b) -> p a b", a=12), w=["ESA"], grp="ESA")
        DMA("sp", ESC, ESC_d[:, :].rearrange("p (a b) -> p a b", a=12), w=["ESC"], grp="ESC")
        DMA("sp", esink[64:65, :], sinks[l][64:65, :], w=["esink"], grp="esink")
        ACTV(esink[64:65, :], esink[64:65, :], AF.Exp, r=["esink"], w=["esink"])
        qT = [A.tile([128, NT], BF16) for _ in range(2)]
        kT = [A.tile([128, NT], BF16) for _ in range(2)]
        acc = [A.tile([65, NT], F32) for _ in range(2)]
        Vt = [A.tile([128, 16, 65], BF16) for _ in range(2)]
        Pt = [A.tile([128, 256], BF16) for _ in range(3)]
        mo = [A.tile([64, NT], BF16) for _ in range(2)]
        NPB_A, NPB_C = 8, 1
        kc = [A.tile([128, 768], F32) for _ in range(2)]
        kcT = A.tile([128, 6, NPB_A * 128], BF16)
        vc = A.tile([128, NPB_A, 12, 65], BF16)
        vnew = A.tile([4, 12, 65], BF16)
        Ps = [A.tile([128, 36], BF16) for _ in range(2)]
        for s in range(2):
            MEMSET("pool", Vt[s][:, :, 64:65], 1.0, w=["Vt%d" % s])
        MEMSET("pool", vc[:, :, :, 64:65], 1.0, w=["vc"])
        MEMSET("pool", vnew[:, :, 64:65], 1.0, w=["vnew"])
        kbctr = [0]
        hctr = [0]

        vsrc_key = {}
        akv_l = akv[l]
        ckv_l = ckv[l]
        vsrc_key[id(akv_l)] = "akv"
        vsrc_key[id(ckv_l)] = "ckv"

        def load_sample_cache_A(sq_):
            for blk in range(NPB_A):
                if blk < 4:
                    rows = cdk[l, sq_, 512 + 128 * blk:512 + 128 * blk + 128, :]
                else:
                    rows = cdk[l, sq_, (blk - 4) * 128:(blk - 4) * 128 + 128, :]
                s = blk % 2
                kkc = "kc%d" % s
                DMA("sp", kc[s], rows[:, 0:768], w=[kkc], grp=kkc)
                DMA("pool", vc[:, blk, :, 0:64], rows[:, 768:1536].rearrange("p (h e) -> p h e", h=12), w=["cAv"], grp="vc")
                for pr in range(6):
                    b = 4 + (blk * 6 + pr) % 4
                    TR(PS[b][:, 0:128], kc[s][:, pr * 128:(pr + 1) * 128], identf, r=[kkc, "identf"], w=[psk(b)])
                    CP("act" if pr % 2 == 0 else "dve", kcT[:, pr, blk * 128:(blk + 1) * 128], PS[b][:, 0:128], r=[psk(b)], w=["cAk"])
            DMA("pool", vnew[:, :, 0:64], akv_l[NPR + 4 * sq_:NPR + 4 * sq_ + 4, 768:1536].rearrange("p (h e) -> p h e", h=12),
                r=["akv"], w=["cAn"], grp="vnew")

        accS = A.tile([65, 24, NS], F32)

        def sample_attn(sq_, hslot, qap, knew, nblk, kblk_fn, vblk_fn, vnew_ap, E_, rkeys):
            i = kbctr[0]
            kbctr[0] += 1
            bS, bO, pslot = 4 + i % 2, 6 + i % 2, i % 2
            kp = "Ps%d" % pslot
            for blk in range(nblk):
                MM(PS[bS][:, blk * 4:blk * 4 + 4], kblk_fn(blk), qap, r=rkeys, w=[psk(bS)])
            MM(PS[bS][0:4, nblk * 4:nblk * 4 + 4], knew, qap, r=rkeys, w=[psk(bS)])
            ACTV(Ps[pslot][:, 0:nblk * 4], PS[bS][:, 0:nblk * 4], AF.Exp, r=[psk(bS)], w=[kp], scale=ATT_SCALE)
            ACTV(Ps[pslot][0:4, nblk * 4:nblk * 4 + 4], PS[bS][0:4, nblk * 4:nblk * 4 + 4], AF.Exp, r=[psk(bS)], w=[kp], scale=ATT_SCALE)
            TT("dve", Ps[pslot][:, 0:nblk * 4], Ps[pslot][:, 0:nblk * 4], E_[:, 0:nblk * 4], ALU.mult, r=[kp, "ESA", "ESC"], w=[kp])
            TT("dve", Ps[pslot][0:4, nblk * 4:nblk * 4 + 4], Ps[pslot][0:4, nblk * 4:nblk * 4 + 4], E_[0:4, nblk * 4:nblk * 4 + 4],
               ALU.mult, r=[kp, "ESA", "ESC"], w=[kp])
            for blk in range(nblk):
                MM(PS[bO][0:65, 0:4], vblk_fn(blk), Ps[pslot][:, blk * 4:blk * 4 + 4], start=(blk == 0), stop=False,
                   r=[kp] + rkeys, w=[psk(bO)])
            MM(PS[bO][0:65, 0:4], vnew_ap, Ps[pslot][0:4, nblk * 4:nblk * 4 + 4], start=False, stop=True, r=[kp] + rkeys, w=[psk(bO)])
            CP("act", accS[:, hslot, 4 * sq_:4 * sq_ + 4], PS[bO][0:65, 0:4], r=[psk(bO)], w=["accS"])

        PsB = [A.tile([128, 12, 36], BF16) for _ in range(2)]

        def sample_attn_batch(sq_, hslot0, W, nblk, qap_fn, knew_fn, kblk_fn, vblk_fn, vnew_fn, E3, rkeys):
            i = kbctr[0]
            kbctr[0] += 1
            bS, bO, pslot = 4 + i % 2, 6 + i % 2, i % 2
            kp = "PsB%d" % pslot
            nb4 = nblk * 4
            psS = PS[bS][:, 0:12 * W].rearrange("p (h w) -> p h w", h=12)
            psO = PS[bO][0:65, 0:48].rearrange("p (h w) -> p h w", h=12)
            Pb = PsB[pslot]
            for h in range(12):
                for blk in range(nblk):
                    MM(PS[bS][:, h * W + blk * 4:h * W + blk * 4 + 4], kblk_fn(h, blk), qap_fn(h), r=rkeys, w=[psk(bS)])
                MM(PS[bS][0:4, h * W + nb4:h * W + nb4 + 4], knew_fn(h), qap_fn(h), r=rkeys, w=[psk(bS)])
            ACTV(Pb[:, :, 0:nb4], psS[:, :, 0:nb4], AF.Exp, r=[psk(bS)], w=[kp], scale=ATT_SCALE)
            ACTV(Pb[0:4, :, nb4:W], psS[0:4, :, nb4:W], AF.Exp, r=[psk(bS)], w=[kp], scale=ATT_SCALE)
            TT("dve", Pb[:, :, 0:nb4], Pb[:, :, 0:nb4], E3[:, :, 0:nb4], ALU.mult, r=[kp, "ESA", "ESC"], w=[kp])
            TT("dve", Pb[0:4, :, nb4:W], Pb[0:4, :, nb4:W], E3[0:4, :, nb4:W], ALU.mult, r=[kp, "ESA", "ESC"], w=[kp])
            for h in range(12):
                for blk in range(nblk):
                    MM(PS[bO][0:65, h * 4:h * 4 + 4], vblk_fn(h, blk), Pb[:, h, blk * 4:blk * 4 + 4], start=(blk == 0), stop=False,
                       r=[kp] + rkeys, w=[psk(bO)])
                MM(PS[bO][0:65, h * 4:h * 4 + 4], vnew_fn(h), Pb[0:4, h, nb4:W], start=False, stop=True, r=[kp] + rkeys, w=[psk(bO)])
            CP("act", accS[:, hslot0:hslot0 + 12, 4 * sq_:4 * sq_ + 4], psO, r=[psk(bO)], w=["accS"])

        qsA = A.tile([128, 6, NS], BF16)
        ksA = A.tile([128, 6, NS], BF16)
        qsC = A.tile([128, 6, NS], BF16)
        ksC = A.tile([128, 2, NS], BF16)
        DMA("sp", qsA, zT[AQ * 128:(AQ + 6) * 128, NPR:NT].rearrange("(c p) t -> p c t", p=128), r=["zT%d" % j for j in range(AQ, AQ + 6)], w=["qsA"], grp="qsA")
        DMA("sp", ksA, zT[AK * 128:(AK + 6) * 128, NPR:NT].rearrange("(c p) t -> p c t", p=128), r=["zT%d" % j for j in range(AK, AK + 6)], w=["ksA"], grp="ksA")
        DMA("sp", qsC, zT[CQ * 128:(CQ + 6) * 128, NPR:NT].rearrange("(c p) t -> p c t", p=128), r=["zT%d" % j for j in range(CQ, CQ + 6)], w=["qsC"], grp="qsC")
        DMA("sp", ksC, zT[CK * 128:(CK + 2) * 128, NPR:NT].rearrange("(c p) t -> p c t", p=128), r=["zT%d" % j for j in range(CK, CK + 2)], w=["ksC"], grp="ksC")
        kcc = A.tile([128, 256], F32)
        kccT = A.tile([128, 2, 128], BF16)
        vcc = A.tile([128, 4, 65], BF16)
        vnewc = A.tile([4, 4, 65], BF16)
        MEMSET("pool", vcc[:, :, 64:65], 1.0, w=["cCv"])
        MEMSET("pool", vnewc[:, :, 64:65], 1.0, w=["cCn"])
        for sq_ in range(NSS):
            load_sample_cache_A(sq_)
            rk = ["cAk", "cAv", "cAn", "qsA", "ksA"]
            t4 = slice(4 * sq_, 4 * sq_ + 4)
            sample_attn_batch(sq_, 0, 36, NPB_A,
                              lambda h: qsA[(h % 2) * 64:(h % 2) * 64 + 64, h // 2, t4],
                              lambda h: ksA[(h % 2) * 64:(h % 2) * 64 + 64, h // 2, t4],
                              lambda h, blk: kcT[(h % 2) * 64:(h % 2) * 64 + 64, h // 2, blk * 128:(blk + 1) * 128],
                              lambda h, blk: vc[:, blk, h, :], lambda h: vnew[:, h, :], ESA, rk)
            DMA("sp", kcc, csw[l, sq_, :, 0:256], w=["kcc"], grp="kcc")
            DMA("pool", vcc[:, :, 0:64], csw[l, sq_, :, 256:512].rearrange("p (h e) -> p h e", h=4), w=["cCv"], grp="vcc")
            DMA("pool", vnewc[:, :, 0:64], ckv_l[NPR + 4 * sq_:NPR + 4 * sq_ + 4, 256:512].rearrange("p (h e) -> p h e", h=4),
                r=["ckv"], w=["cCn"], grp="vnewc")
            for pr in range(2):
                b = 4 + pr
                TR(PS[b][:, 0:128], kcc[:, pr * 128:(pr + 1) * 128], identf, r=["kcc", "identf"], w=[psk(b)])
                CP("act", kccT[:, pr, :], PS[b][:, 0:128], r=[psk(b)], w=["cC"])
            rk = ["cC", "cCv", "cCn", "qsC", "ksC"]
            nkv = lambda pos: CPERM[pos] // 3
            sample_attn_batch(sq_, 12, 8, NPB_C,
                              lambda pos: qsC[(pos % 2) * 64:(pos % 2) * 64 + 64, pos // 2, t4],
                              lambda pos: ksC[(pos % 2) * 64:(pos % 2) * 64 + 64, nkv(pos) // 2, t4],
                              lambda pos, blk: kccT[(pos % 2) * 64:(pos % 2) * 64 + 64, nkv(pos) // 2, :],
                              lambda pos, blk: vcc[:, nkv(pos), :], lambda pos: vnewc[:, nkv(pos), :], ESC, rk)

        def attn_prompt_head(qchunk, kchunk, half, branches, Eap_fn, vsrc, vkey, vcol, mix_row, sink_col, hslot, load_q, load_k, ps_q, ps_k):
            hs = hctr[0] % 2
            hctr[0] += 1
            kq, kk_, ka = "qT%d" % ps_q, "kT%d" % ps_k, "acc%d" % hs
            if load_q:
                DMA("sp", qT[ps_q], zT[qchunk * 128:(qchunk + 1) * 128, :], r=["zT%d" % qchunk], w=[kq], grp=kq)
            if load_k:
                DMA("sp", kT[ps_k], zT[kchunk * 128:(kchunk + 1) * 128, :], r=["zT%d" % kchunk], w=[kk_], grp=kk_)
            pb = half * 64
            MEMSET("pool", acc[hs][:, 0:NPR], 0.0, w=[ka])
            CP("pool", acc[hs][:, NPR:NT], accS[:, hslot, :], r=["accS"], w=[ka])
            pend = None
            for bi, dil in enumerate(branches):
                nblk = (NPR // dil) // 128
                vs = kbctr[0] % 2
                kbctr[0] += 1
                kv_ = "Vt%d" % vs
                vv = vsrc[0:NPR, vcol:vcol + 64].rearrange("(b p r) e -> r p b e", p=128, r=dil)
                for r_ in range(dil):
                    DMA("pool", Vt[vs][:, r_ * nblk:(r_ + 1) * nblk, 0:64], vv[r_], r=[vkey], w=[kv_ + "_%d" % r_] + ([kv_ + "_%d" % x for x in range(1, 16)] if r_ == 0 else []), grp=kv_)
                vkeys = [kv_ + "_%d" % r_ for r_ in range(dil)] + [kv_]
                E_ = Eap_fn(bi)
                for r_ in range(dil):
                    for blk in range(nblk):
                        nq = 256 if blk < nblk - 1 else 128
                        t0 = r_ + dil * 128 * blk
                        kap = kT[ps_k][pb:pb + 64, t0:t0 + dil * 127 + 1:dil]
                        qap = qT[ps_q][pb:pb + 64, t0:t0 + dil * (nq - 1) + 1:dil]
                        i = kbctr[0]
                        kbctr[0] += 1
                        bS, bO, pslot = i % 2, 2 + i % 2, i % 3
                        MM(PS[bS][:, 0:nq], kap, qap, r=[kq, kk_], w=[psk(bS)])
                        if pend is not None:
                            pend()
                        kp = "Pt%d" % pslot
                        ACTV(Pt[pslot][:, 0:nq], PS[bS][:, 0:nq], AF.Exp, r=[psk(bS)], w=[kp], scale=ATT_SCALE)
                        TT("dve", Pt[pslot][:, 0:nq], Pt[pslot][:, 0:nq], E_[:, 0:nq], ALU.mult, r=[kp, "EA", "EC"], w=[kp])

                        def fin(bO=bO, pslot=pslot, kp=kp, vs=vs, vkeys=vkeys, idx=r_ * nblk + blk, nq=nq, t0=t0, dil=dil):
                            MM(PS[bO][0:65, 0:nq], Vt[vs][:, idx, :], Pt[pslot][:, 0:nq], r=[kp] + vkeys, w=[psk(bO)])
                            av = acc[hs][:, t0:t0 + dil * (nq - 1) + 1:dil]
                            TT("dve", av, PS[bO][0:65, 0:nq], av, ALU.add, r=[psk(bO), ka], w=[ka])
                        pend = fin
            if pend is not None:
                pend()
            if sink_col is not None:
                TS("dve", acc[hs][64:65, :], acc[hs][64:65, :], esink[64:65, sink_col:sink_col + 1], None, ALU.add,
                   r=[ka, "esink"], w=[ka])
            P.dve(lambda e: e.reciprocal(out=acc[hs][64:65, :], in_=acc[hs][64:65, :]), [ka], [ka])
            km = "mo%d" % hs
            for gi, (t0, n) in enumerate(TOK_GROUPS):
                b = 4 + gi % 4
                MM(PS[b][0:64, 0:n], sel, acc[hs][:, t0:t0 + n], r=[ka, "sel"], w=[psk(b)])
                TT("dve", mo[hs][:, t0:t0 + n], acc[hs][0:64, t0:t0 + n], PS[b][0:64, 0:n], ALU.mult, r=[psk(b), ka], w=[km])
            DMA("sp", mixT[mix_row:mix_row + 64, :], mo[hs], r=[km], w=[U("mixT")], grp=km + "o")

        for h in range(12):
            pr, half = h // 2, h % 2
            attn_prompt_head(AQ + pr, AK + pr, half, [1, 4, 16], lambda bi, h=h: EA[:, h * 3 + bi, :], akv_l, "akv", 768 + h * 64,
                             h * 64, None, h, half == 0, half == 0, pr % 2, pr % 2)
        for pos in range(12):
            pr, half = pos // 2, pos % 2
            n_kv = CPERM[pos] // 3
            attn_prompt_head(CQ + pr, CK + n_kv // 2, half, [1], lambda bi, pos=pos: EC[:, pos, :], ckv_l, "ckv", 256 + n_kv * 64,
                             1280 + pos * 64, pos, 12 + pos, half == 0, pos % 6 == 0, pr % 2, (pos // 6) % 2)
        P.barrier()

        A.reset(base_mark)
        P.enabled = "3b" in phases
        QN = A.tile([128, 4, NPR], F32)
        KN = A.tile([128, 4, NPR], F32)
        VN = A.tile([128, 4, NPR], F32)
        ZS = A.tile([128, 4, NPR], BF16)
        X = A.tile([128, 3 + NPR], F32)
        SQ = A.tile([128, NPR], F32)
        MT = A.tile([36, NPR], F32)
        cw = A.tile([128, 48], F32)
        bp = A.tile([36, 2], F32)
        dgb = A.tile([64, 128], F32)
        Sst = A.tile([128, 4, 128], F32)
        tmpn = A.tile([128, 512], F32)
        DMA("sp", cw, convw[l], w=["cw"], grp="cw")
        DMA("sp", bp[32:36, :], bpar[l][32:36, :], w=["bp"], grp="bp")
        DMA("sp", dgb, dng[l][0:1, :].partition_broadcast(64), w=["dgb"], grp="dgb")
        ACTV(bp[32:36, 0:1], bp[32:36, 0:1], AF.Exp, r=["bp"], w=["bp"])
        TS("dve", bp[32:36, 0:1], bp[32:36, 0:1], -1.0, None, ALU.mult, r=["bp"], w=["bp"])
        U8 = 8
        bg = A.tile([64, 2, 36], F32)
        gcl = A.tile([64, 2, 8], F32)
        egc = A.tile([64, 2, 4], F32)
        ekd = A.tile([64, 2, 4], F32)
        nbeta = A.tile([64, 2, 4], F32)
        cbe = A.tile([64, 2, 4], F32)
        gcr = A.tile([128, U8, 64], F32)
        egr = A.tile([128, U8, 64], F32)
        tU = A.tile([64, U8, 64], F32)
        gU = A.tile([64, U8, 64], F32)
        tL = A.tile([64, U8, 64], F32)
        Nm = [A.tile([64, U8, 64], F32) for _ in range(2)]
        Pm = [A.tile([64, U8, 64], F32) for _ in range(2)]
        TTm = A.tile([64, U8, 64], F32)
        attnT = A.tile([64, U8, 64], F32)
        kbe = A.tile([64, U8, 128], F32)
        kd = A.tile([64, U8, 128], F32)
        vb = A.tile([64, U8, 128], F32)
        nwT = A.tile([128, U8, 64], F32)
        qdT = A.tile([128, U8, 64], F32)
        vn = A.tile([64, 4, 128], F32)
        osq = A.tile([64, 4, 128], F32)
        on = A.tile([64, 4, 128], F32)
        oss = A.tile([64, 4], F32)

        def delta_seq(L, Lpad, col0, conv_src, s0_src, oseq):
            nchunk = Lpad // 64
            sk = U("seq")
            en_seq = ("3b" in phases) and (conv_src is None or "3bs" in phases)
            P.enabled = en_seq
            if L < Lpad:
                for t_, nm in ((QN, "QN"), (KN, "KN"), (VN, "VN")):
                    MEMSET("pool", t_[:, :, 0:Lpad], 0.0, w=[nm])
                MEMSET("pool", MT[:, 0:Lpad], 0.0, w=["MT"])
            for kind, (tile_, nm, chunk0) in enumerate(((QN, "QN", BQ), (KN, "KN", BK), (VN, "VN", BV))):
                for hb in range(4):
                    ch = chunk0 + hb
                    wc = (kind * 4 + hb) * 4
                    if conv_src is None:
                        MEMSET("pool", X[:, 0:3], 0.0, w=["X"])
                    else:
                        P.dma("pool", lambda e, hb=hb, kind=kind: e.dma_start(
                            out=X[:, 0:3], in_=conv_src[:, (kind * 4 + hb) * 128:(kind * 4 + hb + 1) * 128].rearrange("t c -> c t"),
                            allow_slow_non_contiguous=True), [], ["X"], "Xc")
                    DMA("pool", X[:, 3:3 + L], zT[ch * 128:(ch + 1) * 128, col0:col0 + L], r=["zT%d" % ch], w=["X"], grp="X")
                    tg = tile_[:, hb, 0:L]
                    e0 = "dve"
                    TS(e0, tg, X[:, 0:L], cw[:, wc:wc + 1], None, ALU.mult, r=["X", "cw"], w=[nm])
                    for j in range(1, 4):
                        STT(e0, tg, X[:, j:j + L], cw[:, wc + j:wc + j + 1], tg, ALU.mult, ALU.add, r=["X", "cw", nm], w=[nm])
                    ACTV(tg, tg, AF.Silu, r=[nm], w=[nm])
                    if kind < 2:
                        TT("pool", SQ[:, 0:L], tg, tg, ALU.mult, r=[nm], w=["SQ"])
                        for g0 in range(0, L, 512):
                            n = min(512, L - g0)
                            b = (g0 // 512) % 4
                            MM(PS[b][:, 0:n], onesf, SQ[:, g0:g0 + n], r=["SQ", "onesf"], w=[psk(b)])
                            ACTV(tmpn[:, 0:n], PS[b][:, 0:n], AF.Ln, r=[psk(b), "eps"], w=["tmpn"], bias=eps_t[:, 0:1])
                            ACTV(tmpn[:, 0:n], tmpn[:, 0:n], AF.Exp, r=["tmpn"], w=["tmpn"], scale=-0.5)
                            if kind == 0:
                                STT("dve", tile_[:, hb, g0:g0 + n], tile_[:, hb, g0:g0 + n], float(128 ** -0.5), tmpn[:, 0:n], ALU.mult, ALU.mult,
                                    r=[nm, "tmpn"], w=[nm])
                            else:
                                TT("dve", tile_[:, hb, g0:g0 + n], tile_[:, hb, g0:g0 + n], tmpn[:, 0:n], ALU.mult, r=[nm, "tmpn"], w=[nm])
            for hb in range(4):
                ch = BZ + hb
                DMA("sp", ZS[:, hb, 0:L], zT[ch * 128:(ch + 1) * 128, col0:col0 + L], r=["zT%d" % ch], w=["ZS"], grp="ZS")
            ACTV(ZS[:, :, 0:L], ZS[:, :, 0:L], AF.Silu, r=["ZS"], w=["ZS"])
            DMA("pool", MT[0:36, 0:L], zT[BM * 128:BM * 128 + 36, col0:col0 + L], r=["zT%d" % BM], w=["MT"], grp="MT")
            ACTV(MT[0:4, 0:L], MT[0:4, 0:L], AF.Sigmoid, r=["MT"], w=["MT"])
            ACTV(MT[32:36, 0:L], MT[32:36, 0:L], AF.Exp, r=["MT", "bp"], w=["MT"], bias=bp[32:36, 1:2])
            ACTV(MT[32:36, 0:L], MT[32:36, 0:L], AF.Ln, r=["MT", "one"], w=["MT"], bias=one_t[32:36, 0:1])
            TS("dve", MT[32:36, 0:L], MT[32:36, 0:L], bp[32:36, 0:1], None, ALU.mult, r=["MT", "bp"], w=["MT"])
            if s0_src is None:
                MEMSET("pool", Sst, 0.0, w=["S"])
            else:
                DMA("sp", Sst, s0_src.rearrange("h k v -> k h v"), w=["S"], grp="S0")
            b2stop = max([int(p[1:]) for p in phases if p.startswith("s") and p[1:].isdigit()] + [0]) or 99

            def G(n):
                P.enabled = en_seq and ("3b2" in phases) and n <= b2stop
            G(0)
            for c0 in range(0, nchunk, 2):
                ncb = min(2, nchunk - c0)
                nu = ncb * 4
                cols = [slice(64 * (c0 + ci), 64 * (c0 + ci) + 64) for ci in range(ncb)]
                G(1)
                for ci in range(ncb):
                    TR(PS[0][0:64, ci * 36:(ci + 1) * 36], MT[0:36, cols[ci]], identf[0:36, 0:36], r=["MT", "identf"], w=[psk(0)])
                CP("dve", bg[:, 0:ncb, :], PS[0][0:64, 0:ncb * 36].rearrange("p (a b) -> p a b", a=ncb), r=[psk(0)], w=["bg"])
                for ci in range(ncb):
                    MM(PS[1][0:64, ci * 8:ci * 8 + 4], U64, bg[:, ci, 32:36], r=["bg", "U64"], w=[psk(1)])
                    MM(PS[1][0:64, ci * 8 + 4:ci * 8 + 8], onesf[0:64, 0:64], bg[:, ci, 32:36], r=["bg", "onesf"], w=[psk(1)])
                CP("dve", gcl[:, 0:ncb, :], PS[1][0:64, 0:ncb * 8].rearrange("p (a b) -> p a b", a=ncb), r=[psk(1)], w=["gcl"])
                ACTV(egc[:, 0:ncb, :], gcl[:, 0:ncb, 0:4], AF.Exp, r=["gcl"], w=["egc"])
                TT("dve", ekd[:, 0:ncb, :], gcl[:, 0:ncb, 4:8], gcl[:, 0:ncb, 0:4], ALU.subtract, r=["gcl"], w=["ekd"])
                ACTV(ekd[:, 0:ncb, :], ekd[:, 0:ncb, :], AF.Exp, r=["ekd"], w=["ekd"])
                TS("dve", nbeta[:, 0:ncb, :], bg[:, 0:ncb, 0:4], -1.0, None, ALU.mult, r=["bg"], w=["nbeta"])
                TT("dve", cbe[:, 0:ncb, :], bg[:, 0:ncb, 0:4], egc[:, 0:ncb, :], ALU.mult, r=["bg", "egc"], w=["cbe"])
                G(3)
                for ci in range(ncb):
                    for hb in range(4):
                        u = ci * 4 + hb
                        TS("dve", gU[:, u, :], U64, bg[:, ci, 32 + hb:33 + hb], None, ALU.mult, r=["bg", "U64"], w=["gU"])
                        MM(PS[2][:, u * 64:(u + 1) * 64], onesf[0:64, :], gU[:, u, :], r=["gU", "onesf"], w=[psk(2)])
                psv2 = PS[2][:, 0:nu * 64].rearrange("p (a b) -> p a b", a=nu)
                CP("dve", gcr[:, 0:nu, :], psv2, r=[psk(2)], w=["gcr"])
                ACTV(egr[:, 0:nu, :], gcr[:, 0:nu, :], AF.Exp, r=["gcr"], w=["egr"])
                G(4)
                for ci in range(ncb):
                    for hb in range(4):
                        u = ci * 4 + hb
                        TS("dve", tU[:, u, :], gcr[0:64, u, :], gcl[:, ci, hb:hb + 1], 0.0, ALU.subtract, ALU.min, r=["gcr", "gcl"], w=["tU"])
                        TS("dve", tL[:, u, :], gcr[0:64, u, :], gcl[:, ci, hb:hb + 1], 0.0, ALU.subtract, ALU.max, r=["gcr", "gcl"], w=["tL"])
                ACTV(tU[:, 0:nu, :], tU[:, 0:nu, :], AF.Exp, r=["tU"], w=["tU"])
                ACTV(tL[:, 0:nu, :], tL[:, 0:nu, :], AF.Exp, r=["tL"], w=["tL"], scale=-1.0)
                TT("dve", tU[:, 0:nu, :], tU[:, 0:nu, :], U64.unsqueeze(1).to_broadcast([64, nu, 64]), ALU.mult, r=["tU", "U64"], w=["tU"])
                TT("pool", tL[:, 0:nu, :], tL[:, 0:nu, :], LS64.unsqueeze(1).to_broadcast([64, nu, 64]), ALU.mult, r=["tL", "LS64"], w=["tL"])
                G(5)
                for ci in range(ncb):
                    for hb in range(4):
                        u = ci * 4 + hb
                        MM(PS[3][0:64, u * 64:(u + 1) * 64], KN[:, hb, cols[ci]], KN[:, hb, cols[ci]], r=["KN"], w=[psk(3)])
                        MM(PS[4][0:64, u * 64:(u + 1) * 64], KN[:, hb, cols[ci]], QN[:, hb, cols[ci]], r=["KN", "QN"], w=[psk(4)])
                for ci in range(ncb):
                    for hb in range(4):
                        u = ci * 4 + hb
                        STT("dve", Nm[0][:, u, :], PS[3][0:64, u * 64:(u + 1) * 64], nbeta[:, ci, hb:hb + 1], tL[:, u, :], ALU.mult, ALU.mult,
                            r=[psk(3), "nbeta", "tL"], w=["N0"])
                TT("dve", attnT[:, 0:nu, :], PS[4][0:64, 0:nu * 64].rearrange("p (a b) -> p a b", a=nu), tU[:, 0:nu, :], ALU.mult,
                   r=[psk(4), "tU"], w=["attnT"])
                G(6)
                for u in range(nu):
                    TR(PS[5][0:64, u * 64:(u + 1) * 64], Nm[0][:, u, :], identf[0:64, 0:64], r=["N0", "identf"], w=[psk(5)])
                psv5 = PS[5][0:64, 0:nu * 64].rearrange("p (a b) -> p a b", a=nu)
                CP("act", Pm[0][:, 0:nu, :], psv5, r=[psk(5)], w=["P0"])
                TT("dve", TTm[:, 0:nu, :], Pm[0][:, 0:nu, :], identf[0:64, 0:64].unsqueeze(1).to_broadcast([64, nu, 64]), ALU.add, r=["P0", "identf"], w=["TT"])
                G(7)
                cur = 0
                for step in range(1, 6):
                    nx = 1 - cur
                    for u in range(nu):
                        MM(PS[6][0:64, u * 64:(u + 1) * 64], Pm[cur][:, u, :], Nm[cur][:, u, :], r=["P%d" % cur, "N%d" % cur], w=[psk(6)])
                    CP("act", Nm[nx][:, 0:nu, :], PS[6][0:64, 0:nu * 64].rearrange("p (a b) -> p a b", a=nu), r=[psk(6)], w=["N%d" % nx])
                    if step < 5:
                        for u in range(nu):
                            MM(PS[7][0:64, u * 64:(u + 1) * 64], Nm[cur][:, u, :], Pm[cur][:, u, :], r=["P%d" % cur, "N%d" % cur], w=[psk(7)])
                        CP("dve", Pm[nx][:, 0:nu, :], PS[7][0:64, 0:nu * 64].rearrange("p (a b) -> p a b", a=nu), r=[psk(7)], w=["P%d" % nx])
                    for u in range(nu):
                        MM(PS[5][0:64, u * 64:(u + 1) * 64], Nm[nx][:, u, :], TTm[:, u, :], r=["N%d" % nx, "TT"], w=[psk(5)])
                    TT("dve", TTm[:, 0:nu, :], TTm[:, 0:nu, :], PS[5][0:64, 0:nu * 64].rearrange("p (a b) -> p a b", a=nu), ALU.add,
                       r=[psk(5), "TT"], w=["TT"])
                    cur = nx
                G(8)
                for which, (src, nm) in enumerate(((KN, "KN"), (VN, "VN"))):
                    for ci in range(ncb):
                        for hb in range(4):
                            u = ci * 4 + hb
                            b = 0 + (u // 4) + 2 * which
                            TR(PS[b][0:64, (u % 4) * 128:(u % 4 + 1) * 128], src[:, hb, cols[ci]], identf, r=[nm, "identf"], w=[psk(b)])
                    for ci in range(ncb):
                        for hb in range(4):
                            u = ci * 4 + hb
                            b = 0 + (u // 4) + 2 * which
                            pv = PS[b][0:64, (u % 4) * 128:(u % 4 + 1) * 128]
                            if which == 0:
                                ACTV(kbe[:, u, :], pv, AF.Copy, r=[psk(b), "cbe"], w=["kbe"], scale=cbe[:, ci, hb:hb + 1])
                                ACTV(kd[:, u, :], pv, AF.Copy, r=[psk(b), "ekd"], w=["kd"], scale=ekd[:, ci, hb:hb + 1])
                            else:
                                ACTV(vb[:, u, :], pv, AF.Copy, r=[psk(b), "bg"], w=["vb"], scale=bg[:, ci, hb:hb + 1])
                G(9)
                for u in range(nu):
                    MM(PS[6][:, u * 64:(u + 1) * 64], kbe[:, u, :], TTm[:, u, :], r=["kbe", "TT"], w=[psk(6)])
                TS("dve", nwT[:, 0:nu, :], PS[6][:, 0:nu * 64].rearrange("p (a b) -> p a b", a=nu), -1.0, None, ALU.mult, r=[psk(6)], w=["nwT"])
                for ci in range(ncb):
                    TT("pool", qdT[:, ci * 4:ci * 4 + 4, :], QN[:, :, cols[ci]], egr[:, ci * 4:ci * 4 + 4, :], ALU.mult, r=["QN", "egr"], w=["qdT"])
                G(10)
                for ci in range(ncb):
                    for hb in range(4):
                        u = ci * 4 + hb
                        MM(PS[7][0:64, hb * 128:(hb + 1) * 128], TTm[:, u, :], vb[:, u, :], start=True, stop=False, r=["TT", "vb"], w=[psk(7)])
                        MM(PS[7][0:64, hb * 128:(hb + 1) * 128], nwT[:, u, :], Sst[:, hb, :], start=False, stop=True, r=["nwT", "S"], w=[psk(7)])
                    CP("act", vn, PS[7][0:64, :].rearrange("p (a b) -> p a b", a=4), r=[psk(7)], w=["vn"])
                    for hb in range(4):
                        u = ci * 4 + hb
                        MM(PS[4][0:64, hb * 128:(hb + 1) * 128], qdT[:, u, :], Sst[:, hb, :], start=True, stop=False, r=["qdT", "S"], w=[psk(4)])
                        MM(PS[4][0:64, hb * 128:(hb + 1) * 128], attnT[:, u, :], vn[:, hb, :], start=False, stop=True, r=["attnT", "vn"], w=[psk(4)])
                    for hb in range(4):
                        u = ci * 4 + hb
                        MM(PS[3][:, hb * 128:(hb + 1) * 128], kd[:, u, :], vn[:, hb, :], r=["kd", "vn"], w=[psk(3)])
                    TT("dve", Sst, Sst, egr[:, ci * 4:ci * 4 + 4, 63:64].to_broadcast([128, 4, 128]), ALU.mult, r=["S", "egr"], w=["S"])
                    TT("dve", Sst, Sst, PS[3][:, :].rearrange("p (a b) -> p a b", a=4), ALU.add, r=["S", psk(3)], w=["S"])
                    psv4 = PS[4][0:64, :].rearrange("p (a b) -> p a b", a=4)
                    ACTV(osq, psv4, AF.Square, r=[psk(4)], w=["osq"])
                    P.dve(lambda e: e.tensor_reduce(out=oss, in_=osq, axis=AX.X, op=ALU.add), ["osq"], ["oss"])
                    ACTV(oss, oss, AF.Ln, r=["oss", "eps"], w=["oss"], bias=eps_t[0:64, 0:1], scale=1.0 / 128)
                    ACTV(oss, oss, AF.Exp, r=["oss"], w=["oss"], scale=-0.5)
                    TT("dve", on, psv4, oss.unsqueeze(2).to_broadcast([64, 4, 128]), ALU.mult, r=[psk(4), "oss"], w=["on"])
                    TT("pool", on, on, dgb.unsqueeze(1).to_broadcast([64, 4, 128]), ALU.mult, r=["on", "dgb"], w=["on"])
                    for hb in range(4):
                        TR(PS[2][:, hb * 64:(hb + 1) * 64], on[:, hb, :], identf[0:64, 0:64], r=["on", "identf"], w=[psk(2)])
                    TT("dve", ZS[:, :, cols[ci]], ZS[:, :, cols[ci]], PS[2][:, 0:256].rearrange("p (a b) -> p a b", a=4), ALU.mult,
                       r=["ZS", psk(2)], w=["ZS"])
            P.enabled = en_seq
            for hb in range(4):
                DMA("sp", mixT[768 + hb * 128:768 + (hb + 1) * 128, col0:col0 + L], ZS[:, hb, 0:L], r=["ZS"], w=[U("mixT")], grp="zso")
            DMA("sp", s_out[l, oseq].rearrange("h k v -> k h v"), Sst, r=["S"], w=[U("sout")], grp="so")

        delta_seq(NPR, NPR, 0, None, None, 0)
        for sq_ in range(NSS):
            delta_seq(4, 64, NPR + 4 * sq_, sdc[l, sq_], sds[l, sq_], 1 + sq_)
        P.barrier()

        A.reset(base_mark)
        P.enabled = "4" in phases
        wo = A.tile([128, 16, D], BF16)
        gb = A.tile([128, D], F32)
        for q4 in range(4):
            DMA("pool", wo[:, q4 * 4:(q4 + 1) * 4, :], w_out[l][q4 * 512:(q4 + 1) * 512, :].rearrange("(k p) n -> p k n", p=128), w=["wo%d" % q4], grp="wo%d" % q4)
        DMA("sp", gb, gom[l][0:1, :].partition_broadcast(128), w=["gb"], grp="gb")
        mt_ = [A.tile([128, 16, 128], BF16) for _ in range(2)]
        xt4 = [A.tile([128, D], F32) for _ in range(2)]
        yo = [A.tile([128, D], F32) for _ in range(2)]
        sqj = A.tile([128, 512], BF16)
        ss4 = [A.tile([128, 4], F32) for _ in range(2)]
        ss4r = [A.tile([128, 1], F32) for _ in range(2)]
        for ti, (r0, n) in enumerate(TOK_TILES):
            s = ti % 2
            km_, kx, ky, ks = "mt%d" % s, "xt4%d" % s, "yo%d" % s, "ss4%d" % s
            DMA("sp", mt_[s][:, :, 0:n], mixT[:, r0:r0 + n].rearrange("(k p) t -> p k t", p=128), r=["mixT"], w=[km_], grp=km_)
            DMA("sp", xt4[s][0:n, :], xsrc[r0:r0 + n, :], w=[kx], grp=kx)
            for cgp in range(4):
                b = (ti % 2) * 4 + cgp
                for k in range(16):
                    MM(PS[b][0:n, :], mt_[s][:, k, 0:n], wo[:, k, cgp * 512:(cgp + 1) * 512], start=(k == 0), stop=(k == 15),
                       r=[km_, "wo%d" % (k // 4)], w=[psk(b)])
                ACTV(sqj[0:n, :], PS[b][0:n, :], AF.Square, r=[psk(b)], w=["sqj", ks + "_%d" % cgp], accum=ss4[s][0:n, cgp:cgp + 1])
            P.dve(lambda e, s=s, n=n: e.tensor_reduce(out=ss4r[s][0:n, 0:1], in_=ss4[s][0:n, :], axis=AX.X, op=ALU.add),
                  [ks + "_%d" % c for c in range(4)], [ks])
            rstd_from_ss(ss4r[s][0:n, 0:1], n, D, [ks], [ks])
            for cgp in range(4):
                b = (ti % 2) * 4 + cgp
                STT("dve", yo[s][0:n, cgp * 512:(cgp + 1) * 512], PS[b][0:n, :], ss4r[s][0:n, 0:1], gb[0:n, cgp * 512:(cgp + 1) * 512],
                    ALU.mult, ALU.mult, r=[psk(b), ks, "gb"], w=[ky])
            TT("pool", yo[s][0:n, :], yo[s][0:n, :], xt4[s][0:n, :], ALU.add, r=[ky, kx], w=[ky])
            DMA("sp", xres[r0:r0 + n, :], yo[s][0:n, :], r=[ky], w=[U("xres")], grp=ky + "o")
        P.barrier()

        A.reset(base_mark)
        P.enabled = "5" in phases
        gl = A.tile([128, 16], F32)
        gb5 = A.tile([128, D], F32)
        DMA("sp", gl, gpl[l], w=["gl"], grp="gl")
        DMA("sp", gb5, gol[l][0:1, :].partition_broadcast(128), w=["gb5"], grp="gb5")
        hm = A.tile([128, 16, 528], BF16)
        yacc = A.tile([128, 5, D], F32)
        uT = [A.tile([128, 8, 528], BF16) for _ in range(2)]
        wu = [A.tile([128, 16, 512], BF16) for _ in range(2)]
        wd = [A.tile([128, D], BF16) for _ in range(10)]
        rl = [A.tile([128, 512], F32) for _ in range(2)]
        nt5 = nt_alloc()
        m5 = A.mark()
        groups = [[(g * 512 + i * 128, 128) for i in range(4)] for g in range(4)]
        groups[3].append((NPR, NS))
        fctr = 0
        for g, tiles in enumerate(groups):
            segs = [(g * 512, 512, 0)] + ([(NPR, NS, 512)] if g == 3 else [])
            A.reset(m5)
            norm_transpose(xres, tiles, hm, "hm", gl, "gl", 0, "p5", ps_base=0, bufs=nt5)
            allhm = hkeys("hm", 0, 528)
            for fb in range(8):
                us = fb % 2
                ku = "uT%d" % us
                for fl in range(8):
                    f = fb * 8 + fl
                    s = (fctr // 4) % 2
                    kwu = "wu%d" % s
                    fo = (fl % 4) * 128
                    if fl % 4 == 0 and "nowu" not in _EXP:
                        DMA("pool", wu[s], w_up[l][:, f * 128:(f + 4) * 128].rearrange("(k p) n -> p k n", p=128), w=[kwu], grp=kwu)
                    sd = fctr % 10
                    kwd = "wd%d" % sd
                    if "nowd" not in _EXP:
                        DMA("pool", wd[sd], w_down[l][f * 128:(f + 1) * 128, :], w=[kwd], grp=kwd)
                    for si, (t0, n, c0) in enumerate(segs):
                        b = (fctr * 2 + si) % 2
                        for k in range(16):
                            MM(PS[b][:, 0:n], wu[s][:, k, fo:fo + 128], hm[:, k, c0:c0 + n], start=(k == 0), stop=(k == 15), r=[kwu] + allhm, w=[psk(b)])
                        rs_ = (fctr + si) % 2
                        kr = "rl%d" % rs_
                        ACTV(rl[rs_][:, 0:n], PS[b][:, 0:n], AF.Relu, r=[psk(b)], w=[kr])
                        TT("pool", uT[us][:, fl, c0:c0 + n], rl[rs_][:, 0:n], rl[rs_][:, 0:n], ALU.mult, r=[kr], w=[ku])
                    fctr += 1
                for ti, (r0, n) in enumerate(tiles):
                    c0 = ti * 128 if r0 < NPR else 512
                    for cgp in range(4):
                        b = 2 + (ti * 4 + cgp) % 6
                        for fl in range(8):
                            sd = (fctr - 8 + fl) % 10
                            MM(PS[b][0:n, :], uT[us][:, fl, c0:c0 + n], wd[sd][:, cgp * 512:(cgp + 1) * 512], start=(fl == 0), stop=(fl == 7),
                               r=[ku, "wd%d" % sd], w=[psk(b)])
                        ya = yacc[0:n, ti, cgp * 512:(cgp + 1) * 512]
                        if fb == 0:
                            CP("act", ya, PS[b][0:n, :], r=[psk(b)], w=["yacc%d" % ti])
                        else:
                            TT("dve", ya, ya, PS[b][0:n, :], ALU.add, r=[psk(b), "yacc%d" % ti], w=["yacc%d" % ti])
            xt5, _, sq5, ss5 = nt5
            for ti, (r0, n) in enumerate(tiles):
                s = ti % 2
                kx, ks, kya = "p5_xt%d" % s, "p5_ss%d" % s, "yacc%d" % ti
                DMA("sp", xt5[s][0:n, :], xres[r0:r0 + n, :], r=["xres"], w=[kx], grp=kx)
                ACTV(sq5[0:n, :], yacc[0:n, ti, :], AF.Square, r=[kya], w=["p5_sq", ks], accum=ss5[s][0:n, 0:1])
                rstd_from_ss(ss5[s][0:n, 0:1], n, D, [ks], [ks])
                STT("dve", yacc[0:n, ti, :], yacc[0:n, ti, :], ss5[s][0:n, 0:1], gb5[0:n, :], ALU.mult, ALU.mult, r=[kya, ks, "gb5"], w=[kya])
                TT("pool", xt5[s][0:n, :], xt5[s][0:n, :], yacc[0:n, ti, :], ALU.add, r=[kx, kya], w=[kx])
                DMA("sp", xdst[r0:r0 + n, :], xt5[s][0:n, :], r=[kx], w=[U("xdst")], grp=kx + "o")
        P.barrier()

    P.emit(st)
    st.close()
    return nc, P


def _constants():
    bf = ml_dtypes.bfloat16
    sa = alibi_slopes(12)
    p = np.arange(128)[:, None].astype(np.float64)
    q = np.arange(256)[None, :].astype(np.float64)
    dist = q - p
    valid = (dist >= 0) & (dist <= 128)
    EA = np.zeros((128, 36, 256), np.float64)
    for h in range(12):
        for bi, dil in enumerate((1, 4, 16)):
            EA[:, h * 3 + bi, :] = np.where(valid, np.exp(-sa[h] * dil * np.maximum(dist, 0)), 0.0)
    EC = np.zeros((128, 12, 256), np.float64)
    for pos in range(12):
        EC[:, pos, :] = np.where(valid, np.exp(-sa[CPERM[pos]] * np.maximum(dist, 0)), 0.0)
    ESA = np.zeros((128, 12, 36), np.float64)
    pp = np.arange(128)
    for h in range(12):
        for qi in range(4):
            for c in range(4):
                rho = 1536 + 128 * c + pp
                d_ = 2048 + qi - rho
                w = np.where(d_ <= 128, np.exp(-sa[h] * d_), 0.0)
                w = w + np.where((d_ % 4 == 0) & (d_ <= 512), np.exp(-sa[h] * d_), 0.0)
                ESA[:, h, c * 4 + qi] = w
            for r in range(4):
                rho = r + 16 * pp
                d_ = 2048 + qi - rho
                ESA[:, h, (4 + r) * 4 + qi] = np.where(r == qi, np.exp(-sa[h] * d_), 0.0)
            for j in range(4):
                w = 0.0
                if j < qi:
                    w = np.exp(-sa[h] * (qi - j))
                elif j == qi:
                    w = 3.0
                ESA[j, h, 32 + qi] = w
    ESC = np.zeros((128, 12, 8), np.float64)
    for pos in range(12):
        s_ = sa[CPERM[pos]]
        for qi in range(4):
            d_ = 128 + qi - pp
            ESC[:, pos, qi] = np.where(d_ <= 128, np.exp(-s_ * d_), 0.0)
            for j in range(4):
                if j <= qi:
                    ESC[j, pos, 4 + qi] = np.exp(-s_ * (qi - j))
    i64 = np.arange(64)
    U64 = (i64[None, :] >= i64[:, None]).astype(np.float32)
    LS64 = (i64[None, :] < i64[:, None]).astype(np.float32)
    sel = np.zeros((65, 64), np.float32)
    sel[64, :] = 1.0
    return dict(identf=np.eye(128, dtype=np.float32), EA=EA.reshape(128, -1).astype(bf), EC=EC.reshape(128, -1).astype(bf),
                ESA=ESA.reshape(128, -1).astype(bf), ESC=ESC.reshape(128, -1).astype(bf), U64=U64, LS64=LS64, sel=sel)


def _fm_columns():
    cols = []
    cols += list(range(0, 768))
    cols += list(range(768, 1536))
    cols += list(range(2304, 2304 + 1536))
    cols += list(range(3840, 3840 + 512))
    misc = [-1] * 128
    for h in range(4):
        misc[h] = 4352 + h
        misc[32 + h] = 4356 + h
    cols += misc
    cq0 = 4360
    for pos in range(12):
        cols += list(range(cq0 + CPERM[pos] * 64, cq0 + CPERM[pos] * 64 + 64))
    ck0 = 4360 + 768
    cols += list(range(ck0, ck0 + 256))
    assert len(cols) == NFM * 128
    return np.asarray(cols)


_CACHE = {}
_EXP = set()
_CDK_ROWS = np.concatenate([np.arange(r, 2048, 16) for r in range(4)] + [np.arange(1536, 2048)])


def _prepare_weights(depth, w_in, w_out):
    fm = _fm_columns()
    w_in = np.asarray(w_in)
    w_fm = np.zeros((depth, D, WALL), np.float32)
    ok = np.nonzero(fm >= 0)[0]
    w_fm[:, :, ok] = w_in[:depth][:, :, fm[ok]]
    w_fm[:, :, NFM * 128:NFM * 128 + 768] = w_in[:depth][:, :, 1536:2304]
    w_fm[:, :, NFM * 128 + 768:] = w_in[:depth][:, :, 5128 + 256:5640]
    rows = list(range(0, 1280))
    for pos in range(12):
        rows += list(range(1280 + CPERM[pos] * 64, 1280 + CPERM[pos] * 64 + 64))
    w_out_p = np.ascontiguousarray(np.asarray(w_out)[:depth][:, np.asarray(rows), :])
    return w_fm, w_out_p


def kernel(x_prompt, x_sample, cache_dilated_kv, cache_swa_kv, state_delta_s, state_delta_conv,
           g_pre_mix, w_in, delta_conv_w, delta_a_log, delta_dt_bias, delta_norm_g, swa_sinks,
           w_out, g_post_mix, g_pre_mlp, w_up, w_down, g_post_mlp, _depth=None, _phases=ALL_PHASES, _ncores=8):
    depth = DEPTH if _depth is None else _depth
    f32 = np.float32
    x_prompt = np.asarray(x_prompt, f32)
    x_sample = np.asarray(x_sample, f32)
    if "nc" not in _CACHE or _CACHE.get("depth") != (depth, tuple(_phases)):
        _CACHE["nc"] = build_program(depth, tuple(_phases))[0]
        _CACHE["depth"] = (depth, tuple(_phases))
    nc = _CACHE["nc"]
    consts = _constants()
    w_fm, w_out_p = _prepare_weights(depth, w_in, w_out)
    gT = lambda g: np.ascontiguousarray(np.asarray(g, f32)[:depth].reshape(depth, 16, 128).transpose(0, 2, 1))
    convw = np.asarray(delta_conv_w, f32)[:depth]
    convw = np.ascontiguousarray(convw.reshape(depth, 4, 12, 128).transpose(0, 3, 2, 1).reshape(depth, 128, 48))
    bpar = np.zeros((depth, 36, 2), f32)
    bpar[:, 32:36, 0] = np.asarray(delta_a_log, f32)[:depth]
    bpar[:, 32:36, 1] = np.asarray(delta_dt_bias, f32)[:depth]
    sk = np.zeros((depth, 65, 12), f32)
    sk[:, 64, :] = np.asarray(swa_sinks, f32)[:depth][:, CPERM]
    shared = dict(w_fm=w_fm, w_out=w_out_p, w_up=np.asarray(w_up, f32)[:depth], w_down=np.asarray(w_down, f32)[:depth],
                  gpm=gT(g_pre_mix), gpl=gT(g_pre_mlp), gom=np.asarray(g_post_mix, f32)[:depth].reshape(depth, 1, D),
                  gol=np.asarray(g_post_mlp, f32)[:depth].reshape(depth, 1, D), convw=convw, bpar=bpar,
                  dng=np.asarray(delta_norm_g, f32)[:depth].reshape(depth, 1, 128), sinks=sk, **consts)
    cdk_all = np.asarray(cache_dilated_kv, f32)
    csw_all = np.asarray(cache_swa_kv, f32)
    sds_all = np.asarray(state_delta_s, f32)
    sdc_all = np.asarray(state_delta_conv, f32)
    if "12" not in _phases:
        del shared["w_fm"]
    if "4" not in _phases:
        del shared["w_out"]
    if "5" not in _phases:
        del shared["w_up"], shared["w_down"]
    in_maps = []
    for c in range(_ncores):
        b = c % 4
        ss = slice(NSS * c, NSS * c + NSS)
        m = dict(shared)
        m["xin"] = np.concatenate([x_prompt[b], x_sample[ss].reshape(NS, D)], axis=0)
        if "3a" in _phases:
            m["cdk"] = np.ascontiguousarray(cdk_all[:depth, ss][:, :, _CDK_ROWS].reshape(depth, NSS, 1024, 1536))
        m["csw"] = np.ascontiguousarray(csw_all[:depth, ss].reshape(depth, NSS, 128, 512))
        m["sds"] = np.ascontiguousarray(sds_all[:depth, ss])
        m["sdc"] = np.ascontiguousarray(sdc_all[:depth, ss])
        in_maps.append(m)
    res = run_bass_kernel_spmd(nc, in_maps, core_ids=list(range(_ncores)))
    R = list(res.results)
    R = R + [R[i % _ncores] for i in range(len(R), 8)]
    if "dbg" in _phases:
        _CACHE["dbg"] = [{k: np.asarray(r[k]) for k in ("xres", "zT", "mixT")} for r in R]
    B = x_prompt.shape[0]
    yp = np.stack([R[b]["y"][:NPR] for b in range(B)])
    ys = np.concatenate([R[c]["y"][NPR:].reshape(NSS, 4, D) for c in range(8)])
    p_akv = np.stack([R[b]["akv"][:, :NPR].reshape(depth, NPR, 2, 12, 64) for b in range(B)], axis=1)
    p_ckv = np.stack([R[b]["ckv"][:, NPR - 128:NPR].reshape(depth, 128, 2, 4, 64) for b in range(B)], axis=1)
    p_s = np.stack([R[b]["s_out"][:, 0] for b in range(B)], axis=1)
    p_conv = np.stack([R[b]["conv_out"][:, 0] for b in range(B)], axis=1)
    s_akv = np.concatenate([R[c]["akv"][:, NPR:].reshape(depth, NSS, 4, 2, 12, 64) for c in range(8)], axis=1)
    s_ckv = np.concatenate([R[c]["ckv"][:, NPR:].reshape(depth, NSS, 4, 2, 4, 64) for c in range(8)], axis=1)
    s_s = np.concatenate([R[c]["s_out"][:, 1:] for c in range(8)], axis=1)
    s_conv = np.concatenate([R[c]["conv_out"][:, 1:] for c in range(8)], axis=1)
    outs = (yp, ys, p_akv, p_ckv, p_s, p_conv, s_akv, s_ckv, s_s, s_conv)
    return tuple(np.ascontiguousarray(o, dtype=np.float32) for o in outs)
```
